# Optimizing a Trainium2 kernel written in Bass

```python
import math
import jax, jax.numpy as jnp
from jax import lax
import numpy as np

D_MODEL = 1024
BATCH = 8
SEQ = 2048
DEPTH = 4
DEC_BATCH = 128
DEC_SEQ = 1
PAST_LEN = 16384
PAGE_SIZE = 128

N_EVEN = (DEPTH + 1) // 2
N_ODD = DEPTH // 2
D_A = D_MODEL // 2
D_B = D_MODEL - D_A
D_C = D_MODEL // 2
D_D = D_MODEL - D_C
D_IN = 2 * D_MODEL
A_HEADS = 4
A_HEAD_DIM = D_A // A_HEADS
CHUNK = 128
B_CONV_WIDTH = 31
C_CONV_WIDTH = 3
FFN_CONV_WIDTH = 3
S5_GROUP = 16
S5_GROUPS = D_D // S5_GROUP
S5_STATE = 64
D_FF = 2816
EPS = 1e-6
DT_MIN = 1e-3
DT_MAX = 1e-1

kernel_name = 'hybrid_gmlp_conformer_shortconv_s5_decoder_step'


def rms_norm(x, g):
    xf = x.astype(jnp.float32)
    y = xf * lax.rsqrt(jnp.mean(xf * xf, axis=-1, keepdims=True) + EPS)
    return (y * g.astype(jnp.float32)).astype(x.dtype)


def layer_norm(x, g, b):
    xf = x.astype(jnp.float32)
    mu = jnp.mean(xf, axis=-1, keepdims=True)
    var = jnp.mean(jnp.square(xf - mu), axis=-1, keepdims=True)
    y = (xf - mu) * lax.rsqrt(var + EPS)
    return (y * g.astype(jnp.float32) + b.astype(jnp.float32)).astype(x.dtype)


def causal_dwconv(x, w, buf):
    k = w.shape[0]
    if buf is None:
        buf = jnp.zeros((x.shape[0], k - 1, x.shape[2]), x.dtype)
    xp = jnp.concatenate([buf.astype(x.dtype), x], axis=1)
    y = lax.conv_general_dilated(xp, w[:, None, :].astype(x.dtype), window_strides=(1,), padding='VALID',
                                 dimension_numbers=('NWC', 'WIO', 'NWC'), feature_group_count=x.shape[2])
    return y, xp[:, -(k - 1):]


def chunk_spatial_mix(v, w_s, b_s):
    bsz, t, h, dh = v.shape
    n_chunks = -(-t // CHUNK)
    pad = n_chunks * CHUNK - t
    vp = jnp.pad(v, ((0, 0), (0, pad), (0, 0), (0, 0))).reshape(bsz, n_chunks, CHUNK, h, dh)
    mask = jnp.tril(jnp.ones((CHUNK, CHUNK), dtype=bool))
    w = jnp.where(mask[None], w_s, 0).astype(v.dtype)
    mixed = jnp.einsum('hts,bnshd->bnthd', w, vp) + b_s.T[None, None, :, :, None].astype(v.dtype)
    return mixed.reshape(bsz, n_chunks * CHUNK, h, dh)[:, :t]


def s5_scan(u, lam_re, lam_im, log_dt, b_re, b_im, c_re, c_im, d_skip, h0_re, h0_im):
    f32 = jnp.float32
    lam = lax.complex(lam_re.astype(f32), lam_im.astype(f32))
    dt = jnp.exp(log_dt.astype(f32))[:, None]
    lam_bar = jnp.exp(lam * dt)
    b = lax.complex(b_re.astype(f32), b_im.astype(f32))
    b_bar = ((lam_bar - 1.0) / lam)[..., None] * b
    c = lax.complex(c_re.astype(f32), c_im.astype(f32))
    uf = u.astype(f32)
    bu = jnp.einsum('gpc,btgc->btgp', b_bar, uf)
    if h0_re is not None:
        h0 = lax.complex(h0_re.astype(f32), h0_im.astype(f32))
        bu = bu.at[:, 0].add(lam_bar * h0)
    a = jnp.broadcast_to(lam_bar, bu.shape)

    def combine(left, right):
        a_l, b_l = left
        a_r, b_r = right
        return a_l * a_r, a_r * b_l + b_r

    _, h = lax.associative_scan(combine, (a, bu), axis=1)
    y = jnp.einsum('gcp,btgp->btgc', c, h).real + d_skip.astype(f32) * uf
    h_last = h[:, -1]
    return y.astype(u.dtype), h_last.real.astype(u.dtype), h_last.imag.astype(u.dtype)


def even_mixer(xn, w_in, w_out, a_ln_g, a_ln_b, a_ws, a_bs, b_conv_w, b_conv_b, b_ln_g, b_ln_b, buf_b):
    bsz, t, _ = xn.shape
    z = xn @ w_in
    za = jax.nn.gelu(z[..., :2 * D_A])
    zb = z[..., 2 * D_A:]
    u = za[..., :D_A]
    v = layer_norm(za[..., D_A:].reshape(bsz, t, A_HEADS, A_HEAD_DIM), a_ln_g, a_ln_b)
    gate = chunk_spatial_mix(v, a_ws, a_bs).reshape(bsz, t, D_A)
    out_a = u * gate
    glu = zb[..., :D_B] * jax.nn.sigmoid(zb[..., D_B:])
    conv, new_buf_b = causal_dwconv(glu, b_conv_w, buf_b)
    out_b = jax.nn.silu(layer_norm(conv + b_conv_b.astype(conv.dtype), b_ln_g, b_ln_b))
    out = jnp.concatenate([out_a, out_b], axis=-1) @ w_out
    return out, new_buf_b, v.reshape(bsz, t, D_A)


def odd_mixer(xn, w_in, w_out, c_conv_w, lam_re, lam_im, log_dt, b_re, b_im, c_re, c_im, d_skip,
              glu_w, glu_b, buf_c, h0_re, h0_im):
    bsz, t, _ = xn.shape
    z = xn @ w_in
    h_in = z[..., :D_C]
    gate_b = z[..., D_C:2 * D_C]
    gate_c = z[..., 2 * D_C:3 * D_C]
    conv, new_buf_c = causal_dwconv(gate_c * h_in, c_conv_w, buf_c)
    out_c = gate_b * conv
    u = z[..., 3 * D_C:].reshape(bsz, t, S5_GROUPS, S5_GROUP)
    y, h_re, h_im = s5_scan(u, lam_re, lam_im, log_dt, b_re, b_im, c_re, c_im, d_skip, h0_re, h0_im)
    g = jax.nn.gelu(y.reshape(bsz, t, D_D))
    out_d = g * jax.nn.sigmoid(g @ glu_w + glu_b)
    out = jnp.concatenate([out_c, out_d], axis=-1) @ w_out
    return out, new_buf_c, h_re, h_im


def conv_ffn(xn, w_in, conv_w, w_down, buf):
    z = xn @ w_in
    g, new_buf = causal_dwconv(z[..., :D_FF], conv_w, buf)
    return (jax.nn.silu(g) * z[..., D_FF:]) @ w_down, new_buf


def pick(s, i):
    return None if s is None else s[i]


def trunk(x, buf_b, buf_c, ssm_re, ssm_im, buf_f,
          norm_mix, norm_ffn, norm_final, w_mix_in, w_mix_out,
          a_ln_g, a_ln_b, a_ws, a_bs, b_conv_w, b_conv_b, b_ln_g, b_ln_b,
          c_conv_w, s5_lam_re, s5_lam_im, s5_log_dt, s5_b_re, s5_b_im, s5_c_re, s5_c_im, s5_d,
          s5_glu_w, s5_glu_b, ffn_w_in, ffn_conv_w, ffn_w_down):
    v_rows, nb, nc, nre, nim, nf = [], [], [], [], [], []
    for l in range(DEPTH):
        xn = rms_norm(x, norm_mix[l])
        if l % 2 == 0:
            e = l // 2
            out, b_new, v = even_mixer(xn, w_mix_in[l], w_mix_out[l], a_ln_g[e], a_ln_b[e], a_ws[e], a_bs[e],
                                       b_conv_w[e], b_conv_b[e], b_ln_g[e], b_ln_b[e], pick(buf_b, e))
            v_rows.append(v)
            nb.append(b_new)
        else:
            o = l // 2
            out, c_new, h_re, h_im = odd_mixer(xn, w_mix_in[l], w_mix_out[l], c_conv_w[o], s5_lam_re[o],
                                               s5_lam_im[o], s5_log_dt[o], s5_b_re[o], s5_b_im[o], s5_c_re[o],
                                               s5_c_im[o], s5_d[o], s5_glu_w[o], s5_glu_b[o], pick(buf_c, o),
                                               pick(ssm_re, o), pick(ssm_im, o))
            nc.append(c_new)
            nre.append(h_re)
            nim.append(h_im)
        x = x + out
        f, f_new = conv_ffn(rms_norm(x, norm_ffn[l]), ffn_w_in[l], ffn_conv_w[l], ffn_w_down[l], pick(buf_f, l))
        x = x + f
        nf.append(f_new)
    return (rms_norm(x, norm_final), jnp.stack(v_rows), jnp.stack(nb), jnp.stack(nc),
            jnp.stack(nre), jnp.stack(nim), jnp.stack(nf))


def setup_inputs(seed: int = 0) -> dict:
    key = jax.random.key(seed)
    ks = jax.random.split(key, 40)
    nrm = jax.random.normal
    f32 = jnp.float32
    d = {}
    d['x_prompt'] = nrm(ks[0], (BATCH, SEQ, D_MODEL), f32)
    d['x_sample'] = nrm(ks[1], (DEC_BATCH, DEC_SEQ, D_MODEL), f32)
    d['state_conv_b'] = 0.5 * nrm(ks[2], (N_EVEN, DEC_BATCH, B_CONV_WIDTH - 1, D_B), f32)
    d['state_conv_c'] = 0.5 * nrm(ks[3], (N_ODD, DEC_BATCH, C_CONV_WIDTH - 1, D_C), f32)
    d['state_ssm_re'] = 0.5 * nrm(ks[4], (N_ODD, DEC_BATCH, S5_GROUPS, S5_STATE), f32)
    d['state_ssm_im'] = 0.5 * nrm(ks[5], (N_ODD, DEC_BATCH, S5_GROUPS, S5_STATE), f32)
    d['state_ffn_conv'] = nrm(ks[6], (DEPTH, DEC_BATCH, FFN_CONV_WIDTH - 1, D_FF), f32)
    d['norm_mix'] = 1.0 + 0.02 * nrm(ks[7], (DEPTH, D_MODEL), f32)
    d['norm_ffn'] = 1.0 + 0.02 * nrm(ks[8], (DEPTH, D_MODEL), f32)
    d['norm_final'] = 1.0 + 0.02 * nrm(ks[9], (D_MODEL,), f32)
    d['w_mix_in'] = nrm(ks[10], (DEPTH, D_MODEL, D_IN), f32) * D_MODEL ** -0.5
    d['w_mix_out'] = nrm(ks[11], (DEPTH, D_MODEL, D_MODEL), f32) * D_MODEL ** -0.5
    d['a_ln_g'] = 1.0 + 0.02 * nrm(ks[12], (N_EVEN, A_HEADS, A_HEAD_DIM), f32)
    d['a_ln_b'] = 0.02 * nrm(ks[13], (N_EVEN, A_HEADS, A_HEAD_DIM), f32)
    d['a_ws'] = nrm(ks[14], (N_EVEN, A_HEADS, CHUNK, CHUNK), f32) * CHUNK ** -0.5
    d['a_bs'] = 1.0 + 0.02 * nrm(ks[15], (N_EVEN, A_HEADS, CHUNK), f32)
    d['b_conv_w'] = nrm(ks[16], (N_EVEN, B_CONV_WIDTH, D_B), f32) * B_CONV_WIDTH ** -0.5
    d['b_conv_b'] = 0.02 * nrm(ks[17], (N_EVEN, D_B), f32)
    d['b_ln_g'] = 1.0 + 0.02 * nrm(ks[18], (N_EVEN, D_B), f32)
    d['b_ln_b'] = 0.02 * nrm(ks[19], (N_EVEN, D_B), f32)
    d['c_conv_w'] = nrm(ks[20], (N_ODD, C_CONV_WIDTH, D_C), f32) * C_CONV_WIDTH ** -0.5
    d['s5_lam_re'] = -0.5 + 0.01 * nrm(ks[21], (N_ODD, S5_GROUPS, S5_STATE), f32)
    d['s5_lam_im'] = (jnp.pi * jnp.arange(S5_STATE, dtype=f32))[None, None, :] + 0.01 * nrm(ks[22], (N_ODD, S5_GROUPS, S5_STATE), f32)
    d['s5_log_dt'] = jax.random.uniform(ks[23], (N_ODD, S5_GROUPS), f32, minval=math.log(DT_MIN), maxval=math.log(DT_MAX))
    d['s5_b_re'] = nrm(ks[24], (N_ODD, S5_GROUPS, S5_STATE, S5_GROUP), f32) * (2 * S5_GROUP) ** -0.5
    d['s5_b_im'] = nrm(ks[25], (N_ODD, S5_GROUPS, S5_STATE, S5_GROUP), f32) * (2 * S5_GROUP) ** -0.5
    d['s5_c_re'] = nrm(ks[26], (N_ODD, S5_GROUPS, S5_GROUP, S5_STATE), f32) * (2 * S5_STATE) ** -0.5
    d['s5_c_im'] = nrm(ks[27], (N_ODD, S5_GROUPS, S5_GROUP, S5_STATE), f32) * (2 * S5_STATE) ** -0.5
    d['s5_d'] = nrm(ks[28], (N_ODD, S5_GROUPS, S5_GROUP), f32)
    d['s5_glu_w'] = nrm(ks[29], (N_ODD, D_D, D_D), f32) * D_D ** -0.5
    d['s5_glu_b'] = 0.02 * nrm(ks[30], (N_ODD, D_D), f32)
    d['ffn_w_in'] = nrm(ks[31], (DEPTH, D_MODEL, 2 * D_FF), f32) * D_MODEL ** -0.5
    d['ffn_conv_w'] = nrm(ks[32], (DEPTH, FFN_CONV_WIDTH, D_FF), f32) * FFN_CONV_WIDTH ** -0.5
    d['ffn_w_down'] = nrm(ks[33], (DEPTH, D_FF, D_MODEL), f32) * D_FF ** -0.5
    return d


def reference(x_prompt, x_sample, state_conv_b, state_conv_c, state_ssm_re, state_ssm_im, state_ffn_conv,
              norm_mix, norm_ffn, norm_final, w_mix_in, w_mix_out,
              a_ln_g, a_ln_b, a_ws, a_bs, b_conv_w, b_conv_b, b_ln_g, b_ln_b,
              c_conv_w, s5_lam_re, s5_lam_im, s5_log_dt, s5_b_re, s5_b_im, s5_c_re, s5_c_im, s5_d,
              s5_glu_w, s5_glu_b, ffn_w_in, ffn_conv_w, ffn_w_down):
    weights = (norm_mix, norm_ffn, norm_final, w_mix_in, w_mix_out,
               a_ln_g, a_ln_b, a_ws, a_bs, b_conv_w, b_conv_b, b_ln_g, b_ln_b,
               c_conv_w, s5_lam_re, s5_lam_im, s5_log_dt, s5_b_re, s5_b_im, s5_c_re, s5_c_im, s5_d,
               s5_glu_w, s5_glu_b, ffn_w_in, ffn_conv_w, ffn_w_down)
    y_prompt, _, conv_b_p, conv_c_p, ssm_re_p, ssm_im_p, ffn_conv_p = trunk(
        x_prompt, None, None, None, None, None, *weights)
    y_sample, v_rows_s, conv_b_s, conv_c_s, ssm_re_s, ssm_im_s, ffn_conv_s = trunk(
        x_sample, state_conv_b, state_conv_c, state_ssm_re, state_ssm_im, state_ffn_conv, *weights)
    return (y_prompt, y_sample, v_rows_s, conv_b_p, conv_b_s, conv_c_p, conv_c_s,
            ssm_re_p, ssm_re_s, ssm_im_p, ssm_im_s, ffn_conv_p, ffn_conv_s)
```

```python
import math
import os
import numpy as np
import concourse.bass as bass
import concourse.mybir as mybir
from concourse.bass_utils import run_bass_kernel_spmd

F32 = mybir.dt.float32
BF16 = mybir.dt.bfloat16
AF = mybir.ActivationFunctionType
ALU = mybir.AluOpType
AX = mybir.AxisListType

D_MODEL = 1024
SEQ = 2048
DEPTH = 4
D_FF = 2816
NFC = D_FF // 128
EPS = 1e-6
NSAMP = 16
TPH = 1024
TT = TPH + NSAMP
N_CORES = 8
SBUF_BASE = 16512
SBUF_LIMIT = 229376

ENGS = ("pe", "act", "dve", "pool", "sp")
SEM_ROLL = 30000


class Sched:
    def __init__(self, nc, n_dma_sems=20):
        self.nc = nc
        self.ops = {e: [] for e in ENGS}
        self.cur_sem = {}
        self.cnt = {}
        self.nsem = 0
        for e in ENGS:
            self._new_sem(e)
        self.dq = {}
        for q in ("sp", "pool", "act"):
            self.dq[q] = dict(sems=[self._alloc(f"dma_{q}{i}") for i in range(n_dma_sems)],
                              val=[0] * n_dma_sems, rr=0)
        self.seen = {e: {} for e in ENGS}
        self.writer = {}
        self.readers = {}

    def _alloc(self, name):
        self.nsem += 1
        return self.nc.alloc_semaphore(name)

    def _new_sem(self, e):
        self.cur_sem[e] = self._alloc(f"s_{e}_{self.nsem}")
        self.cnt[e] = 0

    def _deps(self, eng, reads, writes):
        deps = {}

        def add(ev, skip_same):
            if ev is None:
                return
            sem, val, oe = ev
            if oe == eng and skip_same:
                return
            k = id(sem)
            if self.seen[eng].get(k, 0) >= val:
                return
            if k not in deps or deps[k][1] < val:
                deps[k] = (sem, val)

        for k in reads:
            add(self.writer.get(k), eng == "pe")
        for k in writes:
            add(self.writer.get(k), eng == "pe")
            for ev in self.readers.get(k, {}).values():
                add(ev, eng == "pe")
        out = list(deps.values())
        for sem, val in out:
            self.seen[eng][id(sem)] = val
        return out

    def _commit(self, ev, reads, writes):
        for k in writes:
            self.writer[k] = ev
            self.readers[k] = {}
        for k in reads:
            self.readers.setdefault(k, {})[(ev[2], id(ev[0]))] = ev

    def op(self, eng, fn, reads=(), writes=()):
        waits = self._deps(eng, reads, writes)
        if self.cnt[eng] >= SEM_ROLL:
            self._new_sem(eng)
        self.cnt[eng] += 1
        sem = self.cur_sem[eng]
        ev = (sem, self.cnt[eng], eng)
        self.ops[eng].append((waits, fn, (sem, 1)))
        self._commit(ev, reads, writes)
        return ev

    def dma(self, q, fn, reads=(), writes=()):
        d = self.dq[q]
        i = d["rr"]
        d["rr"] = (i + 1) % len(d["sems"])
        sem = d["sems"][i]
        waits = self._deps(q, reads, writes)
        pv = d["val"][i]
        if pv > 0 and self.seen[q].get(id(sem), 0) < pv:
            waits.append((sem, pv))
            self.seen[q][id(sem)] = pv
        d["val"][i] += 16
        ev = (sem, d["val"][i], "dma")
        self.ops[q].append((waits, fn, (sem, 16)))
        self._commit(ev, reads, writes)
        return ev

    def fence(self, engs=("pe", "act", "dve", "pool", "sp")):
        evs = [(self.cur_sem[e], self.cnt[e]) for e in engs if self.cnt[e] > 0]
        d = self.dq["sp"]
        evs += [(sem, d["val"][i]) for i, sem in enumerate(d["sems"]) if d["val"][i] > 0]
        for e in engs:
            waits = []
            for sem, val in evs:
                if sem is self.cur_sem[e]:
                    continue
                if self.seen[e].get(id(sem), 0) < val:
                    waits.append((sem, val))
                    self.seen[e][id(sem)] = val
            if waits:
                self.ops[e].append((waits, None, None))

    def finish(self):
        waits = []
        for q, d in self.dq.items():
            for i, sem in enumerate(d["sems"]):
                if d["val"][i] > 0:
                    waits.append((sem, d["val"][i]))
        for e in ("pe", "act", "dve", "pool"):
            if self.cnt[e] > 0:
                waits.append((self.cur_sem[e], self.cnt[e]))
        self.ops["sp"].append((waits, None, None))

    def emit(self):
        nc = self.nc
        engmap = {"pe": "tensor", "act": "scalar", "dve": "vector", "pool": "gpsimd", "sp": "sync"}
        with nc.Block() as block:
            for e in ENGS:
                ops = self.ops[e]

                def body(engine, ops=ops):
                    for waits, fn, inc in ops:
                        for sem, val in waits:
                            engine.wait_ge(sem, val)
                        if fn is not None:
                            ins = fn(engine)
                            if inc is not None:
                                ins.then_inc(inc[0], inc[1])

                getattr(block, engmap[e])(body)


IN_SHAPES = dict(
    xp=(SEQ, D_MODEL), xs=(NSAMP, D_MODEL),
    st_cb=(2, NSAMP, 30, 512), st_cc=(2, NSAMP, 2, 512),
    st_re=(2, NSAMP, 32, 64), st_im=(2, NSAMP, 32, 64), st_ff=(4, NSAMP, 2, D_FF),
    norm_mix=(4, 1024), norm_ffn=(4, 1024), norm_final=(1024,),
    w_mix_in=(4, 1024, 2048), w_mix_out=(4, 1024, 1024),
    a_ln_g=(2, 4, 128), a_ln_b=(2, 4, 128), a_ws=(2, 4, 128, 128), a_bs=(2, 4, 128),
    b_conv_w=(2, 31, 512), b_conv_b=(2, 512), b_ln_g=(2, 512), b_ln_b=(2, 512),
    c_conv_w=(2, 3, 512), s5_lam_re=(2, 32, 64), s5_lam_im=(2, 32, 64), s5_log_dt=(2, 32),
    s5_b_re=(2, 32, 64, 16), s5_b_im=(2, 32, 64, 16), s5_c_re=(2, 32, 16, 64), s5_c_im=(2, 32, 16, 64),
    s5_d=(2, 32, 16), s5_glu_w=(2, 512, 512), s5_glu_b=(2, 512),
    ffn_w_in=(4, 1024, 2 * D_FF), ffn_conv_w=(4, 3, D_FF), ffn_w_down=(4, D_FF, 1024),
    cst_ident=(128, 128), cst_masku=(128, 128), cst_ttab=(128, 128), cst_rowmask=(128, 8), cst_sign=(128, 2), cst_etab=(128, 17), cst_bmask=(128, 128),
)
OUT_SHAPES = dict(
    yp=(SEQ, D_MODEL), ys=(NSAMP, D_MODEL), vrows=(2, NSAMP, 512),
    cbp=(2, 30, 512), cbs=(2, NSAMP, 30, 512), ccp=(2, 2, 512), ccs=(2, NSAMP, 2, 512),
    srp=(2, 32, 64), srs=(2, NSAMP, 32, 64), sip=(2, 32, 64), sis=(2, NSAMP, 32, 64),
    ffp=(4, 2, D_FF), ffs=(4, NSAMP, 2, D_FF),
)


class StopBuild(Exception):
    pass


def build(n_layers=DEPTH):
    DBG = os.environ.get("KDBG", "").split(",")
    STOPAT = None
    for d_ in DBG:
        if d_.startswith("stop="):
            STOPAT = d_[5:]

    def stage_mark(name):
        if STOPAT == name:
            raise StopBuild()
    nc = bass.Bass("TRN2", target_bir_lowering=False)
    S = Sched(nc)
    DI = {k: nc.dram_tensor(k, list(v), F32, kind="ExternalInput") for k, v in IN_SHAPES.items()}
    DO = {k: nc.dram_tensor(k, list(v), F32, kind="ExternalOutput") for k, v in OUT_SHAPES.items()}

    SCR = {}
    for o_ in range(2):
        SCR[f"sm{o_}"] = nc.dram_tensor(f"scr_sm{o_}", [128, 160], F32, kind="Internal")
        for k_ in range(4):
            for nm_ in ("CLT", "BJT", "BJTS", "KT", "CPAD"):
                SCR[f"{nm_}{o_}{k_}"] = nc.dram_tensor(f"scr_{nm_}{o_}{k_}", [128, 1024], BF16, kind="Internal")
            for nm_ in ("COS", "SINP"):
                SCR[f"{nm_}{o_}{k_}"] = nc.dram_tensor(f"scr_{nm_}{o_}{k_}", [128, 1024], F32, kind="Internal")

    def dap(t, off, pat):
        return bass.AP(t, off, [list(p) for p in pat])

    class Bump:
        def __init__(self, base):
            self.off = base
            self.n = 0

        def alloc(self, shape, dt=F32, name=None):
            nbytes = int(np.prod(shape[1:])) * (4 if dt == F32 else 2)
            nbytes = (nbytes + 31) // 32 * 32
            self.n += 1
            t = nc.alloc_sbuf_tensor_at(f"sb_{name or 't'}_{self.off}_{self.n}", list(shape), dt, offset=self.off)
            self.off += nbytes
            assert self.off <= SBUF_LIMIT, (name, self.off)
            return t.ap()

    fx = Bump(SBUF_BASE)
    X = fx.alloc([128, 8, TT], F32, "X")
    XN_OFF = fx.off
    XN = fx.alloc([128, 8, TT], BF16, "XN")
    NRING = 6
    RING = [fx.alloc([128, NFC, 128], BF16, f"ring{i}") for i in range(NRING)]
    PCOL = fx.alloc([128, 648], F32, "PCOL")
    ident = fx.alloc([128, 128], F32, "ident")
    identb = fx.alloc([128, 128], BF16, "identb")
    masku = fx.alloc([128, 128], F32, "masku")
    ttab = fx.alloc([128, 128], F32, "ttab")
    rowmask = fx.alloc([128, 8], F32, "rowmask")
    signc = fx.alloc([128, 2], F32, "signc")
    etab = fx.alloc([128, 17], F32, "etab")
    bmask = fx.alloc([128, 128], F32, "bmask")
    onesb = fx.alloc([128, 128], BF16, "onesb")
    onesf = fx.alloc([128, 128], F32, "onesf")
    cst = fx.alloc([128, 8], F32, "cst")
    CB = fx.alloc([128, 2, 4, 30], F32, "CB")
    CC = fx.alloc([128, 2, 4, 2], F32, "CC")
    CF = fx.alloc([128, 4, NFC, 2], F32, "CF")
    HL = fx.alloc([128, 2, 32], F32, "HL")
    RS_OFF = fx.off
    RS = fx.alloc([128, 512], F32, "RS")
    SQ_OFF = fx.off
    SQ = fx.alloc([128, 8, 512], BF16, "SQ")
    ARENA0 = fx.off

    PS = [nc.alloc_psum_tensor(f"ps{i}", [128, 512], F32).ap() for i in range(8)]
    psrr = {"m": 0, "a": 0}

    def psum(kind="m"):
        if kind in ("ra", "ga"):
            base = 4 if kind == "ra" else 6
            i = psrr.get(kind, 0)
            psrr[kind] = (i + 1) % 2
            return PS[base + i], f"PS{base + i}"
        if kind == "m":
            i = psrr["m"]
            psrr["m"] = (i + 1) % 4
            return PS[i], f"PS{i}"
        i = psrr["a"]
        psrr["a"] = (i + 1) % 4
        return PS[4 + i], f"PS{4 + i}"

    def act(out, in_, func, r, w, bias=None, scale=None):
        kw = {}
        if bias is not None:
            kw["bias"] = bias
        if scale is not None:
            kw["scale"] = scale
        S.op("act", lambda e: e.activation(out=out, in_=in_, func=func, **kw), r, w)

    def tt(out, a, b, op, r, w, eng="dve"):
        S.op(eng, lambda e: e.tensor_tensor(out=out, in0=a, in1=b, op=op), r, w)

    def ts(out, a, s1, s2, op0, op1, r, w, eng="dve"):
        if op1 is None:
            S.op(eng, lambda e: e.tensor_scalar(out=out, in0=a, scalar1=s1, scalar2=None, op0=op0), r, w)
        else:
            S.op(eng, lambda e: e.tensor_scalar(out=out, in0=a, scalar1=s1, scalar2=s2, op0=op0, op1=op1), r, w)

    def stt(out, a, s, b, op0, op1, r, w):
        S.op("dve", lambda e: e.scalar_tensor_tensor(out=out, in0=a, scalar=s, in1=b, op0=op0, op1=op1), r, w)

    def cp(out, in_, r, w, eng="dve"):
        if eng == "act":
            S.op("act", lambda e: e.activation(out=out, in_=in_, func=AF.Copy), r, w)
        else:
            S.op(eng, lambda e: e.tensor_copy(out=out, in_=in_), r, w)

    def memset(ap, v, w, eng="pool"):
        S.op(eng, lambda e: e.memset(ap, v), (), w)

    def mm(ps_ap, pairs, r, w):
        pairs = list(pairs)

        def fn(e):
            n = len(pairs)
            ins = None
            for i, (l, rh) in enumerate(pairs):
                ins = e.matmul(ps_ap, lhsT=l, rhs=rh, start=(i == 0), stop=(i == n - 1))
            return ins
        S.op("pe", fn, r, w)

    def tr(ps_ap, in_, r, w, idt=None):
        k = in_.shape[0]
        idn = (ident if idt is None else idt)[0:k, 0:k]
        S.op("pe", lambda e: e.transpose(out=ps_ap, in_=in_, identity=idn), list(r) + ["ident"], w)

    def dma(out, in_, r, w, q="sp", slow=False):
        if slow:
            S.dma(q, lambda e: e.dma_start(out=out, in_=in_, allow_slow_non_contiguous=True), r, w)
        else:
            S.dma(q, lambda e: e.dma_start(out=out, in_=in_), r, w)

    dma(ident, DI["cst_ident"].ap(), (), ["ident"])
    dma(masku, DI["cst_masku"].ap(), (), ["masku"])
    dma(ttab, DI["cst_ttab"].ap(), (), ["ttab"])
    dma(rowmask, DI["cst_rowmask"].ap(), (), ["rowmask"])
    dma(signc, DI["cst_sign"].ap(), (), ["signc"])
    dma(etab, DI["cst_etab"].ap(), (), ["etab"])
    dma(bmask, DI["cst_bmask"].ap(), (), ["bmask"])
    cp(identb, ident, ["ident"], ["identb"])
    memset(onesb, 1.0, ["onesb"])
    memset(onesf, 1.0 / 512.0, ["onesf"])
    memset(cst[:, 0:1], EPS, ["cst"])
    memset(cst[:, 1:2], 0.0, ["cst"])
    memset(cst[:, 2:3], math.pi / 2, ["cst"])
    memset(cst[:, 3:4], 1.0, ["cst"])
    memset(CB, 0.0, ["CB"])
    memset(CC, 0.0, ["CC"])
    memset(CF, 0.0, ["CF"])
    memset(HL, 0.0, ["HL"])

    pcol_map = {}
    stage_rows = []

    def reg(name, t, off, nrows):
        stage_rows.append((name, t, off, nrows))

    for l in range(4):
        reg(f"nm{l}", DI["norm_mix"], l * 1024, 8)
        reg(f"nf{l}", DI["norm_ffn"], l * 1024, 8)
    reg("nfin", DI["norm_final"], 0, 8)
    for e_ in range(2):
        reg(f"bcw{e_}", DI["b_conv_w"], e_ * 31 * 512, 124)
        reg(f"bcb{e_}", DI["b_conv_b"], e_ * 512, 4)
        reg(f"blg{e_}", DI["b_ln_g"], e_ * 512, 4)
        reg(f"blb{e_}", DI["b_ln_b"], e_ * 512, 4)
        reg(f"ccw{e_}", DI["c_conv_w"], e_ * 3 * 512, 12)
        reg(f"sd{e_}", DI["s5_d"], e_ * 512, 4)
        reg(f"sgb{e_}", DI["s5_glu_b"], e_ * 512, 4)
    for l in range(4):
        reg(f"fcw{l}", DI["ffn_conv_w"], l * 3 * D_FF, 66)
    col = 0
    stage = []
    stages = []
    used = 0
    for item in stage_rows:
        if used + item[3] > 128:
            stages.append(stage)
            stage = []
            used = 0
        stage.append((item, used))
        used += item[3]
    stages.append(stage)
    ar = Bump(ARENA0)
    STG = ar.alloc([128, 128], F32, "STG")
    for si, stage in enumerate(stages):
        nrow = 0
        for (name, t, off, nrows), r0 in stage:
            dma(STG[r0:r0 + nrows, :], dap(t, off, [[128, nrows], [1, 128]]), (), ["STG"])
            pcol_map[name] = col + r0
            nrow = r0 + nrows
        pp, pk = psum("a")
        tr(pp[:, 0:nrow], STG[0:nrow, :], ["STG"], [pk])
        cp(PCOL[:, col:col + nrow], pp[:, 0:nrow], [pk], ["PCOL"])
        col += nrow
    assert col <= 648, col

    def pc(name, j=0, n=1):
        c0 = pcol_map[name] + j
        return PCOL[:, c0:c0 + n]

    plan = []

    def plan_layer(l):
        o = l // 2
        wi, wo = DI["w_mix_in"], DI["w_mix_out"]
        bi, bo = l * 1024 * 2048, l * 1024 * 1024
        if l % 2 == 0 or "evenonly" in DBG:
            order = [0, 1, 2, 3] + [12, 8, 13, 9, 14, 10, 15, 11]
            for n in order:
                plan.append((f"L{l}in{n}", wi, bi + n * 128, 2048, 8))
        else:
            order = []
            for c in range(4):
                order += [c, 8 + c, 4 + c]
            order += [12, 13, 14, 15]
            for n in order:
                plan.append((f"L{l}in{n}", wi, bi + n * 128, 2048, 8))
            for n in range(4):
                plan.append((f"L{l}glu{n}", DI["s5_glu_w"], o * 512 * 512 + n * 128, 512, 4))
        for n in range(8):
            plan.append((f"L{l}out{n}", wo, bo + n * 128, 1024, 8))
        fi, fd = DI["ffn_w_in"], DI["ffn_w_down"]
        for f in range(NFC):
            plan.append((f"L{l}f1_{f}", fi, l * 1024 * 5632 + f * 128, 5632, 8))
            plan.append((f"L{l}f2_{f}", fi, l * 1024 * 5632 + D_FF + f * 128, 5632, 8))
        for n in range(8):
            plan.append((f"L{l}dn{n}", fd, l * D_FF * 1024 + n * 128, 1024, NFC))

    LAYERS = list(range(n_layers))
    for d_ in DBG:
        if d_.startswith("layers="):
            LAYERS = [int(c) for c in d_[7:]]
    for half in range(2):
        for l in LAYERS:
            plan_layer(l)
    wstate = {"issued": 0, "next": 0}

    def w_issue_upto(i):
        while wstate["issued"] <= min(i, len(plan) - 1):
            j = wstate["issued"]
            tag, t, off, rs, kc = plan[j]
            slot = j % NRING
            src = dap(t, off, [[rs, 128], [rs * 128, kc], [1, 128]])
            dst = RING[slot][:, 0:kc, :]
            S.dma("pool", lambda e, dst=dst, src=src: e.dma_start(out=dst, in_=src), (), [f"RING{slot}"])
            wstate["issued"] += 1

    def wnext(tag):
        i = wstate["next"]
        assert plan[i][0] == tag, (plan[i][0], tag)
        w_issue_upto(i + NRING - 1)
        wstate["next"] += 1
        slot = i % NRING
        return RING[slot], f"RING{slot}"

    def tiles_of(half):
        t = [(0, 512), (512, 1024)]
        if half == 1:
            t.append((1024, TT))
        return t

    def kx(c, ti):
        return f"X{c}.{ti}"

    def kxn(c, ti):
        return f"XN{c}.{ti}"

    def rmsnorm(half, gname, final=False):
        for ti, (a, b) in enumerate(tiles_of(half)):
            n = b - a
            for c in range(8):
                act(SQ[:, c, 0:n], X[:, c, a:b], AF.Square, [kx(c, ti)], [f"SQ{c}"])
            pp, pk = psum("a")
            mm(pp[:, 0:n], [(onesb, SQ[:, c, 0:n]) for c in range(8)], [f"SQ{c}" for c in range(8)] + ["onesb"], [pk])
            act(RS[:, 0:n], pp[:, 0:n], AF.Sqrt, [pk, "cst"], ["RS"], bias=cst[:, 0:1], scale=1.0 / 1024.0)
            S.op("dve", lambda e, n=n: e.reciprocal(out=RS[:, 0:n], in_=RS[:, 0:n]), ["RS"], ["RS"])
            for c in range(8):
                if final:
                    stt(X[:, c, a:b], X[:, c, a:b], pc(gname, c), RS[:, 0:n], ALU.mult, ALU.mult,
                        [kx(c, ti), "RS", "PCOL"], [kx(c, ti)])
                else:
                    stt(XN[:, c, a:b], X[:, c, a:b], pc(gname, c), RS[:, 0:n], ALU.mult, ALU.mult,
                        [kx(c, ti), "RS", "PCOL"], [kxn(c, ti)])

    def panel_mm(pan, pkey, kc, rhs_fn, rkeys_fn, half, consume):
        for ti, (a, b) in enumerate(tiles_of(half)):
            n = b - a
            pp, pk = psum("m")
            mm(pp[:, 0:n], [(pan[:, k, :], rhs_fn(k, a, b)) for k in range(kc)], [pkey] + rkeys_fn(ti), [pk])
            consume(ti, a, b, pp[:, 0:n], pk)

    def xn_rhs(k, a, b):
        return XN[:, k, a:b]

    def xn_keys(ti):
        return [kxn(c, ti) for c in range(8)]

    def load_x(half):
        ar = Bump(ARENA0)
        XT = [ar.alloc([128, 1024], F32, f"XT{i}") for i in range(4)]
        for tb in range(8):
            t0 = half * TPH + tb * 128
            buf = XT[tb % 4]
            dma(buf, dap(DI["xp"], t0 * 1024, [[1024, 128], [1, 1024]]), (), [f"XT{tb % 4}"])
            for g in range(2):
                pp, pk = psum("a")
                for j in range(4):
                    c = g * 4 + j
                    tr(pp[:, j * 128:(j + 1) * 128], buf[:, c * 128:(c + 1) * 128], [f"XT{tb % 4}"], [pk])
                ti = tb // 4
                dst = X[:, g * 4:g * 4 + 4, tb * 128:(tb + 1) * 128]
                src = pp.rearrange("p (j t) -> p j t", j=4)
                cp(dst, src, [pk], [kx(g * 4 + j, ti) for j in range(4)], eng=("act" if g == 0 else "dve"))
        if half == 1:
            XS = ar.alloc([NSAMP, 1024], F32, "XS")
            dma(XS, DI["xs"].ap(), (), ["XS"])
            pp, pk = psum("a")
            for c in range(8):
                tr(pp[:, c * 16:(c + 1) * 16], XS[:, c * 128:(c + 1) * 128], ["XS"], [pk])
            cp(X[:, :, TPH:TT], pp[:, 0:128].rearrange("p (c t) -> p c t", c=8), [pk], [kx(c, 2) for c in range(8)])

    def store_y(half):
        ar = Bump(ARENA0)
        YT = [ar.alloc([128, 1024], F32, f"YT{i}") for i in range(2)]
        for tb in range(8):
            t0 = half * TPH + tb * 128
            buf = YT[tb % 2]
            for g in range(2):
                pp, pk = psum("a")
                for j in range(4):
                    c = g * 4 + j
                    tr(pp[:, j * 128:(j + 1) * 128], X[:, c, tb * 128:(tb + 1) * 128], [kx(c, tb // 4)], [pk])
                cp(buf[:, g * 512:(g + 1) * 512], pp, [pk], [f"YT{tb % 2}"], eng=("act" if g == 0 else "dve"))
            dma(dap(DO["yp"], t0 * 1024, [[1024, 128], [1, 1024]]), buf, [f"YT{tb % 2}"], ())
        if half == 1:
            YS = ar.alloc([NSAMP, 1024], F32, "YS")
            for g in range(2):
                pp, pk = psum("a")
                for j in range(4):
                    c = g * 4 + j
                    tr(pp[0:NSAMP, j * 128:(j + 1) * 128], X[:, c, TPH:TT], [kx(c, 2)], [pk])
                cp(YS[:, g * 512:(g + 1) * 512], pp[0:NSAMP, :], [pk], ["YS"])
            dma(DO["ys"].ap(), YS, ["YS"], ())

    def resid_consumer(n_chunk):
        def consume(ti, a, b, pp, pk):
            tt(X[:, n_chunk, a:b], X[:, n_chunk, a:b], pp, ALU.add, [kx(n_chunk, ti), pk], [kx(n_chunk, ti)])
        return consume

    def ffn(half, l):
        S.fence()
        ar = Bump(ARENA0)
        HID = ar.alloc([128, NFC, TT], BF16, "HID")
        Z1 = [ar.alloc([128, 2 + TT], F32, f"Z1_{i}") for i in range(2)]
        ACC = [ar.alloc([128, TT], F32, f"ACCF{i}") for i in range(2)]
        SIL = [ar.alloc([128, TT], F32, f"SIL{i}") for i in range(2)]
        SHF = ar.alloc([128, NFC, 32], F32, "SHF") if half == 1 else None
        ZS = ar.alloc([128, NFC, 18], F32, "ZS") if half == 1 else None
        STF = ar.alloc([32, D_FF], F32, "STF") if half == 1 else None
        OTF = ar.alloc([18, D_FF], F32, "OTF") if half == 1 else None
        tiles = tiles_of(half)
        nt = len(tiles)
        rmsnorm(half, f"nf{l}")
        if half == 1:
            dma(STF, dap(DI["st_ff"], l * NSAMP * 2 * D_FF, [[D_FF, 32], [1, D_FF]]), (), ["STF"])
            for f4 in range(0, NFC, 4):
                nn = min(4, NFC - f4)
                pp, pk = psum("a")
                for j in range(nn):
                    tr(pp[:, j * 32:(j + 1) * 32], STF[:, (f4 + j) * 128:(f4 + j + 1) * 128], ["STF"], [pk])
                cp(SHF[:, f4:f4 + nn, :], pp[:, 0:nn * 32].rearrange("p (j t) -> p j t", j=nn), [pk], ["SHF"])
            dma(dap(DO["ffs"], l * NSAMP * 2 * D_FF, [[2 * D_FF, NSAMP], [1, D_FF]]),
                dap(DI["st_ff"], l * NSAMP * 2 * D_FF + D_FF, [[2 * D_FF, NSAMP], [1, D_FF]]), (), ())
        pre = None
        if half == 0 and (l + 1) in LAYERS and (l + 1) % 2 == 1 and "evenonly" not in DBG:
            pre = s5_pregen(l + 1, ar)

        def pump(n_):
            nonlocal pre
            for _ in range(n_):
                if pre is None:
                    return
                try:
                    next(pre)
                except StopIteration:
                    pre = None
        for f in range(NFC):
            z1 = Z1[f % 2]
            acc = ACC[f % 2]
            sil = SIL[f % 2]
            zk, ak, sk = f"Z1_{f % 2}", f"ACCF{f % 2}", f"SIL{f % 2}"
            w0, w1, w2 = pc(f"fcw{l}", 0 * NFC + f), pc(f"fcw{l}", 1 * NFC + f), pc(f"fcw{l}", 2 * NFC + f)
            cp(z1[:, 0:2], CF[:, l, f, :], ["CF"], [zk + ".h"], eng="pool")
            pan, pkey = wnext(f"L{l}f1_{f}")

            def cons1(ti, a, b, pp, pk, z1=z1, zk=zk):
                cp(z1[:, 2 + a:2 + b], pp, [pk], [f"{zk}.{ti}"], eng="act")
                pump(2)
            panel_mm(pan, pkey, 8, xn_rhs, xn_keys, half, cons1)
            zkeys = [zk + ".h"] + [f"{zk}.{ti}" for ti in range(2)]
            ts(acc[:, 0:TPH], z1[:, 0:TPH], w0, None, ALU.mult, None, zkeys + ["PCOL"], [ak])
            pump(1)
            stt(acc[:, 0:TPH], z1[:, 1:TPH + 1], w1, acc[:, 0:TPH], ALU.mult, ALU.add, zkeys + ["PCOL", ak], [ak])
            pump(1)
            stt(acc[:, 0:TPH], z1[:, 2:TPH + 2], w2, acc[:, 0:TPH], ALU.mult, ALU.add, zkeys + ["PCOL", ak], [ak])
            cp(CF[:, l, f, :], z1[:, TPH:TPH + 2], zkeys, ["CF"], eng="pool")
            if half == 1:
                sh = SHF[:, f, :].rearrange("p (t r) -> p t r", r=2)
                zs = z1[:, 2 + TPH:2 + TT]
                ts(acc[:, TPH:TT], sh[:, :, 0], w0, None, ALU.mult, None, ["SHF", "PCOL"], [ak + "s"])
                stt(acc[:, TPH:TT], sh[:, :, 1], w1, acc[:, TPH:TT], ALU.mult, ALU.add, ["SHF", "PCOL", ak + "s"], [ak + "s"])
                stt(acc[:, TPH:TT], zs, w2, acc[:, TPH:TT], ALU.mult, ALU.add, [f"{zk}.2", "PCOL", ak + "s"], [ak + "s"])
                cp(ZS[:, f, 0:2], z1[:, TPH:TPH + 2], zkeys, ["ZS"], eng="pool")
                cp(ZS[:, f, 2:18], zs, [f"{zk}.2"], ["ZS"], eng="pool")
            ncol = TT if half == 1 else TPH
            akeys = [ak] + ([ak + "s"] if half == 1 else [])
            act(sil[:, 0:ncol], acc[:, 0:ncol], AF.Silu, akeys, [sk])
            pan, pkey = wnext(f"L{l}f2_{f}")

            def cons2(ti, a, b, pp, pk, sil=sil, sk=sk, f=f):
                tt(HID[:, f, a:b], sil[:, a:b], pp, ALU.mult, [sk, pk], [f"HID{f}.{ti}"])
                pump(2)
            panel_mm(pan, pkey, 8, xn_rhs, xn_keys, half, cons2)
        if half == 1:
            for f4 in range(0, NFC, 4):
                nn = min(4, NFC - f4)
                pp, pk = psum("a")
                for j in range(nn):
                    tr(pp[0:18, j * 128:(j + 1) * 128], ZS[:, f4 + j, :], ["ZS"], [pk])
                cp(OTF[:, f4 * 128:(f4 + nn) * 128], pp[0:18, 0:nn * 128], [pk], ["OTF"])
            dma(dap(DO["ffp"], l * 2 * D_FF, [[D_FF, 2], [1, D_FF]]), OTF[0:2, :], ["OTF"], ())
            dma(dap(DO["ffs"], l * NSAMP * 2 * D_FF + D_FF, [[2 * D_FF, NSAMP], [1, D_FF]]), OTF[2:18, :], ["OTF"], ())
        pump(100000)
        for n in range(8):
            pan, pkey = wnext(f"L{l}dn{n}")
            panel_mm(pan, pkey, NFC, lambda k, a, b: HID[:, k, a:b],
                     lambda ti: [f"HID{f}.{ti}" for f in range(NFC)], half, resid_consumer(n))

    def mixer_even(half, l):
        e_ = l // 2
        S.fence()
        ar = Bump(ARENA0)
        UA = ar.alloc([128, 4, TT], F32, "UA")
        GLU = ar.alloc([128, 4, 30 + TPH], BF16, "GLUB")
        GLT = ar.alloc([128, 4, 30], F32, "GLT")
        GLS = ar.alloc([128, 4, NSAMP], F32, "GLS")
        DG = ar.alloc([128, 31, 128], BF16, "DG")
        MIXO = ar.alloc([128, 8, TT], BF16, "MIXO")
        WV = ar.alloc([128, 8, 512], BF16, "WV")
        SIGF = ar.alloc([128, TT], F32, "SIGF")
        OFF_VG = ar.off
        VG = [ar.alloc([128, 512], F32, f"VG{i}") for i in range(2)]
        VF = [ar.alloc([128, 512], F32, f"VF{i}") for i in range(2)]
        VB = [ar.alloc([128, 512], BF16, f"VB{i}") for i in range(2)]
        STT_ = [ar.alloc([128, 4, 6], F32, f"BST{i}") for i in range(2)]
        MV = [ar.alloc([128, 4, 2], F32, f"MV{i}") for i in range(2)]
        RSD = [ar.alloc([128, 4], F32, f"RSD{i}") for i in range(2)]
        GT = ar.alloc([128, 512], F32, "GT")
        BT = ar.alloc([128, 512], F32, "BT")
        BS = ar.alloc([128, 512], F32, "BS")
        WS = ar.alloc([128, 4, 128], F32, "WS")
        WMT = ar.alloc([128, 4, 128], BF16, "WMT")
        W00 = ar.alloc([NSAMP, 4], F32, "W00")
        W00D = ar.alloc([NSAMP, 4, NSAMP], BF16, "W00D")
        GTMP = [ar.alloc([128, 512], F32, f"GTMP{i}") for i in range(2)]
        SHB = ar.alloc([128, 4, NSAMP, 30], F32, "SHB")
        MEAN = ar.alloc([128, 512], F32, "MEAN")
        VAR = ar.alloc([128, 512], F32, "VAR")
        SQF = [ar.alloc([128, 512], F32, f"SQF{i}") for i in range(2)]
        T1 = [ar.alloc([128, 512], F32, f"T1_{i}") for i in range(2)]
        STB = GT[0:120, :]
        PRD = BT[:, 0:NSAMP * 30].rearrange("p (t k) -> p t k", k=30)
        OTB = BS[0:30, :]
        OTS = GTMP[0][0:NSAMP, :]
        DG2 = nc.alloc_sbuf_tensor_at(f"sb_DG2_{OFF_VG}_{l}_{half}", [128, 31, 128], BF16, offset=OFF_VG).ap()
        DGS = [(DG, ["DG"]), (DG2, ["VG0", "VG1", "VF0", "VF1"])]
        tiles = tiles_of(half)
        rmsnorm(half, f"nm{l}")
        dma(WV, dap(DI["w_mix_in"], l * 1024 * 2048 + 512, [[2048, 128], [2048 * 128, 8], [1, 512]]), (), ["WV"], q="pool")
        dma(GT, dap(DI["a_ln_g"], e_ * 512, [[0, 128], [1, 512]]), (), ["GT"])
        dma(BT, dap(DI["a_ln_b"], e_ * 512, [[0, 128], [1, 512]]), (), ["BT"])
        dma(BS, dap(DI["a_bs"], e_ * 512, [[0, 128], [1, 512]]), (), ["BS"])
        dma(WS, dap(DI["a_ws"], e_ * 4 * 128 * 128, [[128, 128], [128 * 128, 4], [1, 128]]), (), ["WS"])
        pp, pk = psum("a")
        for h in range(4):
            tr(pp[:, h * 128:(h + 1) * 128], WS[:, h, :], ["WS"], [pk])
        tt(WMT, pp.rearrange("p (h t) -> p h t", h=4), masku.unsqueeze(1).to_broadcast([128, 4, 128]), ALU.mult,
           [pk, "masku"], ["WMT"])
        if half == 1:
            dma(W00, dap(DI["a_ws"], e_ * 4 * 128 * 128, [[0, NSAMP], [128 * 128, 4]]), (), ["W00"], slow=True)
            for h in range(4):
                ts(W00D[:, h, :], ident[0:NSAMP, 0:NSAMP], W00[:, h:h + 1], None, ALU.mult, None, ["ident", "W00"], ["W00D"])
        for c in range(4):
            pan, pkey = wnext(f"L{l}in{c}")

            def cons(ti, a, b, pp, pk, c=c):
                act(UA[:, c, a:b], pp, AF.Gelu_apprx_tanh, [pk], [f"UA{c}.{ti}"])
            panel_mm(pan, pkey, 8, xn_rhs, xn_keys, half, cons)
        nblk = 8 + (1 if half == 1 else 0)
        def e2_block(blk):
            if True:
                samp = blk == 8
                nt = NSAMP if samp else 128
                a = TPH if samp else blk * 128
                ti = 2 if samp else blk // 4
                i2 = blk % 2
                pb_ = 4 + 2 * i2
                pp, pk = PS[pb_], f"PS{pb_}"
                mm(pp[0:nt, :], [(XN[:, k, a:a + nt], WV[:, k, :]) for k in range(8)], xn_keys(ti) + ["WV"], [pk])
                yield
                vg, vf, vb = VG[i2], VF[i2], VB[i2]
                act(vg[0:nt, :], pp[0:nt, :], AF.Gelu_apprx_tanh, [pk], [f"VG{i2}"])
                yield
                for h in range(4):
                    S.op("dve", lambda e, h=h, vg=vg, i2=i2, nt=nt: e.bn_stats(out=STT_[i2][0:nt, h, :], in_=vg[0:nt, h * 128:(h + 1) * 128]),
                         [f"VG{i2}"], [f"BST{i2}"])
                yield
                for h in range(4):
                    S.op("dve", lambda e, h=h, i2=i2, nt=nt: e.bn_aggr(out=MV[i2][0:nt, h, :], in_=STT_[i2][0:nt, h, :]),
                         [f"BST{i2}"], [f"MV{i2}"])
                yield
                yield
                act(RSD[i2][0:nt, :], MV[i2][0:nt, :, 1], AF.Sqrt, [f"MV{i2}", "cst"], [f"RSD{i2}"], bias=cst[0:nt, 0:1], scale=1.0)
                S.op("dve", lambda e, i2=i2, nt=nt: e.reciprocal(out=RSD[i2][0:nt, :], in_=RSD[i2][0:nt, :]), [f"RSD{i2}"], [f"RSD{i2}"])
                yield
                for h in range(4):
                    ts(vf[0:nt, h * 128:(h + 1) * 128], vg[0:nt, h * 128:(h + 1) * 128], MV[i2][0:nt, h, 0:1], RSD[i2][0:nt, h:h + 1],
                       ALU.subtract, ALU.mult, [f"VG{i2}", f"MV{i2}", f"RSD{i2}"], [f"VF{i2}"])
                yield
                tt(vf[0:nt, :], vf[0:nt, :], GT[0:nt, :], ALU.mult, [f"VF{i2}", "GT"], [f"VF{i2}"])
                yield
                if samp:
                    tt(vf[0:nt, :], vf[0:nt, :], BT[0:nt, :], ALU.add, [f"VF{i2}", "BT"], [f"VF{i2}"])
                    yield
                    cp(vb[0:nt, :], vf[0:nt, :], [f"VF{i2}"], [f"VB{i2}"], eng="act")
                else:
                    tt(vb[0:nt, :], vf[0:nt, :], BT[0:nt, :], ALU.add, [f"VF{i2}", "BT"], [f"VB{i2}"])
                yield
                if samp:
                    dma(dap(DO["vrows"], e_ * NSAMP * 512, [[512, NSAMP], [1, 512]]), vf[0:nt, :], [f"VF{i2}"], ())
                pg, pgk = PS[pb_ + 1], f"PS{pb_ + 1}"
                gt_ = GTMP[i2]
                if not samp:
                    for h in range(4):
                        mm(pg[:, h * 128:(h + 1) * 128], [(vb[:, h * 128:(h + 1) * 128], WMT[:, h, :])], [f"VB{i2}", "WMT"], [pgk])
                    yield
                    tt(gt_, pg, BS, ALU.add, [pgk, "BS"], [f"GTMP{i2}"])
                    yield
                    tt(MIXO[:, 0:4, a:a + 128], gt_.rearrange("p (h t) -> p h t", h=4), UA[:, :, a:a + 128], ALU.mult,
                       [f"GTMP{i2}"] + [f"UA{c}.{ti}" for c in range(4)], [f"MIXO{c}.{ti}" for c in range(4)])
                else:
                    for h in range(4):
                        mm(pg[:, h * NSAMP:(h + 1) * NSAMP], [(vb[0:NSAMP, h * 128:(h + 1) * 128], W00D[:, h, :])], [f"VB{i2}", "W00D"], [pgk])
                    bsv = BS.rearrange("p (h t) -> p h t", h=4)[:, :, 0:1].to_broadcast([128, 4, NSAMP])
                    g3 = gt_[:, 0:4 * NSAMP].rearrange("p (h t) -> p h t", h=4)
                    tt(g3, pg[:, 0:4 * NSAMP].rearrange("p (h t) -> p h t", h=4), bsv, ALU.add, [pgk, "BS"], [f"GTMP{i2}"])
                    tt(MIXO[:, 0:4, TPH:TT], g3, UA[:, :, TPH:TT], ALU.mult,
                       [f"GTMP{i2}"] + [f"UA{c}.2" for c in range(4)], [f"MIXO{c}.2" for c in range(4)])
                yield
        cp(GLU[:, :, 0:30], CB[:, e_, :, :], ["CB"], ["GLU.h"], eng="pool")
        def e3a_gen():
            for c in range(4):
                pan, pkey = wnext(f"L{l}in{12 + c}")

                def consg(ti, a, b, pp, pk, c=c):
                    act(SIGF[:, a:b], pp, AF.Sigmoid, [pk], [f"SIGF.{ti}"])
                panel_mm(pan, pkey, 8, xn_rhs, xn_keys, half, consg)
                yield
                pan2, pkey2 = wnext(f"L{l}in{8 + c}")

                def consa(ti, a, b, pp, pk, c=c):
                    if ti < 2:
                        tt(GLU[:, c, 30 + a:30 + b], pp, SIGF[:, a:b], ALU.mult, [pk, f"SIGF.{ti}"], [f"GLU{c}.{ti}"])
                        if ti == 1:
                            tt(GLT[:, c, :], pp[:, 482:512], SIGF[:, b - 30:b], ALU.mult, [pk, f"SIGF.{ti}"], ["GLT"])
                    else:
                        tt(GLS[:, c, :], pp, SIGF[:, a:b], ALU.mult, [pk, f"SIGF.{ti}"], ["GLS"])
                panel_mm(pan2, pkey2, 8, xn_rhs, xn_keys, half, consa)
                yield
        pending = list(range(nblk))
        streams = [e3a_gen()]
        nact = 0
        blkset = set()
        while streams or pending:
            while pending and nact < 2:
                g_ = e2_block(pending.pop(0))
                blkset.add(id(g_))
                streams.append(g_)
                nact += 1
            for st_ in list(streams):
                try:
                    next(st_)
                except StopIteration:
                    streams.remove(st_)
                    if id(st_) in blkset:
                        nact -= 1
        if half == 1:
            for rt in range(4):
                dma(STB, dap(DI["st_cb"], e_ * NSAMP * 30 * 512 + rt * 120 * 512, [[512, 120], [1, 512]]), (), ["GT"])
                pp, pk = psum("a")
                for c in range(4):
                    tr(pp[:, c * 120:(c + 1) * 120], STB[:, c * 128:(c + 1) * 128], ["GT"], [pk])
                cp(SHB[:, :, rt * 4:(rt + 1) * 4, :].rearrange("p c t k -> p c (t k)"),
                   pp[:, 0:480].rearrange("p (c x) -> p c x", c=4), [pk], ["SHB"])
            dma(dap(DO["cbs"], e_ * NSAMP * 30 * 512, [[30 * 512, NSAMP], [1, 29 * 512]]),
                dap(DI["st_cb"], e_ * NSAMP * 30 * 512 + 512, [[30 * 512, NSAMP], [1, 29 * 512]]), (), ())
        def build_dg(c):
            dg, dk = DGS[c % 2]
            for k in range(31):
                act(dg[:, k, :], ident, AF.Copy, ["ident", "PCOL"], dk, scale=pc(f"bcw{e_}", k * 4 + c))
        build_dg(0)
        for c in range(4):
            gk = ["GLU.h"] + [f"GLU{c}.{ti}" for ti in range(2)]
            if c < 3:
                build_dg(c + 1)
            dg, dk = DGS[c % 2]
            for ti in range(2):
                a = ti * 512
                pp, pk = psum("m")
                mm(pp, [(dg[:, k, :], GLU[:, c, a + k:a + k + 512]) for k in range(31)], gk + dk, [pk])
                ts(UA[:, c, a:a + 512], pp, pc(f"bcb{e_}", c), None, ALU.add, None, [pk, "PCOL"], [f"UA{c}.{ti}"])
            if half == 1:
                wrow = PCOL[:, pcol_map[f"bcw{e_}"] + c: pcol_map[f"bcw{e_}"] + c + 4 * 29 + 1: 4]
                tt(PRD, SHB[:, c, :, :], wrow.unsqueeze(1).to_broadcast([128, NSAMP, 30]), ALU.mult, ["SHB", "PCOL"], ["BT"])
                S.op("dve", lambda e, c=c: e.tensor_reduce(out=UA[:, c, TPH:TT], in_=PRD, axis=AX.X, op=ALU.add), ["BT"], [f"UA{c}.2"])
                stt(UA[:, c, TPH:TT], GLS[:, c, :], pc(f"bcw{e_}", 30 * 4 + c), UA[:, c, TPH:TT], ALU.mult, ALU.add,
                    ["GLS", "PCOL", f"UA{c}.2"], [f"UA{c}.2"])
                ts(UA[:, c, TPH:TT], UA[:, c, TPH:TT], pc(f"bcb{e_}", c), None, ALU.add, None, [f"UA{c}.2", "PCOL"], [f"UA{c}.2"])
        cp(CB[:, e_, :, :], GLT, ["GLT"], ["CB"], eng="pool")
        if half == 1:
            pp, pk = psum("a")
            for c in range(4):
                tr(pp[0:30, c * 128:(c + 1) * 128], GLT[:, c, :], ["GLT"], [pk])
            cp(OTB, pp[0:30, :], [pk], ["BS"])
            dma(dap(DO["cbp"], e_ * 30 * 512, [[512, 30], [1, 512]]), OTB, ["BS"], ())
            pp, pk = psum("a")
            for c in range(4):
                tr(pp[0:NSAMP, c * 128:(c + 1) * 128], GLS[:, c, :], ["GLS"], [pk])
            cp(OTS, pp[0:NSAMP, :], [pk], ["GTMP0"])
            dma(dap(DO["cbs"], e_ * NSAMP * 30 * 512 + 29 * 512, [[30 * 512, NSAMP], [1, 512]]), OTS, ["GTMP0"], ())
        for ti, (a, b) in enumerate(tiles):
            n = b - a
            pm, pmk = psum("a")
            mm(pm[:, 0:n], [(onesf, UA[:, c, a:b]) for c in range(4)], [f"UA{c}.{ti}" for c in range(4)] + ["onesf"], [pmk])
            p2, p2k = psum("a")
            for c in range(4):
                act(SQF[c % 2][:, 0:n], UA[:, c, a:b], AF.Square, [f"UA{c}.{ti}"], [f"SQF{c % 2}"])
                S.op("pe", lambda e, c=c, n=n, p2=p2: e.matmul(p2[:, 0:n], lhsT=onesf, rhs=SQF[c % 2][:, 0:n], start=(c == 0), stop=(c == 3)),
                     [f"SQF{c % 2}", "onesf"], [p2k])
            cp(MEAN[:, 0:n], pm[:, 0:n], [pmk], ["MEAN"], eng="act")
            tt(VAR[:, 0:n], MEAN[:, 0:n], MEAN[:, 0:n], ALU.mult, ["MEAN"], ["VAR"])
            tt(VAR[:, 0:n], p2[:, 0:n], VAR[:, 0:n], ALU.subtract, [p2k, "VAR"], ["VAR"])
            act(VAR[:, 0:n], VAR[:, 0:n], AF.Sqrt, ["VAR", "cst"], ["VAR"], bias=cst[:, 0:1], scale=1.0)
            S.op("dve", lambda e, n=n: e.reciprocal(out=VAR[:, 0:n], in_=VAR[:, 0:n]), ["VAR"], ["VAR"])
            for c in range(4):
                t1 = T1[c % 2]
                tt(t1[:, 0:n], UA[:, c, a:b], MEAN[:, 0:n], ALU.subtract, [f"UA{c}.{ti}", "MEAN"], [f"T1_{c % 2}"])
                tt(t1[:, 0:n], t1[:, 0:n], VAR[:, 0:n], ALU.mult, [f"T1_{c % 2}", "VAR"], [f"T1_{c % 2}"])
                act(MIXO[:, 4 + c, a:b], t1[:, 0:n], AF.Silu, [f"T1_{c % 2}", "PCOL"], [f"MIXO{4 + c}.{ti}"],
                    bias=pc(f"blb{e_}", c), scale=pc(f"blg{e_}", c))
        for n_ in range(8):
            pan, pkey = wnext(f"L{l}out{n_}")
            panel_mm(pan, pkey, 8, lambda k, a, b: MIXO[:, k, a:b], lambda ti: [f"MIXO{c}.{ti}" for c in range(8)],
                     half, resid_consumer(n_))

    def mixer_odd(half, l):
        o = l // 2
        S.fence()
        ar = Bump(ARENA0)
        U32 = ar.alloc([128, 4, TT], F32, "UF")
        UBF = ar.alloc([128, 4, TT], BF16, "UBF")
        MIXO = ar.alloc([128, 8, TT], BF16, "MIXO")
        part2_base = ar.off
        P = [ar.alloc([128, 2 + TT], F32, f"P{i}") for i in range(2)]
        HIN = [ar.alloc([128, TT], F32, f"HIN{i}") for i in range(2)]
        CV = [ar.alloc([128, TT], F32, f"CV{i}") for i in range(2)]
        SHC = ar.alloc([128, 4, 32], F32, "SHC")
        STC = ar.alloc([32, 512], F32, "STC")
        ZC = ar.alloc([128, 4, 18], F32, "ZC")
        OTC = ar.alloc([18, 512], F32, "OTC")
        tiles = tiles_of(half)
        rmsnorm(half, f"nm{l}")
        if half == 1:
            dma(STC, dap(DI["st_cc"], o * NSAMP * 2 * 512, [[512, 32], [1, 512]]), (), ["STC"])
            pp, pk = psum("a")
            for c in range(4):
                tr(pp[:, c * 32:(c + 1) * 32], STC[:, c * 128:(c + 1) * 128], ["STC"], [pk])
            cp(SHC, pp[:, 0:128].rearrange("p (c t) -> p c t", c=4), [pk], ["SHC"])
            dma(dap(DO["ccs"], o * NSAMP * 2 * 512, [[2 * 512, NSAMP], [1, 512]]),
                dap(DI["st_cc"], o * NSAMP * 2 * 512 + 512, [[2 * 512, NSAMP], [1, 512]]), (), ())
        stage_mark("O1")
        for c in range(4):
            hin, p, cv = HIN[c % 2], P[c % 2], CV[c % 2]
            hk, pk_, ck = f"HIN{c % 2}", f"P{c % 2}", f"CV{c % 2}"
            w0, w1, w2 = pc(f"ccw{o}", 0 * 4 + c), pc(f"ccw{o}", 1 * 4 + c), pc(f"ccw{o}", 2 * 4 + c)
            cp(p[:, 0:2], CC[:, o, c, :], ["CC"], [pk_ + ".h"], eng="pool")
            pan, pkey = wnext(f"L{l}in{c}")

            def c1(ti, a, b, pp, pk, hin=hin, hk=hk):
                cp(hin[:, a:b], pp, [pk], [f"{hk}.{ti}"], eng="act")
            panel_mm(pan, pkey, 8, xn_rhs, xn_keys, half, c1)
            pan, pkey = wnext(f"L{l}in{8 + c}")

            def c2(ti, a, b, pp, pk, hin=hin, hk=hk, p=p, pk_=pk_):
                tt(p[:, 2 + a:2 + b], hin[:, a:b], pp, ALU.mult, [f"{hk}.{ti}", pk], [f"{pk_}.{ti}"])
            panel_mm(pan, pkey, 8, xn_rhs, xn_keys, half, c2)
            pkeys = [pk_ + ".h"] + [f"{pk_}.{ti}" for ti in range(2)]
            ts(cv[:, 0:TPH], p[:, 0:TPH], w0, None, ALU.mult, None, pkeys + ["PCOL"], [ck])
            stt(cv[:, 0:TPH], p[:, 1:TPH + 1], w1, cv[:, 0:TPH], ALU.mult, ALU.add, pkeys + ["PCOL", ck], [ck])
            stt(cv[:, 0:TPH], p[:, 2:TPH + 2], w2, cv[:, 0:TPH], ALU.mult, ALU.add, pkeys + ["PCOL", ck], [ck])
            cp(CC[:, o, c, :], p[:, TPH:TPH + 2], pkeys, ["CC"], eng="pool")
            if half == 1:
                sh = SHC[:, c, :].rearrange("p (t r) -> p t r", r=2)
                zs = p[:, 2 + TPH:2 + TT]
                ts(cv[:, TPH:TT], sh[:, :, 0], w0, None, ALU.mult, None, ["SHC", "PCOL"], [ck + "s"])
                stt(cv[:, TPH:TT], sh[:, :, 1], w1, cv[:, TPH:TT], ALU.mult, ALU.add, ["SHC", "PCOL", ck + "s"], [ck + "s"])
                stt(cv[:, TPH:TT], zs, w2, cv[:, TPH:TT], ALU.mult, ALU.add, [f"{pk_}.2", "PCOL", ck + "s"], [ck + "s"])
                cp(ZC[:, c, 0:2], p[:, TPH:TPH + 2], pkeys, ["ZC"], eng="pool")
                cp(ZC[:, c, 2:18], zs, [f"{pk_}.2"], ["ZC"], eng="pool")
            pan, pkey = wnext(f"L{l}in{4 + c}")

            def c3(ti, a, b, pp, pk, cv=cv, ck=ck, c=c):
                tt(MIXO[:, c, a:b], cv[:, a:b], pp, ALU.mult, [ck, ck + "s", pk] if half == 1 else [ck, pk], [f"MIXO{c}.{ti}"])
            panel_mm(pan, pkey, 8, xn_rhs, xn_keys, half, c3)
        if half == 1:
            pp, pk = psum("a")
            for c in range(4):
                tr(pp[0:18, c * 128:(c + 1) * 128], ZC[:, c, :], ["ZC"], [pk])
            cp(OTC, pp[0:18, :], [pk], ["OTC"])
            dma(dap(DO["ccp"], o * 2 * 512, [[512, 2], [1, 512]]), OTC[0:2, :], ["OTC"], ())
            dma(dap(DO["ccs"], o * NSAMP * 2 * 512 + 512, [[2 * 512, NSAMP], [1, 512]]), OTC[2:18, :], ["OTC"], ())
        stage_mark("O2")
        for c in range(4):
            pan, pkey = wnext(f"L{l}in{12 + c}")

            def cu(ti, a, b, pp, pk, c=c):
                if "nocu" in DBG:
                    return
                cp(U32[:, c, a:b], pp, [pk], [f"U32_{c}.{ti}"], eng="act")
                cp(UBF[:, c, a:b], U32[:, c, a:b], [f"U32_{c}.{ti}"], [f"UBF{c}.{ti}"], eng="dve")
            panel_mm(pan, pkey, 8, xn_rhs, xn_keys, half, cu)
        stage_mark("O3")
        S.fence()
        ar = Bump(part2_base)
        s5(half, l, ar, U32, UBF)
        stage_mark("O4")
        GBF = s5.GBF
        SGM = [s5.SGMBUF[:, i * 512:(i + 1) * 512] for i in range(2)]
        for n_ in range(4):
            pan, pkey = wnext(f"L{l}glu{n_}")

            def cg(ti, a, b, pp, pk, n_=n_):
                sg = SGM[ti % 2]
                act(sg[:, 0:b - a], pp, AF.Sigmoid, [pk, "PCOL"], [f"XR{ti % 2}"], bias=pc(f"sgb{o}", n_), scale=1.0)
                tt(MIXO[:, 4 + n_, a:b], U32[:, n_, a:b], sg[:, 0:b - a], ALU.mult, [f"U32_{n_}.{ti}", f"XR{ti % 2}"], [f"MIXO{4 + n_}.{ti}"])
            panel_mm(pan, pkey, 4, lambda k, a, b: GBF[:, k, a:b], lambda ti: [f"GBF{c}.{ti}" for c in range(4)], half, cg)
        stage_mark("O5")
        for n_ in range(8):
            pan, pkey = wnext(f"L{l}out{n_}")
            panel_mm(pan, pkey, 8, lambda k, a, b: MIXO[:, k, a:b], lambda ti: [f"MIXO{c}.{ti}" for c in range(8)],
                     half, resid_consumer(n_))

    def s5_pregen(l, ar):
        o = l // 2
        half = 0
        KSQ = [f"SQ{c_}" for c_ in range(8)]
        sm = lambda name, n=32: ar.alloc([128, n], F32, name)
        LST = ar.alloc([32, 128], F32, "LST")
        LRE, LIM, DTT, LA = sm("LRE"), sm("LIM"), sm("DTT"), sm("LA")
        RHO, TH, TH8, CT1, ST1 = sm("RHO"), sm("TH"), sm("TH8"), sm("CT1"), sm("ST1")
        LBR, LBI = sm("LBR"), sm("LBI")
        CR, CI, DEN, TMPA, TMPB = sm("CR"), sm("CI"), sm("DEN"), sm("TMPA"), sm("TMPB")
        CA, CBm, CA2, CB2 = sm("CA"), sm("CBm"), sm("CA2"), sm("CB2")
        LBIM = sm("LBIM")
        C8s, S8s, RHO8 = sm("C8s"), sm("S8s"), sm("RHO8")
        PWA = ar.alloc([128, 32, 17], F32, "PWA")
        PWB = ar.alloc([128, 32, 17], F32, "PWB")
        BB = ar.alloc([128, 32, 16], F32, "BB")
        BBS = ar.alloc([128, 32, 16], F32, "BBS")
        arq = Bump(SQ_OFF)
        CCT = arq.alloc([128, 128], F32, "CCT")
        CCT2 = arq.alloc([128, 128], F32, "CCT2")
        CTt = arq.alloc([128, 128], F32, "CTt")
        CTS = arq.alloc([128, 128], F32, "CTS")
        B0 = dict(CLT=ar.alloc([128, 8, 128], BF16, "pCLT"), BJT=ar.alloc([128, 8, 128], BF16, "pBJT"),
                  BJTS=ar.alloc([128, 8, 128], BF16, "pBJTS"), KT=ar.alloc([128, 8, 128], BF16, "pKT"),
                  COS=ar.alloc([128, 8, 128], F32, "pCOS"), SINP=ar.alloc([128, 8, 128], F32, "pSINP"),
                  CPAD=arq.alloc([128, 8, 128], BF16, "pCPAD"))
        SL = [B0]
        assert arq.off <= SQ_OFF + 8192
        GX = ar.alloc([128, 2304], F32, "pGX")
        GXa, GXb = GX[:, 0:1152], GX[:, 1152:2304]
        CLall = GXa.rearrange("p (m x) -> p m x", m=9)
        CLtmp = GXb.rearrange("p (m x) -> p m x", m=9)
        BBJ = GXa[:, 0:1024].rearrange("p (j x) -> p j x", j=8)
        BBJ2 = GXb[:, 0:1024].rearrange("p (j x) -> p j x", j=8)
        KG = ["GXa", "GXb"]
        BRE = GX[:, 0:512].rearrange("p (g c) -> p g c", c=16)
        BIM = GX[:, 512:1024].rearrange("p (g c) -> p g c", c=16)
        BTMP = GX[:, 1024:1536].rearrange("p (g c) -> p g c", c=16)
        E1 = GX[:, 0:544]
        E2 = GX[:, 544:1088]
        E5 = GX[:, 1088:1632]
        sgn = signc[:, 0:1]
        F2 = lambda t: t.rearrange("p g t -> p (g t)")

        def L(x):
            return list(x) if isinstance(x, (list, tuple)) else [x]

        def range_reduce(dst, src, tmp, k):
            ks, kd, kt = L(k[0]), L(k[1]), L(k[2])
            S.op("dve", lambda e: e.tensor_scalar(out=tmp.bitcast(mybir.dt.int32), in0=src, scalar1=1.0 / (2 * math.pi), scalar2=None, op0=ALU.mult), ks, kt)
            S.op("dve", lambda e: e.tensor_copy(out=dst, in_=tmp.bitcast(mybir.dt.int32)), kt, kd)
            stt(dst, dst, -2 * math.pi, src, ALU.mult, ALU.add, kd + ks, kd)
            ts(tmp, dst, math.pi, -2 * math.pi, ALU.is_gt, ALU.mult, kd, kt)
            tt(dst, dst, tmp, ALU.add, kd + kt, kd)
            ts(tmp, dst, -math.pi, 2 * math.pi, ALU.is_lt, ALU.mult, kd, kt)
            tt(dst, dst, tmp, ALU.add, kd + kt, kd)

        def sincos(cos_dst, sin_dst, ang, tmp, kc, ks, ka, kt):
            kc, ks, ka, kt = L(kc), L(ks), L(ka), L(kt)
            act(sin_dst, ang, AF.Sin, ka, ks)
            act(tmp, ang, AF.Abs, ka, kt)
            act(cos_dst, tmp, AF.Sin, kt + ["cst"], kc, bias=cst[:, 2:3], scale=-1.0)

        for (dst, src, nm) in ((LRE, "s5_lam_re", "LRE"), (LIM, "s5_lam_im", "LIM")):
            dma(LST[:, 0:64], dap(DI[src], o * 2048, [[64, 32], [1, 64]]), (), ["LST"])
            dma(LST[:, 64:128], dap(DI[src], o * 2048, [[64, 32], [1, 64]]), (), ["LST"])
            pp, pk = psum("ga")
            tr(pp[:, 0:32], LST, ["LST"], [pk])
            cp(dst, pp[:, 0:32], [pk], [nm])
        yield
        dma(DTT, dap(DI["s5_log_dt"], o * 32, [[0, 128], [1, 32]]), (), ["DTT"])
        yield
        act(DTT, DTT, AF.Exp, ["DTT"], ["DTT"])
        yield
        tt(LA, LRE, DTT, ALU.mult, ["LRE", "DTT"], ["LA"])
        yield
        act(RHO, LA, AF.Exp, ["LA"], ["RHO"])
        yield
        tt(TH, LIM, DTT, ALU.mult, ["LIM", "DTT"], ["TH"])
        yield
        ts(TH8, TH, 8.0, None, ALU.mult, None, ["TH"], ["TH8"])
        yield
        range_reduce(TMPA, TH, TMPB, ["TH", "TMPA", "TMPB"])
        yield
        sincos(CT1, ST1, TMPA, TMPB, "CT1", "ST1", "TMPA", "TMPB")
        yield
        tt(LBR, RHO, CT1, ALU.mult, ["RHO", "CT1"], ["LBR"])
        yield
        tt(LBI, RHO, ST1, ALU.mult, ["RHO", "ST1"], ["LBI"])
        yield
        ts(LBIM, LBI, sgn, -1.0, ALU.mult, ALU.mult, ["LBI", "signc"], ["LBIM"])
        e3 = lambda t: t.rearrange("p (g e) -> p g e", e=17)
        thb = TH.unsqueeze(2).to_broadcast([128, 32, 17])
        lab = LA.unsqueeze(2).to_broadcast([128, 32, 17])
        etb = etab.unsqueeze(1).to_broadcast([128, 32, 17])
        yield
        tt(e3(E5), thb, etb, ALU.mult, ["TH", "etab"], KG)
        yield
        range_reduce(E1, E5, E2, [KG, KG, KG])
        PA2 = PWA.rearrange("p g e -> p (g e)")
        PB2 = PWB.rearrange("p g e -> p (g e)")
        yield
        sincos(PA2, PB2, E1, E2, "PWA", "PWB", KG, KG)
        yield
        ts(PB2, PB2, sgn, None, ALU.mult, None, ["PWB", "signc"], ["PWB"])
        yield
        cp(C8s, PWA[:, :, 16], ["PWA"], ["C8s"])
        yield
        cp(S8s, PWB[:, :, 16], ["PWB"], ["S8s"])
        yield
        tt(e3(E5), lab, etb, ALU.mult, ["LA", "etab"], KG)
        yield
        act(E5, E5, AF.Exp, KG, KG)
        yield
        cp(RHO8, e3(E5)[:, :, 16], KG, ["RHO8"])
        yield
        tt(PA2, PA2, E5, ALU.mult, ["PWA"] + KG, ["PWA"])
        yield
        tt(PB2, PB2, E5, ALU.mult, ["PWB"] + KG, ["PWB"])
        yield
        ts(TMPA, LBR, -1.0, None, ALU.add, None, ["LBR"], ["TMPA"])
        yield
        tt(DEN, LRE, LRE, ALU.mult, ["LRE"], ["DEN"])
        yield
        tt(TMPB, LIM, LIM, ALU.mult, ["LIM"], ["TMPB"])
        yield
        tt(DEN, DEN, TMPB, ALU.add, ["DEN", "TMPB"], ["DEN"])
        yield
        S.op("dve", lambda e: e.reciprocal(out=DEN, in_=DEN), ["DEN"], ["DEN"])
        yield
        tt(CR, TMPA, LRE, ALU.mult, ["TMPA", "LRE"], ["CR"])
        yield
        tt(TMPB, LBI, LIM, ALU.mult, ["LBI", "LIM"], ["TMPB"])
        yield
        tt(CR, CR, TMPB, ALU.add, ["CR", "TMPB"], ["CR"])
        yield
        tt(CR, CR, DEN, ALU.mult, ["CR", "DEN"], ["CR"])
        yield
        tt(CI, LBI, LRE, ALU.mult, ["LBI", "LRE"], ["CI"])
        yield
        tt(TMPB, TMPA, LIM, ALU.mult, ["TMPA", "LIM"], ["TMPB"])
        yield
        tt(CI, CI, TMPB, ALU.subtract, ["CI", "TMPB"], ["CI"])
        yield
        tt(CI, CI, DEN, ALU.mult, ["CI", "DEN"], ["CI"])
        yield
        cp(CA[0:64, :], CR[0:64, :], ["CR"], ["CA"])
        yield
        cp(CA[64:128, :], CI[64:128, :], ["CI"], ["CA"])
        yield
        ts(CBm[0:64, :], CI[0:64, :], -1.0, None, ALU.mult, None, ["CI"], ["CBm"])
        yield
        cp(CBm[64:128, :], CR[64:128, :], ["CR"], ["CBm"])
        yield
        cp(CA2[0:64, :], CI[0:64, :], ["CI"], ["CA2"])
        yield
        cp(CA2[64:128, :], CR[64:128, :], ["CR"], ["CA2"])
        yield
        cp(CB2[0:64, :], CR[0:64, :], ["CR"], ["CB2"])
        yield
        ts(CB2[64:128, :], CI[64:128, :], -1.0, None, ALU.mult, None, ["CI"], ["CB2"])
        yield
        for hh in range(2):
            dma(BRE[hh * 64:(hh + 1) * 64, :, :], dap(DI["s5_b_re"], o * 32768, [[16, 64], [1024, 32], [1, 16]]), (), KG)
            dma(BIM[hh * 64:(hh + 1) * 64, :, :], dap(DI["s5_b_im"], o * 32768, [[16, 64], [1024, 32], [1, 16]]), (), KG)

        def bc(t):
            return t.unsqueeze(2).to_broadcast([128, 32, 16])
        yield
        tt(BB, BRE, bc(CA), ALU.mult, [*KG, "CA"], ["BB"])
        yield
        tt(BTMP, BIM, bc(CBm), ALU.mult, [*KG, "CBm"], KG)
        yield
        tt(BB, BB, BTMP, ALU.add, ["BB"] + KG, ["BB"])
        yield
        tt(BBS, BRE, bc(CA2), ALU.mult, [*KG, "CA2"], ["BBS"])
        yield
        tt(BTMP, BIM, bc(CB2), ALU.mult, [*KG, "CB2"], KG)
        yield
        tt(BBS, BBS, BTMP, ALU.add, ["BBS"] + KG, ["BBS"])

        yield
        for i_, (nm_, t_) in enumerate((("RHO8", RHO8), ("C8s", C8s), ("S8s", S8s), ("LBR", LBR), ("LBIM", LBIM))):
            dma(dap(SCR[f"sm{o}"], i_ * 32, [[160, 128], [1, 32]]), t_, [nm_], [f"SCRsm{o}"])
        for i_, (nm_, t_) in enumerate((("RHO8", RHO8), ("C8s", C8s), ("S8s", S8s), ("LBR", LBR), ("LBIM", LBIM))):
            dma(dap(SCR[f"sm{o}"], i_ * 32, [[160, 128], [1, 32]]), t_, [nm_], [f"SCRsm{o}"])
        yield

        def gen(k, sl):
            g0 = k * 8
            B_ = SL[sl]
            CLT, CPAD, BJT, BJTS, KT, COS, SINP = (B_[n] for n in ("CLT", "CPAD", "BJT", "BJTS", "KT", "COS", "SINP"))
            kn = lambda n: f"{n}{sl}"
            cofs = o * 32768 + g0 * 1024
            dma(CCT[:, 0:64], dap(DI["s5_c_re"], cofs, [[64, 128], [1, 64]]), (), ["CCT"] + KSQ)
            dma(CCT[:, 64:128], dap(DI["s5_c_im"], cofs, [[64, 128], [1, 64]]), (), ["CCT"] + KSQ)
            dma(CCT2[:, 0:64], dap(DI["s5_c_im"], cofs, [[64, 128], [1, 64]]), (), ["CCT2"] + KSQ)
            dma(CCT2[:, 64:128], dap(DI["s5_c_re"], cofs, [[64, 128], [1, 64]]), (), ["CCT2"] + KSQ)
            yield
            ts(CCT[:, 64:128], CCT[:, 64:128], -1.0, None, ALU.mult, None, ["CCT"], ["CCT"])
            ts(CCT2[:, 0:64], CCT2[:, 0:64], -1.0, None, ALU.mult, None, ["CCT2"], ["CCT2"])
            pp, pk = psum("ga")
            tr(pp[:, 0:128], CCT, ["CCT"], [pk])
            tr(pp[:, 128:256], CCT2, ["CCT2"], [pk])
            cp(CTt, pp[:, 0:128], [pk], ["CTt"])
            cp(CTS, pp[:, 128:256], [pk], ["CTS"], eng="dve")
            yield
            pwa_c = bass.AP(PWA.tensor, PWA.offset + g0 * 17 + 8, [list(PWA.ap[0]), [1, 9], [17, 8], [0, 16]])
            pwb_c = bass.AP(PWB.tensor, PWB.offset + g0 * 17 + 8, [list(PWB.ap[0]), [1, 9], [17, 8], [0, 16]])
            ct_b = CTt.rearrange("p (g c) -> p g c", g=8).unsqueeze(1).to_broadcast([128, 9, 8, 16])
            cts_b = CTS.rearrange("p (g c) -> p g c", g=8).unsqueeze(1).to_broadcast([128, 9, 8, 16])
            cl4 = CLall.rearrange("p m (g c) -> p m g c", g=8)
            clt4 = CLtmp.rearrange("p m (g c) -> p m g c", g=8)
            tt(cl4, pwa_c, ct_b, ALU.mult, ["PWA", "CTt"], ["GXa"])
            yield
            tt(clt4, pwb_c, cts_b, ALU.mult, ["PWB", "CTS"], ["GXb"])
            yield
            tt(GXa, GXa, GXb, ALU.add, ["GXa", "GXb"], ["GXa"])
            yield
            cp(CLT, CLall[:, 1:9, :], ["GXa"], [kn("CLT")], eng="act")
            if True:
                memset(CPAD, 0.0, [kn("CPAD")] + KSQ)
                for j in range(8):
                    cp(CPAD[:, j, j * 16:(j + 1) * 16], CTt[:, j * 16:(j + 1) * 16], ["CTt"], [kn("CPAD")], eng="pool")
            yield
            pk0, pk0k = psum("ga")
            pk1, pk1k = psum("ga")
            bbv = BB[:, g0:g0 + 8, :].rearrange("p g c -> p (g c)")
            for m in range(8):
                dstp = (pk0 if m < 4 else pk1)[:, (m % 4) * 128:(m % 4 + 1) * 128]
                mm(dstp, [(bbv, CLall[:, m, :])], ["BB", "GXa"], [pk0k if m < 4 else pk1k])
            bmb = bmask.unsqueeze(1).to_broadcast([128, 4, 128])
            tt(KT[:, 0:4, :], pk0.rearrange("p (m x) -> p m x", m=4), bmb, ALU.mult, [pk0k, "bmask"], [kn("KT")])
            yield
            tt(KT[:, 4:8, :], pk1.rearrange("p (m x) -> p m x", m=4), bmb, ALU.mult, [pk1k, "bmask"], [kn("KT")])
            yield
            pwa_b = bass.AP(PWA.tensor, PWA.offset + g0 * 17, [list(PWA.ap[0]), [1, 8], [17, 8], [0, 16]])
            pwb_b = bass.AP(PWB.tensor, PWB.offset + g0 * 17, [list(PWB.ap[0]), [1, 8], [17, 8], [0, 16]])
            bb_b = BB[:, g0:g0 + 8, :].unsqueeze(1).to_broadcast([128, 8, 8, 16])
            bbs_b = BBS[:, g0:g0 + 8, :].unsqueeze(1).to_broadcast([128, 8, 8, 16])
            j4 = lambda t: t.rearrange("p j (g c) -> p j g c", g=8)
            f2 = lambda t: t.rearrange("p j x -> p (j x)")
            for (x1, x2, op_, dstT, nm) in ((bb_b, bbs_b, ALU.subtract, BJT, "BJT"), (bbs_b, bb_b, ALU.add, BJTS, "BJTS")):
                tt(j4(BBJ), pwa_b, x1, ALU.mult, ["PWA", "BB", "BBS"], ["GXa"])
                yield
                tt(j4(BBJ2), pwb_b, x2, ALU.mult, ["PWB", "BB", "BBS"], ["GXb"])
                yield
                tt(f2(BBJ), f2(BBJ), f2(BBJ2), op_, ["GXa", "GXb"], ["GXa"])
                yield
                for hh in range(2):
                    pp, pk = psum("ga")
                    for j in range(4):
                        tr(pp[:, j * 128:(j + 1) * 128], BBJ[:, hh * 4 + j, :], ["GXa"], [pk])
                    cp(dstT[:, hh * 4:hh * 4 + 4, :], pp.rearrange("p (j x) -> p j x", j=4), [pk], [kn(nm)], eng=("act" if hh == 0 else "dve"))
                    yield
            A2, T2, C2, S2 = GXa[:, 0:1024], GXb[:, 0:1024], F2(COS), F2(SINP)
            a3 = A2.rearrange("p (g t) -> p g t", g=8)
            for j in range(8):
                ts(a3[:, j, :], ttab, TH8[:, g0 + j:g0 + j + 1], None, ALU.mult, None, ["ttab", "TH8"], ["GXa"])
                if j % 2 == 1:
                    yield
            S.op("dve", lambda e: e.tensor_scalar(out=T2.bitcast(mybir.dt.int32), in0=A2, scalar1=1.0 / (2 * math.pi), scalar2=None, op0=ALU.mult), ["GXa"], ["GXb"])
            yield
            S.op("dve", lambda e: e.tensor_copy(out=C2, in_=T2.bitcast(mybir.dt.int32)), ["GXb"], [kn("COS")])
            yield
            stt(C2, C2, -2 * math.pi, A2, ALU.mult, ALU.add, [kn("COS"), "GXa"], [kn("COS")])
            yield
            ts(T2, C2, math.pi, -2 * math.pi, ALU.is_gt, ALU.mult, [kn("COS")], ["GXb"])
            yield
            tt(C2, C2, T2, ALU.add, [kn("COS"), "GXb"], [kn("COS")])
            yield
            ts(T2, C2, -math.pi, 2 * math.pi, ALU.is_lt, ALU.mult, [kn("COS")], ["GXb"])
            yield
            tt(A2, C2, T2, ALU.add, [kn("COS"), "GXb"], ["GXa"])
            yield
            act(S2, A2, AF.Sin, ["GXa"], [kn("SINP")])
            act(T2, A2, AF.Abs, ["GXa"], ["GXb"])
            act(C2, T2, AF.Sin, ["GXb", "cst"], [kn("COS")], bias=cst[:, 2:3], scale=-1.0)
            yield
            ts(S2, S2, sgn, None, ALU.mult, None, [kn("SINP"), "signc"], [kn("SINP")])
            yield
            for nm_ in ("CLT", "BJT", "BJTS", "KT", "CPAD", "COS", "SINP"):
                dma(SCR[f"{nm_}{o}{k}"].ap(), B_[nm_].rearrange("p g t -> p (g t)"), [kn(nm_)], [f"SCR{nm_}{o}{k}"])
            yield

        for k in range(4):
            for _ in gen(k, 0):
                yield

    def s5(half, l, ar, U32, UBF):
        o = l // 2
        GBF = ar.alloc([128, 4, TT], BF16, "GBF")
        s5.GBF = GBF
        if "nos5" in DBG:
            return
        sm = lambda name, n=32: ar.alloc([128, n], F32, name)
        LST = ar.alloc([32, 128], F32, "LST")
        LRE, LIM, DTT, LA = sm("LRE"), sm("LIM"), sm("DTT"), sm("LA")
        RHO, TH, TH8, CT1, ST1 = sm("RHO"), sm("TH"), sm("TH8"), sm("CT1"), sm("ST1")
        LBR, LBI = sm("LBR"), sm("LBI")
        CR, CI, DEN, TMPA, TMPB = sm("CR"), sm("CI"), sm("DEN"), sm("TMPA"), sm("TMPB")
        CA, CBm, CA2, CB2 = sm("CA"), sm("CBm"), sm("CA2"), sm("CB2")
        LBIM = sm("LBIM")
        C8s, S8s, RHO8 = sm("C8s"), sm("S8s"), sm("RHO8")
        PWA = ar.alloc([128, 32, 17], F32, "PWA")
        PWB = ar.alloc([128, 32, 17], F32, "PWB")
        BB = ar.alloc([128, 32, 16], F32, "BB")
        BBS = ar.alloc([128, 32, 16], F32, "BBS")
        arq = Bump(SQ_OFF)
        CCT = arq.alloc([128, 128], F32, "CCT")
        CCT2 = arq.alloc([128, 128], F32, "CCT2")
        CTt = arq.alloc([128, 128], F32, "CTt")
        CTS = arq.alloc([128, 128], F32, "CTS")
        arx = Bump(XN_OFF)
        SL = []
        for s_ in range(2):
            a_ = ar if s_ == 0 else arx
            d_ = dict(CLT=a_.alloc([128, 8, 128], BF16, f"CLT{s_}"), BJT=a_.alloc([128, 8, 128], BF16, f"BJT{s_}"),
                      BJTS=a_.alloc([128, 8, 128], BF16, f"BJTS{s_}"), KT=a_.alloc([128, 8, 128], BF16, f"KT{s_}"),
                      COS=a_.alloc([128, 8, 128], F32, f"COS{s_}"), SINP=a_.alloc([128, 8, 128], F32, f"SINP{s_}"),
                      CPAD=arq.alloc([128, 8, 128], BF16, f"CPAD{s_}"))
            SL.append(d_)
        assert arx.off <= XN_OFF + 16640 and arq.off <= SQ_OFF + 8192
        GX = ar.alloc([128, 2304], F32, "GX")
        XXb = ar.alloc([128, 2048], F32, "XXb")
        XRb = XXb[:, 0:1024]
        XSb = XXb[:, 1024:2048]
        UD = ar.alloc([128, 8, 128], BF16, "UD")
        UM4 = XXb[:, 0:2048].bitcast(BF16).rearrange("p (g t) -> p g t", g=4)
        HT = ar.alloc([128, 8, 128], F32, "HT")
        HS = ar.alloc([128, 8, 128], F32, "HS")
        S32 = ar.alloc([128, 8, 129], F32, "SSTATE")
        SBb = Bump(RS_OFF).alloc([128, 8, 128], BF16, "SBb")
        UMS = ar.alloc([128, 8, NSAMP], BF16, "UMS")
        INIT = ar.alloc([128, 8], F32, "INIT")
        HLS = ar.alloc([128, 8], F32, "HLS")
        TM8 = ar.alloc([128, 8], F32, "TM8")
        H0 = arq.alloc([128, 8, NSAMP], F32, "H0")
        H0S = arq.alloc([128, 8, NSAMP], F32, "H0S")
        HN = arq.alloc([128, 8, NSAMP], F32, "HN")
        HNB = ar.alloc([128, 8, NSAMP], BF16, "HNB")
        OHP = arq.alloc([32, 128], F32, "OHP")
        assert arq.off <= SQ_OFF + 8192
        s5.SGMBUF = XRb
        KXR, KXS = ["XR0", "XR1"], ["XS0", "XS1"]
        XR = XRb[:, 0:1024].rearrange("p (g t) -> p g t", g=8)
        XS_ = XSb[:, 0:1024].rearrange("p (g t) -> p g t", g=8)
        GXa, GXb = GX[:, 0:1152], GX[:, 1152:2304]
        CLall = GXa.rearrange("p (m x) -> p m x", m=9)
        CLtmp = GXb.rearrange("p (m x) -> p m x", m=9)
        BBJ = GXa[:, 0:1024].rearrange("p (j x) -> p j x", j=8)
        BBJ2 = GXb[:, 0:1024].rearrange("p (j x) -> p j x", j=8)
        YT = HT.rearrange("p g t -> p (g t)")
        BRE = HT[:, 0:4, :].rearrange("p g (a c) -> p (g a) c", c=16)
        BIM = HT[:, 4:8, :].rearrange("p g (a c) -> p (g a) c", c=16)
        BTMP = HS[:, 0:4, :].rearrange("p g (a c) -> p (g a) c", c=16)
        E1 = XRb[:, 0:544]
        E2 = XSb[:, 0:544]
        E5 = HT.rearrange("p g t -> p (g t)")[:, 0:544]
        OST = XRb[0:NSAMP, 0:1024].rearrange("t (g x) -> t g x", g=8)
        sgn = signc[:, 0:1]

        def L(x):
            return list(x) if isinstance(x, (list, tuple)) else [x]

        def range_reduce(dst, src, tmp, k):
            ks, kd, kt = L(k[0]), L(k[1]), L(k[2])
            S.op("dve", lambda e: e.tensor_scalar(out=tmp.bitcast(mybir.dt.int32), in0=src, scalar1=1.0 / (2 * math.pi), scalar2=None, op0=ALU.mult), ks, kt)
            S.op("dve", lambda e: e.tensor_copy(out=dst, in_=tmp.bitcast(mybir.dt.int32)), kt, kd)
            stt(dst, dst, -2 * math.pi, src, ALU.mult, ALU.add, kd + ks, kd)
            ts(tmp, dst, math.pi, -2 * math.pi, ALU.is_gt, ALU.mult, kd, kt)
            tt(dst, dst, tmp, ALU.add, kd + kt, kd)
            ts(tmp, dst, -math.pi, 2 * math.pi, ALU.is_lt, ALU.mult, kd, kt)
            tt(dst, dst, tmp, ALU.add, kd + kt, kd)

        def sincos(cos_dst, sin_dst, ang, tmp, kc, ks, ka, kt):
            kc, ks, ka, kt = L(kc), L(ks), L(ka), L(kt)
            act(sin_dst, ang, AF.Sin, ka, ks)
            act(tmp, ang, AF.Abs, ka, kt)
            act(cos_dst, tmp, AF.Sin, kt + ["cst"], kc, bias=cst[:, 2:3], scale=-1.0)

        if False:
            for (dst, src, nm) in ((LRE, "s5_lam_re", "LRE"), (LIM, "s5_lam_im", "LIM")):
                dma(LST[:, 0:64], dap(DI[src], o * 2048, [[64, 32], [1, 64]]), (), ["LST"])
                dma(LST[:, 64:128], dap(DI[src], o * 2048, [[64, 32], [1, 64]]), (), ["LST"])
                pp, pk = psum("a")
                tr(pp[:, 0:32], LST, ["LST"], [pk])
                cp(dst, pp[:, 0:32], [pk], [nm])
            dma(DTT, dap(DI["s5_log_dt"], o * 32, [[0, 128], [1, 32]]), (), ["DTT"])
            act(DTT, DTT, AF.Exp, ["DTT"], ["DTT"])
            tt(LA, LRE, DTT, ALU.mult, ["LRE", "DTT"], ["LA"])
            act(RHO, LA, AF.Exp, ["LA"], ["RHO"])
            tt(TH, LIM, DTT, ALU.mult, ["LIM", "DTT"], ["TH"])
            ts(TH8, TH, 8.0, None, ALU.mult, None, ["TH"], ["TH8"])
            range_reduce(TMPA, TH, TMPB, ["TH", "TMPA", "TMPB"])
            sincos(CT1, ST1, TMPA, TMPB, "CT1", "ST1", "TMPA", "TMPB")
            tt(LBR, RHO, CT1, ALU.mult, ["RHO", "CT1"], ["LBR"])
            tt(LBI, RHO, ST1, ALU.mult, ["RHO", "ST1"], ["LBI"])
            ts(LBIM, LBI, sgn, -1.0, ALU.mult, ALU.mult, ["LBI", "signc"], ["LBIM"])
            e3 = lambda t: t.rearrange("p (g e) -> p g e", e=17)
            thb = TH.unsqueeze(2).to_broadcast([128, 32, 17])
            lab = LA.unsqueeze(2).to_broadcast([128, 32, 17])
            etb = etab.unsqueeze(1).to_broadcast([128, 32, 17])
            tt(e3(E5), thb, etb, ALU.mult, ["TH", "etab"], ["HT"])
            range_reduce(E1, E5, E2, ["HT", KXR, KXS])
            PA2 = PWA.rearrange("p g e -> p (g e)")
            PB2 = PWB.rearrange("p g e -> p (g e)")
            sincos(PA2, PB2, E1, E2, "PWA", "PWB", KXR, KXS)
            ts(PB2, PB2, sgn, None, ALU.mult, None, ["PWB", "signc"], ["PWB"])
            cp(C8s, PWA[:, :, 16], ["PWA"], ["C8s"])
            cp(S8s, PWB[:, :, 16], ["PWB"], ["S8s"])
            tt(e3(E5), lab, etb, ALU.mult, ["LA", "etab"], ["HT"])
            act(E5, E5, AF.Exp, ["HT"], ["HT"])
            cp(RHO8, e3(E5)[:, :, 16], ["HT"], ["RHO8"])
            tt(PA2, PA2, E5, ALU.mult, ["PWA", "HT"], ["PWA"])
            tt(PB2, PB2, E5, ALU.mult, ["PWB", "HT"], ["PWB"])
            ts(TMPA, LBR, -1.0, None, ALU.add, None, ["LBR"], ["TMPA"])
            tt(DEN, LRE, LRE, ALU.mult, ["LRE"], ["DEN"])
            tt(TMPB, LIM, LIM, ALU.mult, ["LIM"], ["TMPB"])
            tt(DEN, DEN, TMPB, ALU.add, ["DEN", "TMPB"], ["DEN"])
            S.op("dve", lambda e: e.reciprocal(out=DEN, in_=DEN), ["DEN"], ["DEN"])
            tt(CR, TMPA, LRE, ALU.mult, ["TMPA", "LRE"], ["CR"])
            tt(TMPB, LBI, LIM, ALU.mult, ["LBI", "LIM"], ["TMPB"])
            tt(CR, CR, TMPB, ALU.add, ["CR", "TMPB"], ["CR"])
            tt(CR, CR, DEN, ALU.mult, ["CR", "DEN"], ["CR"])
            tt(CI, LBI, LRE, ALU.mult, ["LBI", "LRE"], ["CI"])
            tt(TMPB, TMPA, LIM, ALU.mult, ["TMPA", "LIM"], ["TMPB"])
            tt(CI, CI, TMPB, ALU.subtract, ["CI", "TMPB"], ["CI"])
            tt(CI, CI, DEN, ALU.mult, ["CI", "DEN"], ["CI"])
            cp(CA[0:64, :], CR[0:64, :], ["CR"], ["CA"])
            cp(CA[64:128, :], CI[64:128, :], ["CI"], ["CA"])
            ts(CBm[0:64, :], CI[0:64, :], -1.0, None, ALU.mult, None, ["CI"], ["CBm"])
            cp(CBm[64:128, :], CR[64:128, :], ["CR"], ["CBm"])
            cp(CA2[0:64, :], CI[0:64, :], ["CI"], ["CA2"])
            cp(CA2[64:128, :], CR[64:128, :], ["CR"], ["CA2"])
            cp(CB2[0:64, :], CR[0:64, :], ["CR"], ["CB2"])
            ts(CB2[64:128, :], CI[64:128, :], -1.0, None, ALU.mult, None, ["CI"], ["CB2"])
            for hh in range(2):
                dma(BRE[hh * 64:(hh + 1) * 64, :, :], dap(DI["s5_b_re"], o * 32768, [[16, 64], [1024, 32], [1, 16]]), (), ["HT"])
                dma(BIM[hh * 64:(hh + 1) * 64, :, :], dap(DI["s5_b_im"], o * 32768, [[16, 64], [1024, 32], [1, 16]]), (), ["HT"])

            def bc(t):
                return t.unsqueeze(2).to_broadcast([128, 32, 16])
            tt(BB, BRE, bc(CA), ALU.mult, ["HT", "CA"], ["BB"])
            tt(BTMP, BIM, bc(CBm), ALU.mult, ["HT", "CBm"], ["HS"])
            tt(BB, BB, BTMP, ALU.add, ["BB", "HS"], ["BB"])
            tt(BBS, BRE, bc(CA2), ALU.mult, ["HT", "CA2"], ["BBS"])
            tt(BTMP, BIM, bc(CB2), ALU.mult, ["HT", "CB2"], ["HS"])
            tt(BBS, BBS, BTMP, ALU.add, ["BBS", "HS"], ["BBS"])

            for i_, (nm_, t_) in enumerate((("RHO8", RHO8), ("C8s", C8s), ("S8s", S8s), ("LBR", LBR), ("LBIM", LBIM))):
                dma(dap(SCR[f"sm{o}"], i_ * 32, [[160, 128], [1, 32]]), t_, [nm_], [f"SCRsm{o}"])
        else:
            for i_, (nm_, t_) in enumerate((("RHO8", RHO8), ("C8s", C8s), ("S8s", S8s), ("LBR", LBR), ("LBIM", LBIM))):
                dma(t_, dap(SCR[f"sm{o}"], i_ * 32, [[160, 128], [1, 32]]), [f"SCRsm{o}"], [nm_])
        stage_mark("S5a")
        F2 = lambda t: t.rearrange("p g t -> p (g t)")

        def gen(k, sl):
            g0 = k * 8
            B_ = SL[sl]
            CLT, CPAD, BJT, BJTS, KT, COS, SINP = (B_[n] for n in ("CLT", "CPAD", "BJT", "BJTS", "KT", "COS", "SINP"))
            kn = lambda n: f"{n}{sl}"
            cofs = o * 32768 + g0 * 1024
            dma(CCT[:, 0:64], dap(DI["s5_c_re"], cofs, [[64, 128], [1, 64]]), (), ["CCT"])
            dma(CCT[:, 64:128], dap(DI["s5_c_im"], cofs, [[64, 128], [1, 64]]), (), ["CCT"])
            dma(CCT2[:, 0:64], dap(DI["s5_c_im"], cofs, [[64, 128], [1, 64]]), (), ["CCT2"])
            dma(CCT2[:, 64:128], dap(DI["s5_c_re"], cofs, [[64, 128], [1, 64]]), (), ["CCT2"])
            yield
            ts(CCT[:, 64:128], CCT[:, 64:128], -1.0, None, ALU.mult, None, ["CCT"], ["CCT"])
            ts(CCT2[:, 0:64], CCT2[:, 0:64], -1.0, None, ALU.mult, None, ["CCT2"], ["CCT2"])
            pp, pk = psum("ga")
            tr(pp[:, 0:128], CCT, ["CCT"], [pk])
            tr(pp[:, 128:256], CCT2, ["CCT2"], [pk])
            cp(CTt, pp[:, 0:128], [pk], ["CTt"])
            cp(CTS, pp[:, 128:256], [pk], ["CTS"], eng="dve")
            yield
            pwa_c = bass.AP(PWA.tensor, PWA.offset + g0 * 17 + 8, [list(PWA.ap[0]), [1, 9], [17, 8], [0, 16]])
            pwb_c = bass.AP(PWB.tensor, PWB.offset + g0 * 17 + 8, [list(PWB.ap[0]), [1, 9], [17, 8], [0, 16]])
            ct_b = CTt.rearrange("p (g c) -> p g c", g=8).unsqueeze(1).to_broadcast([128, 9, 8, 16])
            cts_b = CTS.rearrange("p (g c) -> p g c", g=8).unsqueeze(1).to_broadcast([128, 9, 8, 16])
            cl4 = CLall.rearrange("p m (g c) -> p m g c", g=8)
            clt4 = CLtmp.rearrange("p m (g c) -> p m g c", g=8)
            tt(cl4, pwa_c, ct_b, ALU.mult, ["PWA", "CTt"], ["GXa"])
            yield
            tt(clt4, pwb_c, cts_b, ALU.mult, ["PWB", "CTS"], ["GXb"])
            yield
            tt(GXa, GXa, GXb, ALU.add, ["GXa", "GXb"], ["GXa"])
            yield
            cp(CLT, CLall[:, 1:9, :], ["GXa"], [kn("CLT")], eng="act")
            if True:
                memset(CPAD, 0.0, [kn("CPAD")])
                for j in range(8):
                    cp(CPAD[:, j, j * 16:(j + 1) * 16], CTt[:, j * 16:(j + 1) * 16], ["CTt"], [kn("CPAD")], eng="pool")
            yield
            pk0, pk0k = psum("ga")
            pk1, pk1k = psum("ga")
            bbv = BB[:, g0:g0 + 8, :].rearrange("p g c -> p (g c)")
            for m in range(8):
                dstp = (pk0 if m < 4 else pk1)[:, (m % 4) * 128:(m % 4 + 1) * 128]
                mm(dstp, [(bbv, CLall[:, m, :])], ["BB", "GXa"], [pk0k if m < 4 else pk1k])
            bmb = bmask.unsqueeze(1).to_broadcast([128, 4, 128])
            tt(KT[:, 0:4, :], pk0.rearrange("p (m x) -> p m x", m=4), bmb, ALU.mult, [pk0k, "bmask"], [kn("KT")])
            yield
            tt(KT[:, 4:8, :], pk1.rearrange("p (m x) -> p m x", m=4), bmb, ALU.mult, [pk1k, "bmask"], [kn("KT")])
            yield
            pwa_b = bass.AP(PWA.tensor, PWA.offset + g0 * 17, [list(PWA.ap[0]), [1, 8], [17, 8], [0, 16]])
            pwb_b = bass.AP(PWB.tensor, PWB.offset + g0 * 17, [list(PWB.ap[0]), [1, 8], [17, 8], [0, 16]])
            bb_b = BB[:, g0:g0 + 8, :].unsqueeze(1).to_broadcast([128, 8, 8, 16])
            bbs_b = BBS[:, g0:g0 + 8, :].unsqueeze(1).to_broadcast([128, 8, 8, 16])
            j4 = lambda t: t.rearrange("p j (g c) -> p j g c", g=8)
            f2 = lambda t: t.rearrange("p j x -> p (j x)")
            for (x1, x2, op_, dstT, nm) in ((bb_b, bbs_b, ALU.subtract, BJT, "BJT"), (bbs_b, bb_b, ALU.add, BJTS, "BJTS")):
                tt(j4(BBJ), pwa_b, x1, ALU.mult, ["PWA", "BB", "BBS"], ["GXa"])
                yield
                tt(j4(BBJ2), pwb_b, x2, ALU.mult, ["PWB", "BB", "BBS"], ["GXb"])
                yield
                tt(f2(BBJ), f2(BBJ), f2(BBJ2), op_, ["GXa", "GXb"], ["GXa"])
                yield
                for hh in range(2):
                    pp, pk = psum("ga")
                    for j in range(4):
                        tr(pp[:, j * 128:(j + 1) * 128], BBJ[:, hh * 4 + j, :], ["GXa"], [pk])
                    cp(dstT[:, hh * 4:hh * 4 + 4, :], pp.rearrange("p (j x) -> p j x", j=4), [pk], [kn(nm)], eng=("act" if hh == 0 else "dve"))
                    yield
            A2, T2, C2, S2 = GXa[:, 0:1024], GXb[:, 0:1024], F2(COS), F2(SINP)
            a3 = A2.rearrange("p (g t) -> p g t", g=8)
            for j in range(8):
                ts(a3[:, j, :], ttab, TH8[:, g0 + j:g0 + j + 1], None, ALU.mult, None, ["ttab", "TH8"], ["GXa"])
                if j % 2 == 1:
                    yield
            S.op("dve", lambda e: e.tensor_scalar(out=T2.bitcast(mybir.dt.int32), in0=A2, scalar1=1.0 / (2 * math.pi), scalar2=None, op0=ALU.mult), ["GXa"], ["GXb"])
            yield
            S.op("dve", lambda e: e.tensor_copy(out=C2, in_=T2.bitcast(mybir.dt.int32)), ["GXb"], [kn("COS")])
            yield
            stt(C2, C2, -2 * math.pi, A2, ALU.mult, ALU.add, [kn("COS"), "GXa"], [kn("COS")])
            yield
            ts(T2, C2, math.pi, -2 * math.pi, ALU.is_gt, ALU.mult, [kn("COS")], ["GXb"])
            yield
            tt(C2, C2, T2, ALU.add, [kn("COS"), "GXb"], [kn("COS")])
            yield
            ts(T2, C2, -math.pi, 2 * math.pi, ALU.is_lt, ALU.mult, [kn("COS")], ["GXb"])
            yield
            tt(A2, C2, T2, ALU.add, [kn("COS"), "GXb"], ["GXa"])
            yield
            act(S2, A2, AF.Sin, ["GXa"], [kn("SINP")])
            act(T2, A2, AF.Abs, ["GXa"], ["GXb"])
            act(C2, T2, AF.Sin, ["GXb", "cst"], [kn("COS")], bias=cst[:, 2:3], scale=-1.0)
            yield
            ts(S2, S2, sgn, None, ALU.mult, None, [kn("SINP"), "signc"], [kn("SINP")])
            yield
            for nm_ in ("CLT", "BJT", "BJTS", "KT", "CPAD", "COS", "SINP"):
                dma(SCR[f"{nm_}{o}{k}"].ap(), B_[nm_].rearrange("p g t -> p (g t)"), [kn(nm_)], [f"SCR{nm_}{o}{k}"])
            yield

        def gen_load(k, sl):
            B_ = SL[sl]
            kn = lambda n: f"{n}{sl}"
            for nm_ in ("BJT", "BJTS", "COS", "SINP", "KT", "CLT", "CPAD"):
                dma(B_[nm_].rearrange("p g t -> p (g t)"), SCR[f"{nm_}{o}{k}"].ap(), [f"SCR{nm_}{o}{k}"], [kn(nm_)])
                yield

        def run(k, sl):
            g0 = k * 8
            B_ = SL[sl]
            CLT, CPAD, BJT, BJTS, KT, COS, SINP = (B_[n] for n in ("CLT", "CPAD", "BJT", "BJTS", "KT", "COS", "SINP"))
            kn = lambda n: f"{n}{sl}"
            hl = HL[:, o, g0:g0 + 8]
            cp(SBb[:, :, 0], hl, ["HL"], ["SBb"])
            cp(HLS[0:64, :], hl[64:128, :], ["HL"], ["HLS"], eng="pool")
            cp(HLS[64:128, :], hl[0:64, :], ["HL"], ["HLS"], eng="pool")
            tt(INIT, hl, C8s[:, g0:g0 + 8], ALU.mult, ["HL", "C8s"], ["INIT"])
            yield
            tt(TM8, HLS, S8s[:, g0:g0 + 8], ALU.mult, ["HLS", "S8s"], ["TM8"])
            yield
            tt(INIT, INIT, TM8, ALU.subtract, ["INIT", "TM8"], ["INIT"])
            yield
            pg = [psum("m") for _ in range(4)]
            ukeys = [f"UBF{k}.{ti}" for ti in range(3 if half == 1 else 2)]
            um = UM4.rearrange("p g (r b) -> p g r b", r=8)
            cp(UD, UBF[:, k, 0:TPH].rearrange("p (b r) -> p r b", r=8), ukeys, ["UD"], eng="act")
            yield
            udf = UD.rearrange("p r b -> p (r b)")
            for hq in range(2):
                for j4_ in range(4):
                    ts(UM4[:, j4_, :], udf, rowmask[:, hq * 4 + j4_:hq * 4 + j4_ + 1], None, ALU.mult, None,
                       ["UD", "rowmask"], KXR + KXS)
                    yield
                for sw in range(2):
                    W_ = BJTS if sw else BJT
                    pgp, pgk = pg[sw * 2 + hq]
                    mm(pgp.rearrange("p (g b) -> p g b", g=4), [(W_[:, jj, :], um[:, :, jj, :]) for jj in range(8)],
                       KXR + KXS + [kn("BJTS") if sw else kn("BJT")], [pgk])
                yield
            for hh in range(2):
                (p1, p1k), (p2_, p2k_) = pg[hh], pg[2 + hh]
                sl_ = slice(hh * 4, hh * 4 + 4)
                tt(F2(XR[:, sl_, :]), p1, F2(COS[:, sl_, :]), ALU.mult, [p1k, kn("COS")], [f"XR{hh}"])
                yield
                tt(F2(XS_[:, sl_, :]), p2_, F2(SINP[:, sl_, :]), ALU.mult, [p2k_, kn("SINP")], [f"XS{hh}"])
                yield
                tt(F2(XR[:, sl_, :]), F2(XR[:, sl_, :]), F2(XS_[:, sl_, :]), ALU.add, [f"XR{hh}", f"XS{hh}"], [f"XR{hh}"])
                yield
            for j in range(8):
                S.op("dve", lambda e, j=j, g0=g0: e.tensor_tensor_scan(
                    out=HT[:, j, :], data0=RHO8[:, g0 + j:g0 + j + 1].to_broadcast([128, 128]), data1=XR[:, j, :],
                    initial=INIT[:, j:j + 1], op0=ALU.mult, op1=ALU.add), [f"XR{j // 4}", "RHO8", "INIT"], ["HT"])
                if j % 2 == 1:
                    yield
            cp(HS[0:64, :, :], HT[64:128, :, :], ["HT"], ["HS"], eng="act")
            cp(HS[64:128, :, :], HT[0:64, :, :], ["HT"], ["HS"], eng="act")
            tt(F2(XR), F2(HT), F2(COS), ALU.mult, ["HT", kn("COS")], KXR)
            yield
            tt(F2(XS_), F2(HS), F2(SINP), ALU.mult, ["HS", kn("SINP")], KXS)
            yield
            tt(SBb[:, :, 1:128], XR[:, :, 0:127], XS_[:, :, 0:127], ALU.subtract, KXR + KXS, ["SBb"])
            tt(HL[:, o, g0:g0 + 8], XR[:, :, 127], XS_[:, :, 127], ALU.subtract, KXR + KXS, ["HL"])
            yield
            u8 = UD
            py = [psum("ra"), psum("ra")]
            pyA, pyB = py[0][0], py[1][0]

            def yfn(e, u8=u8, pyA=pyA, pyB=pyB, KT=KT, CLT=CLT):
                ins = None
                for jj in range(8):
                    if jj <= 3:
                        n_ = 4 - jj
                        ins = e.matmul(pyA[:, jj * 128:512], lhsT=u8[:, jj, :], rhs=KT[:, 0:n_, :].rearrange("p m x -> p (m x)"),
                                       start=(jj == 0), stop=False, skip_group_check=True)
                    t0_ = max(jj, 4)
                    n_ = 8 - t0_
                    m0_ = t0_ - jj
                    ins = e.matmul(pyB[:, (t0_ - 4) * 128:512], lhsT=u8[:, jj, :], rhs=KT[:, m0_:m0_ + n_, :].rearrange("p m x -> p (m x)"),
                                   start=(jj == 0), stop=False, skip_group_check=True)
                for j in range(8):
                    for t_ in range(8):
                        pyp = pyA if t_ < 4 else pyB
                        c0_ = (t_ % 4) * 128 + j * 16
                        ins = e.matmul(pyp[:, c0_:c0_ + 16], lhsT=SBb[:, j, :], rhs=CLT[:, t_, j * 16:(j + 1) * 16],
                                       start=False, stop=(j == 7), skip_group_check=True)
                return ins
            S.op("pe", yfn, ["UD", kn("KT"), "SBb", kn("CLT")], [py[0][1], py[1][1]])
            cp(YT[:, 0:512], py[0][0], [py[0][1]], ["HT"], eng="act")
            cp(YT[:, 512:1024], py[1][0], [py[1][1]], ["HT"], eng="act")
            yield
            pt = [psum("ra"), psum("ra")]
            for t_ in range(8):
                ptp, ptk = pt[t_ // 4]
                tr(ptp[:, (t_ % 4) * 128:(t_ % 4 + 1) * 128], YT[:, t_ * 128:(t_ + 1) * 128], ["HT"], [ptk])
            uview = U32[:, k, 0:TPH].rearrange("p (b r) -> p r b", r=8)
            for hh in range(2):
                ptp, ptk = pt[hh]
                stt(uview[:, hh * 4:hh * 4 + 4, :], uview[:, hh * 4:hh * 4 + 4, :], pc(f"sd{o}", k), ptp.rearrange("p (r b) -> p r b", r=4),
                    ALU.mult, ALU.add, [f"U32_{k}.0", f"U32_{k}.1", "PCOL", ptk], [f"U32_{k}.0", f"U32_{k}.1"])
                yield
            if half == 1 and "nosamp" not in DBG:
                SST = F2(COS)[0:NSAMP, :].rearrange("t (g r p) -> t g r p", g=8, r=2)
                SSW = F2(SINP)[0:NSAMP, :].rearrange("t (g r p) -> t g r p", g=8, r=2)
                sso = o * NSAMP * 2048 + g0 * 64
                spat = [[2048, NSAMP], [64, 8], [1, 64]]
                dma(SST[:, :, 0, :], dap(DI["st_re"], sso, spat), (), [kn("COS")])
                dma(SST[:, :, 1, :], dap(DI["st_im"], sso, spat), (), [kn("COS")])
                dma(SSW[:, :, 0, :], dap(DI["st_im"], sso, spat), (), [kn("SINP")])
                dma(SSW[:, :, 1, :], dap(DI["st_re"], sso, spat), (), [kn("SINP")])
                yield
                for (src, dst, nm, sk) in ((SST, H0, "H0", kn("COS")), (SSW, H0S, "H0S", kn("SINP"))):
                    pp, pk = psum("ra")
                    for j in range(8):
                        tr(pp[:, j * NSAMP:(j + 1) * NSAMP], src[:, j, :, :].rearrange("t r p -> t (r p)"), [sk], [pk])
                    cp(dst.rearrange("p g t -> p (g t)"), pp[:, 0:128], [pk], [nm])
                    yield
                bcg = lambda t: t[:, g0:g0 + 8].unsqueeze(2).to_broadcast([128, 8, NSAMP])
                tt(HN, H0, bcg(LBR), ALU.mult, ["H0", "LBR"], ["HN"])
                tt(H0S, H0S, bcg(LBIM), ALU.mult, ["H0S", "LBIM"], ["H0S"])
                tt(HN, HN, H0S, ALU.add, ["HN", "H0S"], ["HN"])
                yield
                pbs, pbsk = psum("ra")
                for j in range(8):
                    ts(UMS[:, j, :], UBF[:, k, TPH:TT], rowmask[:, j:j + 1], None, ALU.mult, None, [f"UBF{k}.2", "rowmask"], ["UMS"])
                for j in range(8):
                    mm(pbs[:, j * NSAMP:(j + 1) * NSAMP], [(BJT[:, 7, :], UMS[:, j, :])], ["UMS", kn("BJT")], [pbsk])
                tt(HN.rearrange("p g t -> p (g t)"), HN.rearrange("p g t -> p (g t)"), pbs[:, 0:128], ALU.add, ["HN", pbsk], ["HN"])
                cp(HNB, HN, ["HN"], ["HNB"], eng="act")
                yield
                pys, pysk = psum("ra")
                mm(pys[:, 0:NSAMP], [(CPAD[:, j, :], HNB[:, j, :]) for j in range(8)], [kn("CPAD"), "HNB"], [pysk])
                stt(U32[:, k, TPH:TT], U32[:, k, TPH:TT], pc(f"sd{o}", k), pys[:, 0:NSAMP], ALU.mult, ALU.add,
                    [f"U32_{k}.2", "PCOL", pysk], [f"U32_{k}.2"])
                po, pok = psum("ra")
                po2, po2k = psum("ra")
                for j in range(8):
                    d_ = (po if j < 4 else po2)[0:NSAMP, (j % 4) * 128:(j % 4 + 1) * 128]
                    tr(d_, HN[:, j, :], ["HN"], [pok if j < 4 else po2k])
                cp(OST[:, 0:4, :].rearrange("t g x -> t (g x)"), po[0:NSAMP, :], [pok], KXR)
                cp(OST[:, 4:8, :].rearrange("t g x -> t (g x)"), po2[0:NSAMP, :], [po2k], KXR)
                dma(dap(DO["srs"], o * NSAMP * 2048 + g0 * 64, [[2048, NSAMP], [64, 8], [1, 64]]), OST[:, :, 0:64], KXR, ())
                dma(dap(DO["sis"], o * NSAMP * 2048 + g0 * 64, [[2048, NSAMP], [64, 8], [1, 64]]), OST[:, :, 64:128], KXR, ())
                yield
            for ti, (a, b) in enumerate(tiles_of(half)):
                act(U32[:, k, a:b], U32[:, k, a:b], AF.Gelu_apprx_tanh, [f"U32_{k}.{ti}"], [f"U32_{k}.{ti}"])
                cp(GBF[:, k, a:b], U32[:, k, a:b], [f"U32_{k}.{ti}"], [f"GBF{k}.{ti}"], eng="pool")
            yield

        gen_ = gen_load
        for _ in gen_(0, 0):
            pass
        for k in range(4):
            streams = [run(k, k % 2)]
            if k < 3 and "noilv" not in DBG:
                streams.append(gen_(k + 1, (k + 1) % 2))
            while streams:
                for st_ in list(streams):
                    try:
                        next(st_)
                    except StopIteration:
                        streams.remove(st_)
            if k < 3 and "noilv" in DBG:
                for _ in gen_(k + 1, (k + 1) % 2):
                    pass
        if half == 1:
            pp, pk = psum("a")
            tr(pp[0:32, 0:128], HL[:, o, :], ["HL"], [pk])
            cp(OHP, pp[0:32, 0:128], [pk], ["OHP"])
            dma(dap(DO["srp"], o * 2048, [[64, 32], [1, 64]]), OHP[:, 0:64], ["OHP"], ())
            dma(dap(DO["sip"], o * 2048, [[64, 32], [1, 64]]), OHP[:, 64:128], ["OHP"], ())

    try:
      for half in range(2):
        S.fence()
        load_x(half)
        for l in LAYERS:
            if l % 2 == 0 or "evenonly" in DBG:
                mixer_even(half, l)
            else:
                mixer_odd(half, l)
            ffn(half, l)
        S.fence()
        rmsnorm(half, "nfin", final=True)
        store_y(half)
    except StopBuild:
        pass
    S.finish()
    S.emit()
    return nc


_CONSTS = None


def _consts():
    global _CONSTS
    if _CONSTS is None:
        i = np.arange(128)
        _CONSTS = dict(
            cst_ident=np.eye(128, dtype=np.float32),
            cst_masku=(i[None, :] >= i[:, None]).astype(np.float32),
            cst_ttab=np.broadcast_to(i[None, :].astype(np.float32), (128, 128)).copy(),
            cst_rowmask=(i[:, None] // 16 == np.arange(8)[None, :]).astype(np.float32),
            cst_etab=np.broadcast_to(np.array([7, 6, 5, 4, 3, 2, 1, 0, 0, 1, 2, 3, 4, 5, 6, 7, 8], np.float32)[None, :], (128, 17)).copy(),
            cst_bmask=(i[:, None] // 16 == i[None, :] // 16).astype(np.float32),
            cst_sign=np.stack([np.where(i < 64, 1.0, -1.0), np.where(i < 64, -1.0, 1.0)], axis=1).astype(np.float32),
        )
    return _CONSTS


_NC_CACHE = {}


def kernel(**inputs):
    f = lambda k: np.ascontiguousarray(np.asarray(inputs[k], dtype=np.float32))
    if "nc" not in _NC_CACHE:
        _NC_CACHE["nc"] = build()
    nc = _NC_CACHE["nc"]
    shared = {k: f(k) for k in IN_SHAPES if not k.startswith("cst_") and k not in
              ("xp", "xs", "st_cb", "st_cc", "st_re", "st_im", "st_ff")}
    shared.update(_consts())
    xp, xs = f("x_prompt"), f("x_sample")
    scb, scc, sre, sim, sff = f("state_conv_b"), f("state_conv_c"), f("state_ssm_re"), f("state_ssm_im"), f("state_ffn_conv")
    in_maps = []
    ncores = int(os.environ.get("KCORES", N_CORES))
    for c in range(ncores):
        sl = slice(c * NSAMP, (c + 1) * NSAMP)
        m = dict(shared)
        m["xp"] = xp[c]
        m["xs"] = np.ascontiguousarray(xs[sl, 0, :])
        m["st_cb"] = np.ascontiguousarray(scb[:, sl])
        m["st_cc"] = np.ascontiguousarray(scc[:, sl])
        m["st_re"] = np.ascontiguousarray(sre[:, sl])
        m["st_im"] = np.ascontiguousarray(sim[:, sl])
        m["st_ff"] = np.ascontiguousarray(sff[:, sl])
        in_maps.append(m)
    res = run_bass_kernel_spmd(nc, in_maps, core_ids=list(range(ncores)))
    R = list(res.results)
    while len(R) < N_CORES:
        R.append({k: np.zeros_like(v) for k, v in R[0].items()})
    cat = lambda k, ax: np.concatenate([np.asarray(r[k], dtype=np.float32) for r in R], axis=ax)
    stk = lambda k, ax: np.stack([np.asarray(r[k], dtype=np.float32) for r in R], axis=ax)
    y_prompt = stk("yp", 0)
    y_sample = cat("ys", 0)[:, None, :]
    v_rows = cat("vrows", 1)[:, :, None, :]
    return (y_prompt, y_sample, v_rows,
            stk("cbp", 1), cat("cbs", 1), stk("ccp", 1), cat("ccs", 1),
            stk("srp", 1), cat("srs", 1), stk("sip", 1), cat("sis", 1),
            stk("ffp", 1), cat("ffs", 1))
```

```python
import math
import os
import numpy as np
import concourse.bass as bass
import concourse.mybir as mybir
from concourse.bass_utils import run_bass_kernel_spmd

F32 = mybir.dt.float32
BF16 = mybir.dt.bfloat16
AF = mybir.ActivationFunctionType
ALU = mybir.AluOpType
AX = mybir.AxisListType

D_MODEL = 1024
SEQ = 2048
DEPTH = 4
D_FF = 2816
NFC = D_FF // 128
EPS = 1e-6
NSAMP = 16
TPH = 1024
TT = TPH + NSAMP
N_CORES = 8
SBUF_BASE = 16512
SBUF_LIMIT = 229376

ENGS = ("pe", "act", "dve", "pool", "sp")
SEM_ROLL = 30000


class Sched:
    def __init__(self, nc, n_dma_sems=20):
        self.nc = nc
        self.ops = {e: [] for e in ENGS}
        self.cur_sem = {}
        self.cnt = {}
        self.nsem = 0
        for e in ENGS:
            self._new_sem(e)
        self.dq = {}
        for q in ("sp", "pool", "act"):
            self.dq[q] = dict(sems=[self._alloc(f"dma_{q}{i}") for i in range(n_dma_sems)],
                              val=[0] * n_dma_sems, rr=0)
        self.seen = {e: {} for e in ENGS}
        self.writer = {}
        self.readers = {}

    def _alloc(self, name):
        self.nsem += 1
        return self.nc.alloc_semaphore(name)

    def _new_sem(self, e):
        self.cur_sem[e] = self._alloc(f"s_{e}_{self.nsem}")
        self.cnt[e] = 0

    def _deps(self, eng, reads, writes):
        deps = {}

        def add(ev, skip_same):
            if ev is None:
                return
            sem, val, oe = ev
            if oe == eng and skip_same:
                return
            k = id(sem)
            if self.seen[eng].get(k, 0) >= val:
                return
            if k not in deps or deps[k][1] < val:
                deps[k] = (sem, val)

        for k in reads:
            add(self.writer.get(k), eng == "pe")
        relaxed = eng in ("pe", "act", "dve")
        for k in writes:
            add(self.writer.get(k), relaxed)
            for ev in self.readers.get(k, {}).values():
                add(ev, relaxed)
        out = list(deps.values())
        for sem, val in out:
            self.seen[eng][id(sem)] = val
        return out

    def _commit(self, ev, reads, writes):
        for k in writes:
            self.writer[k] = ev
            self.readers[k] = {}
        for k in reads:
            self.readers.setdefault(k, {})[(ev[2], id(ev[0]))] = ev

    def op(self, eng, fn, reads=(), writes=()):
        waits = self._deps(eng, reads, writes)
        if self.cnt[eng] >= SEM_ROLL:
            self._new_sem(eng)
        self.cnt[eng] += 1
        sem = self.cur_sem[eng]
        ev = (sem, self.cnt[eng], eng)
        self.ops[eng].append((waits, fn, (sem, 1)))
        self._commit(ev, reads, writes)
        return ev

    def dma(self, q, fn, reads=(), writes=()):
        d = self.dq[q]
        i = d["rr"]
        d["rr"] = (i + 1) % len(d["sems"])
        sem = d["sems"][i]
        waits = self._deps(q, reads, writes)
        pv = d["val"][i]
        if pv > 0 and self.seen[q].get(id(sem), 0) < pv:
            waits.append((sem, pv))
            self.seen[q][id(sem)] = pv
        d["val"][i] += 16
        ev = (sem, d["val"][i], "dma")
        self.ops[q].append((waits, fn, (sem, 16)))
        self._commit(ev, reads, writes)
        return ev

    def fence(self, engs=("pe", "act", "dve", "pool", "sp")):
        evs = [(self.cur_sem[e], self.cnt[e]) for e in engs if self.cnt[e] > 0]
        d = self.dq["sp"]
        evs += [(sem, d["val"][i]) for i, sem in enumerate(d["sems"]) if d["val"][i] > 0]
        for e in engs:
            waits = []
            for sem, val in evs:
                if sem is self.cur_sem[e]:
                    continue
                if self.seen[e].get(id(sem), 0) < val:
                    waits.append((sem, val))
                    self.seen[e][id(sem)] = val
            if waits:
                self.ops[e].append((waits, None, None))

    def finish(self):
        waits = []
        for q, d in self.dq.items():
            for i, sem in enumerate(d["sems"]):
                if d["val"][i] > 0:
                    waits.append((sem, d["val"][i]))
        for e in ("pe", "act", "dve", "pool"):
            if self.cnt[e] > 0:
                waits.append((self.cur_sem[e], self.cnt[e]))
        self.ops["sp"].append((waits, None, None))

    def emit(self):
        nc = self.nc
        engmap = {"pe": "tensor", "act": "scalar", "dve": "vector", "pool": "gpsimd", "sp": "sync"}
        with nc.Block() as block:
            for e in ENGS:
                ops = self.ops[e]

                def body(engine, ops=ops):
                    for waits, fn, inc in ops:
                        for sem, val in waits:
                            engine.wait_ge(sem, val)
                        if fn is not None:
                            ins = fn(engine)
                            if inc is not None:
                                ins.then_inc(inc[0], inc[1])

                getattr(block, engmap[e])(body)


IN_SHAPES = dict(
    xp=(SEQ, D_MODEL), xs=(NSAMP, D_MODEL),
    st_cb=(2, NSAMP, 30, 512), st_cc=(2, NSAMP, 2, 512),
    st_re=(2, NSAMP, 32, 64), st_im=(2, NSAMP, 32, 64), st_ff=(4, NSAMP, 2, D_FF),
    norm_mix=(4, 1024), norm_ffn=(4, 1024), norm_final=(1024,),
    w_mix_in=(4, 1024, 2048), w_mix_out=(4, 1024, 1024),
    a_ln_g=(2, 4, 128), a_ln_b=(2, 4, 128), a_ws=(2, 4, 128, 128), a_bs=(2, 4, 128),
    b_conv_w=(2, 31, 512), b_conv_b=(2, 512), b_ln_g=(2, 512), b_ln_b=(2, 512),
    c_conv_w=(2, 3, 512), s5_lam_re=(2, 32, 64), s5_lam_im=(2, 32, 64), s5_log_dt=(2, 32),
    s5_b_re=(2, 32, 64, 16), s5_b_im=(2, 32, 64, 16), s5_c_re=(2, 32, 16, 64), s5_c_im=(2, 32, 16, 64),
    s5_d=(2, 32, 16), s5_glu_w=(2, 512, 512), s5_glu_b=(2, 512),
    ffn_w_in=(4, 1024, 2 * D_FF), ffn_conv_w=(4, 3, D_FF), ffn_w_down=(4, D_FF, 1024),
    cst_ident=(128, 128), cst_masku=(128, 128), cst_ttab=(128, 128), cst_rowmask=(128, 8), cst_sign=(128, 2), cst_etab=(128, 17), cst_bmask=(128, 128),
)
OUT_SHAPES = dict(
    yp=(SEQ, D_MODEL), ys=(NSAMP, D_MODEL), vrows=(2, NSAMP, 512),
    cbp=(2, 30, 512), cbs=(2, NSAMP, 30, 512), ccp=(2, 2, 512), ccs=(2, NSAMP, 2, 512),
    srp=(2, 32, 64), srs=(2, NSAMP, 32, 64), sip=(2, 32, 64), sis=(2, NSAMP, 32, 64),
    ffp=(4, 2, D_FF), ffs=(4, NSAMP, 2, D_FF),
)


class StopBuild(Exception):
    pass


def build(n_layers=DEPTH):
    DBG = os.environ.get("KDBG", "").split(",")
    STOPAT = None
    for d_ in DBG:
        if d_.startswith("stop="):
            STOPAT = d_[5:]

    def stage_mark(name):
        if STOPAT == name:
            raise StopBuild()
    nc = bass.Bass("TRN2", target_bir_lowering=False)
    S = Sched(nc)
    DI = {k: nc.dram_tensor(k, list(v), F32, kind="ExternalInput") for k, v in IN_SHAPES.items()}
    DO = {k: nc.dram_tensor(k, list(v), F32, kind="ExternalOutput") for k, v in OUT_SHAPES.items()}

    SCR = {}
    for o_ in range(2):
        SCR[f"sm{o_}"] = nc.dram_tensor(f"scr_sm{o_}", [128, 160], F32, kind="Internal")
        for k_ in range(4):
            for nm_ in ("CLT", "BJT", "BJTS", "KT", "CPAD"):
                SCR[f"{nm_}{o_}{k_}"] = nc.dram_tensor(f"scr_{nm_}{o_}{k_}", [128, 1024], BF16, kind="Internal")
            for nm_ in ("COS", "SINP"):
                SCR[f"{nm_}{o_}{k_}"] = nc.dram_tensor(f"scr_{nm_}{o_}{k_}", [128, 1024], F32, kind="Internal")

    def dap(t, off, pat):
        return bass.AP(t, off, [list(p) for p in pat])

    class Bump:
        def __init__(self, base):
            self.off = base
            self.n = 0

        def alloc(self, shape, dt=F32, name=None):
            nbytes = int(np.prod(shape[1:])) * (4 if dt == F32 else 2)
            nbytes = (nbytes + 31) // 32 * 32
            self.n += 1
            t = nc.alloc_sbuf_tensor_at(f"sb_{name or 't'}_{self.off}_{self.n}", list(shape), dt, offset=self.off)
            self.off += nbytes
            assert self.off <= SBUF_LIMIT, (name, self.off)
            return t.ap()

    fx = Bump(SBUF_BASE)
    X = fx.alloc([128, 8, TT], F32, "X")
    XN_OFF = fx.off
    XN = fx.alloc([128, 8, TT], BF16, "XN")
    NRING = 6
    RING = [fx.alloc([128, NFC, 128], BF16, f"ring{i}") for i in range(NRING)]
    PCOL = fx.alloc([128, 648], F32, "PCOL")
    ident = fx.alloc([128, 128], F32, "ident")
    identb = fx.alloc([128, 128], BF16, "identb")
    masku = fx.alloc([128, 128], F32, "masku")
    ttab = fx.alloc([128, 128], F32, "ttab")
    rowmask = fx.alloc([128, 8], F32, "rowmask")
    signc = fx.alloc([128, 2], F32, "signc")
    etab = fx.alloc([128, 17], F32, "etab")
    bmask = fx.alloc([128, 128], F32, "bmask")
    onesb = fx.alloc([128, 128], BF16, "onesb")
    onesf = fx.alloc([128, 128], F32, "onesf")
    cst = fx.alloc([128, 8], F32, "cst")
    CB = fx.alloc([128, 2, 4, 30], F32, "CB")
    CC = fx.alloc([128, 2, 4, 2], F32, "CC")
    CF = fx.alloc([128, 4, NFC, 2], F32, "CF")
    HL = fx.alloc([128, 2, 32], F32, "HL")
    RS_OFF = fx.off
    RS = fx.alloc([128, 512], F32, "RS")
    SQ_OFF = fx.off
    SQ = fx.alloc([128, 8, 512], BF16, "SQ")
    ARENA0 = fx.off

    PS = [nc.alloc_psum_tensor(f"ps{i}", [128, 512], F32).ap() for i in range(8)]
    psrr = {"m": 0, "a": 0}

    def psum(kind="m"):
        if kind in ("ra", "ga"):
            base = 4 if kind == "ra" else 6
            i = psrr.get(kind, 0)
            psrr[kind] = (i + 1) % 2
            return PS[base + i], f"PS{base + i}"
        if kind == "m":
            i = psrr["m"]
            psrr["m"] = (i + 1) % 4
            return PS[i], f"PS{i}"
        i = psrr["a"]
        psrr["a"] = (i + 1) % 4
        return PS[4 + i], f"PS{4 + i}"

    def act(out, in_, func, r, w, bias=None, scale=None):
        kw = {}
        if bias is not None:
            kw["bias"] = bias
        if scale is not None:
            kw["scale"] = scale
        S.op("act", lambda e: e.activation(out=out, in_=in_, func=func, **kw), r, w)

    def tt(out, a, b, op, r, w, eng="dve"):
        S.op(eng, lambda e: e.tensor_tensor(out=out, in0=a, in1=b, op=op), r, w)

    def ts(out, a, s1, s2, op0, op1, r, w, eng="dve"):
        if op1 is None:
            S.op(eng, lambda e: e.tensor_scalar(out=out, in0=a, scalar1=s1, scalar2=None, op0=op0), r, w)
        else:
            S.op(eng, lambda e: e.tensor_scalar(out=out, in0=a, scalar1=s1, scalar2=s2, op0=op0, op1=op1), r, w)

    def stt(out, a, s, b, op0, op1, r, w):
        S.op("dve", lambda e: e.scalar_tensor_tensor(out=out, in0=a, scalar=s, in1=b, op0=op0, op1=op1), r, w)

    def cp(out, in_, r, w, eng="dve"):
        if eng == "act":
            S.op("act", lambda e: e.activation(out=out, in_=in_, func=AF.Copy), r, w)
        else:
            S.op(eng, lambda e: e.tensor_copy(out=out, in_=in_), r, w)

    def memset(ap, v, w, eng="pool"):
        S.op(eng, lambda e: e.memset(ap, v), (), w)

    def mm(ps_ap, pairs, r, w):
        pairs = list(pairs)

        def fn(e):
            n = len(pairs)
            ins = None
            for i, (l, rh) in enumerate(pairs):
                ins = e.matmul(ps_ap, lhsT=l, rhs=rh, start=(i == 0), stop=(i == n - 1))
            return ins
        S.op("pe", fn, r, w)

    def tr(ps_ap, in_, r, w, idt=None):
        k = in_.shape[0]
        idn = (ident if idt is None else idt)[0:k, 0:k]
        S.op("pe", lambda e: e.transpose(out=ps_ap, in_=in_, identity=idn), list(r) + ["ident"], w)

    def dma(out, in_, r, w, q="sp", slow=False):
        if slow:
            S.dma(q, lambda e: e.dma_start(out=out, in_=in_, allow_slow_non_contiguous=True), r, w)
        else:
            S.dma(q, lambda e: e.dma_start(out=out, in_=in_), r, w)

    dma(ident, DI["cst_ident"].ap(), (), ["ident"])
    dma(masku, DI["cst_masku"].ap(), (), ["masku"])
    dma(ttab, DI["cst_ttab"].ap(), (), ["ttab"])
    dma(rowmask, DI["cst_rowmask"].ap(), (), ["rowmask"])
    dma(signc, DI["cst_sign"].ap(), (), ["signc"])
    dma(etab, DI["cst_etab"].ap(), (), ["etab"])
    dma(bmask, DI["cst_bmask"].ap(), (), ["bmask"])
    cp(identb, ident, ["ident"], ["identb"])
    memset(onesb, 1.0, ["onesb"])
    memset(onesf, 1.0 / 512.0, ["onesf"])
    memset(cst[:, 0:1], EPS, ["cst"])
    memset(cst[:, 1:2], 0.0, ["cst"])
    memset(cst[:, 2:3], math.pi / 2, ["cst"])
    memset(cst[:, 3:4], 1.0, ["cst"])
    memset(CB, 0.0, ["CB"])
    memset(CC, 0.0, ["CC"])
    memset(CF, 0.0, ["CF"])
    memset(HL, 0.0, ["HL"])

    pcol_map = {}
    stage_rows = []

    def reg(name, t, off, nrows):
        stage_rows.append((name, t, off, nrows))

    for l in range(4):
        reg(f"nm{l}", DI["norm_mix"], l * 1024, 8)
        reg(f"nf{l}", DI["norm_ffn"], l * 1024, 8)
    reg("nfin", DI["norm_final"], 0, 8)
    for e_ in range(2):
        reg(f"bcw{e_}", DI["b_conv_w"], e_ * 31 * 512, 124)
        reg(f"bcb{e_}", DI["b_conv_b"], e_ * 512, 4)
        reg(f"blg{e_}", DI["b_ln_g"], e_ * 512, 4)
        reg(f"blb{e_}", DI["b_ln_b"], e_ * 512, 4)
        reg(f"ccw{e_}", DI["c_conv_w"], e_ * 3 * 512, 12)
        reg(f"sd{e_}", DI["s5_d"], e_ * 512, 4)
        reg(f"sgb{e_}", DI["s5_glu_b"], e_ * 512, 4)
    for l in range(4):
        reg(f"fcw{l}", DI["ffn_conv_w"], l * 3 * D_FF, 66)
    col = 0
    stage = []
    stages = []
    used = 0
    for item in stage_rows:
        if used + item[3] > 128:
            stages.append(stage)
            stage = []
            used = 0
        stage.append((item, used))
        used += item[3]
    stages.append(stage)
    ar = Bump(ARENA0)
    STG = ar.alloc([128, 128], F32, "STG")
    for si, stage in enumerate(stages):
        nrow = 0
        for (name, t, off, nrows), r0 in stage:
            dma(STG[r0:r0 + nrows, :], dap(t, off, [[128, nrows], [1, 128]]), (), ["STG"])
            pcol_map[name] = col + r0
            nrow = r0 + nrows
        pp, pk = psum("a")
        tr(pp[:, 0:nrow], STG[0:nrow, :], ["STG"], [pk])
        cp(PCOL[:, col:col + nrow], pp[:, 0:nrow], [pk], ["PCOL"])
        col += nrow
    assert col <= 648, col

    def pc(name, j=0, n=1):
        c0 = pcol_map[name] + j
        return PCOL[:, c0:c0 + n]

    plan = []

    def plan_layer(l):
        o = l // 2
        wi, wo = DI["w_mix_in"], DI["w_mix_out"]
        bi, bo = l * 1024 * 2048, l * 1024 * 1024
        if l % 2 == 0 or "evenonly" in DBG:
            order = [0, 1, 2, 3] + [12, 8, 13, 9, 14, 10, 15, 11]
            for n in order:
                plan.append((f"L{l}in{n}", wi, bi + n * 128, 2048, 8))
        else:
            order = []
            for c in range(4):
                order += [c, 8 + c, 4 + c]
            order += [12, 13, 14, 15]
            for n in order:
                plan.append((f"L{l}in{n}", wi, bi + n * 128, 2048, 8))
            for n in range(4):
                plan.append((f"L{l}glu{n}", DI["s5_glu_w"], o * 512 * 512 + n * 128, 512, 4))
        for n in range(8):
            plan.append((f"L{l}out{n}", wo, bo + n * 128, 1024, 8))
        fi, fd = DI["ffn_w_in"], DI["ffn_w_down"]
        for f in range(NFC):
            plan.append((f"L{l}f1_{f}", fi, l * 1024 * 5632 + f * 128, 5632, 8))
            plan.append((f"L{l}f2_{f}", fi, l * 1024 * 5632 + D_FF + f * 128, 5632, 8))
        for n in range(8):
            plan.append((f"L{l}dn{n}", fd, l * D_FF * 1024 + n * 128, 1024, NFC))

    LAYERS = list(range(n_layers))
    for d_ in DBG:
        if d_.startswith("layers="):
            LAYERS = [int(c) for c in d_[7:]]
    for half in range(2):
        for l in LAYERS:
            plan_layer(l)
    wstate = {"issued": 0, "next": 0}

    def w_issue_upto(i):
        while wstate["issued"] <= min(i, len(plan) - 1):
            j = wstate["issued"]
            tag, t, off, rs, kc = plan[j]
            slot = j % NRING
            src = dap(t, off, [[rs, 128], [rs * 128, kc], [1, 128]])
            dst = RING[slot][:, 0:kc, :]
            S.dma("pool", lambda e, dst=dst, src=src: e.dma_start(out=dst, in_=src), (), [f"RING{slot}"])
            wstate["issued"] += 1

    def wnext(tag):
        i = wstate["next"]
        assert plan[i][0] == tag, (plan[i][0], tag)
        w_issue_upto(i + NRING - 1)
        wstate["next"] += 1
        slot = i % NRING
        return RING[slot], f"RING{slot}"

    def tiles_of(half):
        t = [(0, 512), (512, 1024)]
        if half == 1:
            t.append((1024, TT))
        return t

    def kx(c, ti):
        return f"X{c}.{ti}"

    def kxn(c, ti):
        return f"XN{c}.{ti}"

    def rmsnorm(half, gname, final=False):
        for ti, (a, b) in enumerate(tiles_of(half)):
            n = b - a
            for c in range(8):
                act(SQ[:, c, 0:n], X[:, c, a:b], AF.Square, [kx(c, ti)], [f"SQ{c}"])
            pp, pk = psum("a")
            mm(pp[:, 0:n], [(onesb, SQ[:, c, 0:n]) for c in range(8)], [f"SQ{c}" for c in range(8)] + ["onesb"], [pk])
            act(RS[:, 0:n], pp[:, 0:n], AF.Sqrt, [pk, "cst"], ["RS"], bias=cst[:, 0:1], scale=1.0 / 1024.0)
            S.op("dve", lambda e, n=n: e.reciprocal(out=RS[:, 0:n], in_=RS[:, 0:n]), ["RS"], ["RS"])
            for c in range(8):
                if final:
                    stt(X[:, c, a:b], X[:, c, a:b], pc(gname, c), RS[:, 0:n], ALU.mult, ALU.mult,
                        [kx(c, ti), "RS", "PCOL"], [kx(c, ti)])
                else:
                    stt(XN[:, c, a:b], X[:, c, a:b], pc(gname, c), RS[:, 0:n], ALU.mult, ALU.mult,
                        [kx(c, ti), "RS", "PCOL"], [kxn(c, ti)])

    def panel_mm(pan, pkey, kc, rhs_fn, rkeys_fn, half, consume):
        for ti, (a, b) in enumerate(tiles_of(half)):
            n = b - a
            pp, pk = psum("m")
            mm(pp[:, 0:n], [(pan[:, k, :], rhs_fn(k, a, b)) for k in range(kc)], [pkey] + rkeys_fn(ti), [pk])
            consume(ti, a, b, pp[:, 0:n], pk)

    def xn_rhs(k, a, b):
        return XN[:, k, a:b]

    def xn_keys(ti):
        return [kxn(c, ti) for c in range(8)]

    def load_x(half):
        ar = Bump(ARENA0)
        XT = [ar.alloc([128, 1024], F32, f"XT{i}") for i in range(4)]
        for tb in range(8):
            t0 = half * TPH + tb * 128
            buf = XT[tb % 4]
            dma(buf, dap(DI["xp"], t0 * 1024, [[1024, 128], [1, 1024]]), (), [f"XT{tb % 4}"])
            for g in range(2):
                pp, pk = psum("a")
                for j in range(4):
                    c = g * 4 + j
                    tr(pp[:, j * 128:(j + 1) * 128], buf[:, c * 128:(c + 1) * 128], [f"XT{tb % 4}"], [pk])
                ti = tb // 4
                dst = X[:, g * 4:g * 4 + 4, tb * 128:(tb + 1) * 128]
                src = pp.rearrange("p (j t) -> p j t", j=4)
                cp(dst, src, [pk], [kx(g * 4 + j, ti) for j in range(4)], eng=("act" if g == 0 else "dve"))
        if half == 1:
            XS = ar.alloc([NSAMP, 1024], F32, "XS")
            dma(XS, DI["xs"].ap(), (), ["XS"])
            pp, pk = psum("a")
            for c in range(8):
                tr(pp[:, c * 16:(c + 1) * 16], XS[:, c * 128:(c + 1) * 128], ["XS"], [pk])
            cp(X[:, :, TPH:TT], pp[:, 0:128].rearrange("p (c t) -> p c t", c=8), [pk], [kx(c, 2) for c in range(8)])

    def store_y(half):
        ar = Bump(ARENA0)
        YT = [ar.alloc([128, 1024], F32, f"YT{i}") for i in range(2)]
        for tb in range(8):
            t0 = half * TPH + tb * 128
            buf = YT[tb % 2]
            for g in range(2):
                pp, pk = psum("a")
                for j in range(4):
                    c = g * 4 + j
                    tr(pp[:, j * 128:(j + 1) * 128], X[:, c, tb * 128:(tb + 1) * 128], [kx(c, tb // 4)], [pk])
                cp(buf[:, g * 512:(g + 1) * 512], pp, [pk], [f"YT{tb % 2}"], eng=("act" if g == 0 else "dve"))
            dma(dap(DO["yp"], t0 * 1024, [[1024, 128], [1, 1024]]), buf, [f"YT{tb % 2}"], ())
        if half == 1:
            YS = ar.alloc([NSAMP, 1024], F32, "YS")
            for g in range(2):
                pp, pk = psum("a")
                for j in range(4):
                    c = g * 4 + j
                    tr(pp[0:NSAMP, j * 128:(j + 1) * 128], X[:, c, TPH:TT], [kx(c, 2)], [pk])
                cp(YS[:, g * 512:(g + 1) * 512], pp[0:NSAMP, :], [pk], ["YS"])
            dma(DO["ys"].ap(), YS, ["YS"], ())

    def resid_consumer(n_chunk):
        def consume(ti, a, b, pp, pk):
            tt(X[:, n_chunk, a:b], X[:, n_chunk, a:b], pp, ALU.add, [kx(n_chunk, ti), pk], [kx(n_chunk, ti)])
        return consume

    def ffn(half, l):
        S.fence()
        ar = Bump(ARENA0)
        HID = ar.alloc([128, NFC, TT], BF16, "HID")
        Z1 = [ar.alloc([128, 2 + TT], F32, f"Z1_{i}") for i in range(2)]
        ACC = [ar.alloc([128, TT], F32, f"ACCF{i}") for i in range(2)]
        SIL = [ar.alloc([128, TT], F32, f"SIL{i}") for i in range(2)]
        SHF = ar.alloc([128, NFC, 32], F32, "SHF")
        ZS = ar.alloc([128, NFC, 18], F32, "ZS")
        STF = ar.alloc([32, D_FF], F32, "STF")
        OTF = ar.alloc([18, D_FF], F32, "OTF")
        tiles = tiles_of(half)
        nt = len(tiles)
        rmsnorm(half, f"nf{l}")
        if half == 1:
            dma(STF, dap(DI["st_ff"], l * NSAMP * 2 * D_FF, [[D_FF, 32], [1, D_FF]]), (), ["STF"])
            for f4 in range(0, NFC, 4):
                nn = min(4, NFC - f4)
                pp, pk = psum("a")
                for j in range(nn):
                    tr(pp[:, j * 32:(j + 1) * 32], STF[:, (f4 + j) * 128:(f4 + j + 1) * 128], ["STF"], [pk])
                cp(SHF[:, f4:f4 + nn, :], pp[:, 0:nn * 32].rearrange("p (j t) -> p j t", j=nn), [pk], ["SHF"])
            dma(dap(DO["ffs"], l * NSAMP * 2 * D_FF, [[2 * D_FF, NSAMP], [1, D_FF]]),
                dap(DI["st_ff"], l * NSAMP * 2 * D_FF + D_FF, [[2 * D_FF, NSAMP], [1, D_FF]]), (), ())
        for f in range(NFC):
            z1 = Z1[f % 2]
            acc = ACC[f % 2]
            sil = SIL[f % 2]
            zk, ak, sk = f"Z1_{f % 2}", f"ACCF{f % 2}", f"SIL{f % 2}"
            w0, w1, w2 = pc(f"fcw{l}", 0 * NFC + f), pc(f"fcw{l}", 1 * NFC + f), pc(f"fcw{l}", 2 * NFC + f)
            cp(z1[:, 0:2], CF[:, l, f, :], ["CF"], [zk + ".h"], eng="pool")
            pan, pkey = wnext(f"L{l}f1_{f}")

            def cons1(ti, a, b, pp, pk, z1=z1, zk=zk):
                cp(z1[:, 2 + a:2 + b], pp, [pk], [f"{zk}.{ti}"], eng="act")
            panel_mm(pan, pkey, 8, xn_rhs, xn_keys, half, cons1)
            zkeys = [zk + ".h"] + [f"{zk}.{ti}" for ti in range(2)]
            ts(acc[:, 0:TPH], z1[:, 0:TPH], w0, None, ALU.mult, None, zkeys + ["PCOL"], [ak])
            stt(acc[:, 0:TPH], z1[:, 1:TPH + 1], w1, acc[:, 0:TPH], ALU.mult, ALU.add, zkeys + ["PCOL", ak], [ak])
            stt(acc[:, 0:TPH], z1[:, 2:TPH + 2], w2, acc[:, 0:TPH], ALU.mult, ALU.add, zkeys + ["PCOL", ak], [ak])
            cp(CF[:, l, f, :], z1[:, TPH:TPH + 2], zkeys, ["CF"], eng="pool")
            if half == 1:
                sh = SHF[:, f, :].rearrange("p (t r) -> p t r", r=2)
                zs = z1[:, 2 + TPH:2 + TT]
                ts(acc[:, TPH:TT], sh[:, :, 0], w0, None, ALU.mult, None, ["SHF", "PCOL"], [ak + "s"])
                stt(acc[:, TPH:TT], sh[:, :, 1], w1, acc[:, TPH:TT], ALU.mult, ALU.add, ["SHF", "PCOL", ak + "s"], [ak + "s"])
                stt(acc[:, TPH:TT], zs, w2, acc[:, TPH:TT], ALU.mult, ALU.add, [f"{zk}.2", "PCOL", ak + "s"], [ak + "s"])
                cp(ZS[:, f, 0:2], z1[:, TPH:TPH + 2], zkeys, ["ZS"], eng="pool")
                cp(ZS[:, f, 2:18], zs, [f"{zk}.2"], ["ZS"], eng="pool")
            ncol = TT if half == 1 else TPH
            akeys = [ak] + ([ak + "s"] if half == 1 else [])
            act(sil[:, 0:ncol], acc[:, 0:ncol], AF.Silu, akeys, [sk])
            pan, pkey = wnext(f"L{l}f2_{f}")

            def cons2(ti, a, b, pp, pk, sil=sil, sk=sk, f=f):
                tt(HID[:, f, a:b], sil[:, a:b], pp, ALU.mult, [sk, pk], [f"HID{f}.{ti}"])
            panel_mm(pan, pkey, 8, xn_rhs, xn_keys, half, cons2)
        if half == 1:
            for f4 in range(0, NFC, 4):
                nn = min(4, NFC - f4)
                pp, pk = psum("a")
                for j in range(nn):
                    tr(pp[0:18, j * 128:(j + 1) * 128], ZS[:, f4 + j, :], ["ZS"], [pk])
                cp(OTF[:, f4 * 128:(f4 + nn) * 128], pp[0:18, 0:nn * 128], [pk], ["OTF"])
            dma(dap(DO["ffp"], l * 2 * D_FF, [[D_FF, 2], [1, D_FF]]), OTF[0:2, :], ["OTF"], ())
            dma(dap(DO["ffs"], l * NSAMP * 2 * D_FF + D_FF, [[2 * D_FF, NSAMP], [1, D_FF]]), OTF[2:18, :], ["OTF"], ())
        for n in range(8):
            pan, pkey = wnext(f"L{l}dn{n}")
            panel_mm(pan, pkey, NFC, lambda k, a, b: HID[:, k, a:b],
                     lambda ti: [f"HID{f}.{ti}" for f in range(NFC)], half, resid_consumer(n))

    def mixer_even(half, l):
        e_ = l // 2
        S.fence()
        ar = Bump(ARENA0)
        UA = ar.alloc([128, 4, TT], F32, "UA")
        GLU = ar.alloc([128, 4, 30 + TPH], BF16, "GLUB")
        GLT = ar.alloc([128, 4, 30], F32, "GLT")
        GLS = ar.alloc([128, 4, NSAMP], F32, "GLS")
        DG = ar.alloc([128, 31, 128], BF16, "DG")
        MIXO = ar.alloc([128, 8, TT], BF16, "MIXO")
        WV = ar.alloc([128, 8, 512], BF16, "WV")
        SIGF = ar.alloc([128, TT], F32, "SIGF")
        OFF_VG = ar.off
        VG = [ar.alloc([128, 512], F32, f"VG{i}") for i in range(2)]
        VF = [ar.alloc([128, 512], F32, f"VF{i}") for i in range(2)]
        VB = [ar.alloc([128, 512], BF16, f"VB{i}") for i in range(2)]
        STT_ = [ar.alloc([128, 4, 6], F32, f"BST{i}") for i in range(2)]
        MV = [ar.alloc([128, 4, 2], F32, f"MV{i}") for i in range(2)]
        RSD = [ar.alloc([128, 4], F32, f"RSD{i}") for i in range(2)]
        GT = ar.alloc([128, 512], F32, "GT")
        BT = ar.alloc([128, 512], F32, "BT")
        BS = ar.alloc([128, 512], F32, "BS")
        WS = ar.alloc([128, 4, 128], F32, "WS")
        WMT = ar.alloc([128, 4, 128], BF16, "WMT")
        W00 = ar.alloc([NSAMP, 4], F32, "W00")
        W00D = ar.alloc([NSAMP, 4, NSAMP], BF16, "W00D")
        GTMP = [ar.alloc([128, 512], F32, f"GTMP{i}") for i in range(2)]
        SHB = ar.alloc([128, 4, NSAMP, 30], F32, "SHB")
        MEAN = ar.alloc([128, 512], F32, "MEAN")
        VAR = ar.alloc([128, 512], F32, "VAR")
        SQF = [ar.alloc([128, 512], F32, f"SQF{i}") for i in range(2)]
        T1 = [ar.alloc([128, 512], F32, f"T1_{i}") for i in range(2)]
        STB = GT[0:120, :]
        PRD = BT[:, 0:NSAMP * 30].rearrange("p (t k) -> p t k", k=30)
        OTB = BS[0:30, :]
        OTS = GTMP[0][0:NSAMP, :]
        DG2 = nc.alloc_sbuf_tensor_at(f"sb_DG2_{OFF_VG}_{l}_{half}", [128, 31, 128], BF16, offset=OFF_VG).ap()
        DGS = [(DG, ["DG"]), (DG2, ["VG0", "VG1", "VF0", "VF1"])]
        tiles = tiles_of(half)
        rmsnorm(half, f"nm{l}")
        dma(WV, dap(DI["w_mix_in"], l * 1024 * 2048 + 512, [[2048, 128], [2048 * 128, 8], [1, 512]]), (), ["WV"], q="pool")
        dma(GT, dap(DI["a_ln_g"], e_ * 512, [[0, 128], [1, 512]]), (), ["GT"])
        dma(BT, dap(DI["a_ln_b"], e_ * 512, [[0, 128], [1, 512]]), (), ["BT"])
        dma(BS, dap(DI["a_bs"], e_ * 512, [[0, 128], [1, 512]]), (), ["BS"])
        dma(WS, dap(DI["a_ws"], e_ * 4 * 128 * 128, [[128, 128], [128 * 128, 4], [1, 128]]), (), ["WS"])
        pp, pk = psum("a")
        for h in range(4):
            tr(pp[:, h * 128:(h + 1) * 128], WS[:, h, :], ["WS"], [pk])
        tt(WMT, pp.rearrange("p (h t) -> p h t", h=4), masku.unsqueeze(1).to_broadcast([128, 4, 128]), ALU.mult,
           [pk, "masku"], ["WMT"])
        if half == 1:
            dma(W00, dap(DI["a_ws"], e_ * 4 * 128 * 128, [[0, NSAMP], [128 * 128, 4]]), (), ["W00"], slow=True)
            for h in range(4):
                ts(W00D[:, h, :], ident[0:NSAMP, 0:NSAMP], W00[:, h:h + 1], None, ALU.mult, None, ["ident", "W00"], ["W00D"])
        for c in range(4):
            pan, pkey = wnext(f"L{l}in{c}")

            def cons(ti, a, b, pp, pk, c=c):
                act(UA[:, c, a:b], pp, AF.Gelu_apprx_tanh, [pk], [f"UA{c}.{ti}"])
            panel_mm(pan, pkey, 8, xn_rhs, xn_keys, half, cons)
        nblk = 8 + (1 if half == 1 else 0)
        def e2_block(blk):
            if True:
                samp = blk == 8
                nt = NSAMP if samp else 128
                a = TPH if samp else blk * 128
                ti = 2 if samp else blk // 4
                i2 = blk % 2
                pb_ = 4 + 2 * i2
                pp, pk = PS[pb_], f"PS{pb_}"
                mm(pp[0:nt, :], [(XN[:, k, a:a + nt], WV[:, k, :]) for k in range(8)], xn_keys(ti) + ["WV"], [pk])
                yield
                vg, vf, vb = VG[i2], VF[i2], VB[i2]
                act(vg[0:nt, :], pp[0:nt, :], AF.Gelu_apprx_tanh, [pk], [f"VG{i2}"])
                yield
                for h in range(4):
                    S.op("dve", lambda e, h=h, vg=vg, i2=i2, nt=nt: e.bn_stats(out=STT_[i2][0:nt, h, :], in_=vg[0:nt, h * 128:(h + 1) * 128]),
                         [f"VG{i2}"], [f"BST{i2}"])
                yield
                for h in range(4):
                    S.op("dve", lambda e, h=h, i2=i2, nt=nt: e.bn_aggr(out=MV[i2][0:nt, h, :], in_=STT_[i2][0:nt, h, :]),
                         [f"BST{i2}"], [f"MV{i2}"])
                yield
                yield
                act(RSD[i2][0:nt, :], MV[i2][0:nt, :, 1], AF.Sqrt, [f"MV{i2}", "cst"], [f"RSD{i2}"], bias=cst[0:nt, 0:1], scale=1.0)
                S.op("dve", lambda e, i2=i2, nt=nt: e.reciprocal(out=RSD[i2][0:nt, :], in_=RSD[i2][0:nt, :]), [f"RSD{i2}"], [f"RSD{i2}"])
                yield
                for h in range(4):
                    ts(vf[0:nt, h * 128:(h + 1) * 128], vg[0:nt, h * 128:(h + 1) * 128], MV[i2][0:nt, h, 0:1], RSD[i2][0:nt, h:h + 1],
                       ALU.subtract, ALU.mult, [f"VG{i2}", f"MV{i2}", f"RSD{i2}"], [f"VF{i2}"])
                yield
                tt(vf[0:nt, :], vf[0:nt, :], GT[0:nt, :], ALU.mult, [f"VF{i2}", "GT"], [f"VF{i2}"])
                yield
                if samp:
                    tt(vf[0:nt, :], vf[0:nt, :], BT[0:nt, :], ALU.add, [f"VF{i2}", "BT"], [f"VF{i2}"])
                    yield
                    cp(vb[0:nt, :], vf[0:nt, :], [f"VF{i2}"], [f"VB{i2}"], eng="act")
                else:
                    tt(vb[0:nt, :], vf[0:nt, :], BT[0:nt, :], ALU.add, [f"VF{i2}", "BT"], [f"VB{i2}"])
                yield
                if samp:
                    dma(dap(DO["vrows"], e_ * NSAMP * 512, [[512, NSAMP], [1, 512]]), vf[0:nt, :], [f"VF{i2}"], ())
                pg, pgk = PS[pb_ + 1], f"PS{pb_ + 1}"
                gt_ = GTMP[i2]
                if not samp:
                    for h in range(4):
                        mm(pg[:, h * 128:(h + 1) * 128], [(vb[:, h * 128:(h + 1) * 128], WMT[:, h, :])], [f"VB{i2}", "WMT"], [pgk])
                    yield
                    tt(gt_, pg, BS, ALU.add, [pgk, "BS"], [f"GTMP{i2}"])
                    yield
                    tt(MIXO[:, 0:4, a:a + 128], gt_.rearrange("p (h t) -> p h t", h=4), UA[:, :, a:a + 128], ALU.mult,
                       [f"GTMP{i2}"] + [f"UA{c}.{ti}" for c in range(4)], [f"MIXO{c}.{ti}" for c in range(4)])
                else:
                    for h in range(4):
                        mm(pg[:, h * NSAMP:(h + 1) * NSAMP], [(vb[0:NSAMP, h * 128:(h + 1) * 128], W00D[:, h, :])], [f"VB{i2}", "W00D"], [pgk])
                    bsv = BS.rearrange("p (h t) -> p h t", h=4)[:, :, 0:1].to_broadcast([128, 4, NSAMP])
                    g3 = gt_[:, 0:4 * NSAMP].rearrange("p (h t) -> p h t", h=4)
                    tt(g3, pg[:, 0:4 * NSAMP].rearrange("p (h t) -> p h t", h=4), bsv, ALU.add, [pgk, "BS"], [f"GTMP{i2}"])
                    tt(MIXO[:, 0:4, TPH:TT], g3, UA[:, :, TPH:TT], ALU.mult,
                       [f"GTMP{i2}"] + [f"UA{c}.2" for c in range(4)], [f"MIXO{c}.2" for c in range(4)])
                yield
        cp(GLU[:, :, 0:30], CB[:, e_, :, :], ["CB"], ["GLU.h"], eng="pool")
        def e3a_gen():
            for c in range(4):
                pan, pkey = wnext(f"L{l}in{12 + c}")

                def consg(ti, a, b, pp, pk, c=c):
                    act(SIGF[:, a:b], pp, AF.Sigmoid, [pk], [f"SIGF.{ti}"])
                panel_mm(pan, pkey, 8, xn_rhs, xn_keys, half, consg)
                yield
                pan2, pkey2 = wnext(f"L{l}in{8 + c}")

                def consa(ti, a, b, pp, pk, c=c):
                    if ti < 2:
                        tt(GLU[:, c, 30 + a:30 + b], pp, SIGF[:, a:b], ALU.mult, [pk, f"SIGF.{ti}"], [f"GLU{c}.{ti}"])
                        if ti == 1:
                            tt(GLT[:, c, :], pp[:, 482:512], SIGF[:, b - 30:b], ALU.mult, [pk, f"SIGF.{ti}"], ["GLT"])
                    else:
                        tt(GLS[:, c, :], pp, SIGF[:, a:b], ALU.mult, [pk, f"SIGF.{ti}"], ["GLS"])
                panel_mm(pan2, pkey2, 8, xn_rhs, xn_keys, half, consa)
                yield
        pending = list(range(nblk))
        streams = [e3a_gen()]
        nact = 0
        blkset = set()
        while streams or pending:
            while pending and nact < 2:
                g_ = e2_block(pending.pop(0))
                blkset.add(id(g_))
                streams.append(g_)
                nact += 1
            for st_ in list(streams):
                try:
                    next(st_)
                except StopIteration:
                    streams.remove(st_)
                    if id(st_) in blkset:
                        nact -= 1
        if half == 1:
            for rt in range(4):
                dma(STB, dap(DI["st_cb"], e_ * NSAMP * 30 * 512 + rt * 120 * 512, [[512, 120], [1, 512]]), (), ["GT"])
                pp, pk = psum("a")
                for c in range(4):
                    tr(pp[:, c * 120:(c + 1) * 120], STB[:, c * 128:(c + 1) * 128], ["GT"], [pk])
                cp(SHB[:, :, rt * 4:(rt + 1) * 4, :].rearrange("p c t k -> p c (t k)"),
                   pp[:, 0:480].rearrange("p (c x) -> p c x", c=4), [pk], ["SHB"])
            dma(dap(DO["cbs"], e_ * NSAMP * 30 * 512, [[30 * 512, NSAMP], [1, 29 * 512]]),
                dap(DI["st_cb"], e_ * NSAMP * 30 * 512 + 512, [[30 * 512, NSAMP], [1, 29 * 512]]), (), ())
        def build_dg(c):
            dg, dk = DGS[c % 2]
            for k in range(31):
                act(dg[:, k, :], ident, AF.Copy, ["ident", "PCOL"], dk, scale=pc(f"bcw{e_}", k * 4 + c))
        build_dg(0)
        for c in range(4):
            gk = ["GLU.h"] + [f"GLU{c}.{ti}" for ti in range(2)]
            if c < 3:
                build_dg(c + 1)
            dg, dk = DGS[c % 2]
            for ti in range(2):
                a = ti * 512
                pp, pk = psum("m")
                mm(pp, [(dg[:, k, :], GLU[:, c, a + k:a + k + 512]) for k in range(31)], gk + dk, [pk])
                ts(UA[:, c, a:a + 512], pp, pc(f"bcb{e_}", c), None, ALU.add, None, [pk, "PCOL"], [f"UA{c}.{ti}"])
            if half == 1:
                wrow = PCOL[:, pcol_map[f"bcw{e_}"] + c: pcol_map[f"bcw{e_}"] + c + 4 * 29 + 1: 4]
                tt(PRD, SHB[:, c, :, :], wrow.unsqueeze(1).to_broadcast([128, NSAMP, 30]), ALU.mult, ["SHB", "PCOL"], ["BT"])
                S.op("dve", lambda e, c=c: e.tensor_reduce(out=UA[:, c, TPH:TT], in_=PRD, axis=AX.X, op=ALU.add), ["BT"], [f"UA{c}.2"])
                stt(UA[:, c, TPH:TT], GLS[:, c, :], pc(f"bcw{e_}", 30 * 4 + c), UA[:, c, TPH:TT], ALU.mult, ALU.add,
                    ["GLS", "PCOL", f"UA{c}.2"], [f"UA{c}.2"])
                ts(UA[:, c, TPH:TT], UA[:, c, TPH:TT], pc(f"bcb{e_}", c), None, ALU.add, None, [f"UA{c}.2", "PCOL"], [f"UA{c}.2"])
        cp(CB[:, e_, :, :], GLT, ["GLT"], ["CB"], eng="pool")
        if half == 1:
            pp, pk = psum("a")
            for c in range(4):
                tr(pp[0:30, c * 128:(c + 1) * 128], GLT[:, c, :], ["GLT"], [pk])
            cp(OTB, pp[0:30, :], [pk], ["BS"])
            dma(dap(DO["cbp"], e_ * 30 * 512, [[512, 30], [1, 512]]), OTB, ["BS"], ())
            pp, pk = psum("a")
            for c in range(4):
                tr(pp[0:NSAMP, c * 128:(c + 1) * 128], GLS[:, c, :], ["GLS"], [pk])
            cp(OTS, pp[0:NSAMP, :], [pk], ["GTMP0"])
            dma(dap(DO["cbs"], e_ * NSAMP * 30 * 512 + 29 * 512, [[30 * 512, NSAMP], [1, 512]]), OTS, ["GTMP0"], ())
        for ti, (a, b) in enumerate(tiles):
            n = b - a
            pm, pmk = psum("a")
            mm(pm[:, 0:n], [(onesf, UA[:, c, a:b]) for c in range(4)], [f"UA{c}.{ti}" for c in range(4)] + ["onesf"], [pmk])
            p2, p2k = psum("a")
            for c in range(4):
                act(SQF[c % 2][:, 0:n], UA[:, c, a:b], AF.Square, [f"UA{c}.{ti}"], [f"SQF{c % 2}"])
                S.op("pe", lambda e, c=c, n=n, p2=p2: e.matmul(p2[:, 0:n], lhsT=onesf, rhs=SQF[c % 2][:, 0:n], start=(c == 0), stop=(c == 3)),
                     [f"SQF{c % 2}", "onesf"], [p2k])
            cp(MEAN[:, 0:n], pm[:, 0:n], [pmk], ["MEAN"], eng="act")
            tt(VAR[:, 0:n], MEAN[:, 0:n], MEAN[:, 0:n], ALU.mult, ["MEAN"], ["VAR"])
            tt(VAR[:, 0:n], p2[:, 0:n], VAR[:, 0:n], ALU.subtract, [p2k, "VAR"], ["VAR"])
            act(VAR[:, 0:n], VAR[:, 0:n], AF.Sqrt, ["VAR", "cst"], ["VAR"], bias=cst[:, 0:1], scale=1.0)
            S.op("dve", lambda e, n=n: e.reciprocal(out=VAR[:, 0:n], in_=VAR[:, 0:n]), ["VAR"], ["VAR"])
            for c in range(4):
                t1 = T1[c % 2]
                tt(t1[:, 0:n], UA[:, c, a:b], MEAN[:, 0:n], ALU.subtract, [f"UA{c}.{ti}", "MEAN"], [f"T1_{c % 2}"])
                tt(t1[:, 0:n], t1[:, 0:n], VAR[:, 0:n], ALU.mult, [f"T1_{c % 2}", "VAR"], [f"T1_{c % 2}"])
                act(MIXO[:, 4 + c, a:b], t1[:, 0:n], AF.Silu, [f"T1_{c % 2}", "PCOL"], [f"MIXO{4 + c}.{ti}"],
                    bias=pc(f"blb{e_}", c), scale=pc(f"blg{e_}", c))
        for n_ in range(8):
            pan, pkey = wnext(f"L{l}out{n_}")
            panel_mm(pan, pkey, 8, lambda k, a, b: MIXO[:, k, a:b], lambda ti: [f"MIXO{c}.{ti}" for c in range(8)],
                     half, resid_consumer(n_))

    def mixer_odd(half, l):
        o = l // 2
        S.fence()
        ar = Bump(ARENA0)
        U32 = ar.alloc([128, 4, TT], F32, "UF")
        UBF = ar.alloc([128, 4, TT], BF16, "UBF")
        MIXO = ar.alloc([128, 8, TT], BF16, "MIXO")
        part2_base = ar.off
        P = [ar.alloc([128, 2 + TT], F32, f"P{i}") for i in range(2)]
        HIN = [ar.alloc([128, TT], F32, f"HIN{i}") for i in range(2)]
        CV = [ar.alloc([128, TT], F32, f"CV{i}") for i in range(2)]
        SHC = ar.alloc([128, 4, 32], F32, "SHC")
        STC = ar.alloc([32, 512], F32, "STC")
        ZC = ar.alloc([128, 4, 18], F32, "ZC")
        OTC = ar.alloc([18, 512], F32, "OTC")
        tiles = tiles_of(half)
        rmsnorm(half, f"nm{l}")
        if half == 1:
            dma(STC, dap(DI["st_cc"], o * NSAMP * 2 * 512, [[512, 32], [1, 512]]), (), ["STC"])
            pp, pk = psum("a")
            for c in range(4):
                tr(pp[:, c * 32:(c + 1) * 32], STC[:, c * 128:(c + 1) * 128], ["STC"], [pk])
            cp(SHC, pp[:, 0:128].rearrange("p (c t) -> p c t", c=4), [pk], ["SHC"])
            dma(dap(DO["ccs"], o * NSAMP * 2 * 512, [[2 * 512, NSAMP], [1, 512]]),
                dap(DI["st_cc"], o * NSAMP * 2 * 512 + 512, [[2 * 512, NSAMP], [1, 512]]), (), ())
        stage_mark("O1")
        for c in range(4):
            hin, p, cv = HIN[c % 2], P[c % 2], CV[c % 2]
            hk, pk_, ck = f"HIN{c % 2}", f"P{c % 2}", f"CV{c % 2}"
            w0, w1, w2 = pc(f"ccw{o}", 0 * 4 + c), pc(f"ccw{o}", 1 * 4 + c), pc(f"ccw{o}", 2 * 4 + c)
            cp(p[:, 0:2], CC[:, o, c, :], ["CC"], [pk_ + ".h"], eng="pool")
            pan, pkey = wnext(f"L{l}in{c}")

            def c1(ti, a, b, pp, pk, hin=hin, hk=hk):
                cp(hin[:, a:b], pp, [pk], [f"{hk}.{ti}"], eng="act")
            panel_mm(pan, pkey, 8, xn_rhs, xn_keys, half, c1)
            pan, pkey = wnext(f"L{l}in{8 + c}")

            def c2(ti, a, b, pp, pk, hin=hin, hk=hk, p=p, pk_=pk_):
                tt(p[:, 2 + a:2 + b], hin[:, a:b], pp, ALU.mult, [f"{hk}.{ti}", pk], [f"{pk_}.{ti}"])
            panel_mm(pan, pkey, 8, xn_rhs, xn_keys, half, c2)
            pkeys = [pk_ + ".h"] + [f"{pk_}.{ti}" for ti in range(2)]
            ts(cv[:, 0:TPH], p[:, 0:TPH], w0, None, ALU.mult, None, pkeys + ["PCOL"], [ck])
            stt(cv[:, 0:TPH], p[:, 1:TPH + 1], w1, cv[:, 0:TPH], ALU.mult, ALU.add, pkeys + ["PCOL", ck], [ck])
            stt(cv[:, 0:TPH], p[:, 2:TPH + 2], w2, cv[:, 0:TPH], ALU.mult, ALU.add, pkeys + ["PCOL", ck], [ck])
            cp(CC[:, o, c, :], p[:, TPH:TPH + 2], pkeys, ["CC"], eng="pool")
            if half == 1:
                sh = SHC[:, c, :].rearrange("p (t r) -> p t r", r=2)
                zs = p[:, 2 + TPH:2 + TT]
                ts(cv[:, TPH:TT], sh[:, :, 0], w0, None, ALU.mult, None, ["SHC", "PCOL"], [ck + "s"])
                stt(cv[:, TPH:TT], sh[:, :, 1], w1, cv[:, TPH:TT], ALU.mult, ALU.add, ["SHC", "PCOL", ck + "s"], [ck + "s"])
                stt(cv[:, TPH:TT], zs, w2, cv[:, TPH:TT], ALU.mult, ALU.add, [f"{pk_}.2", "PCOL", ck + "s"], [ck + "s"])
                cp(ZC[:, c, 0:2], p[:, TPH:TPH + 2], pkeys, ["ZC"], eng="pool")
                cp(ZC[:, c, 2:18], zs, [f"{pk_}.2"], ["ZC"], eng="pool")
            pan, pkey = wnext(f"L{l}in{4 + c}")

            def c3(ti, a, b, pp, pk, cv=cv, ck=ck, c=c):
                tt(MIXO[:, c, a:b], cv[:, a:b], pp, ALU.mult, [ck, ck + "s", pk] if half == 1 else [ck, pk], [f"MIXO{c}.{ti}"])
            panel_mm(pan, pkey, 8, xn_rhs, xn_keys, half, c3)
        if half == 1:
            pp, pk = psum("a")
            for c in range(4):
                tr(pp[0:18, c * 128:(c + 1) * 128], ZC[:, c, :], ["ZC"], [pk])
            cp(OTC, pp[0:18, :], [pk], ["OTC"])
            dma(dap(DO["ccp"], o * 2 * 512, [[512, 2], [1, 512]]), OTC[0:2, :], ["OTC"], ())
            dma(dap(DO["ccs"], o * NSAMP * 2 * 512 + 512, [[2 * 512, NSAMP], [1, 512]]), OTC[2:18, :], ["OTC"], ())
        stage_mark("O2")
        for c in range(4):
            pan, pkey = wnext(f"L{l}in{12 + c}")

            def cu(ti, a, b, pp, pk, c=c):
                if "nocu" in DBG:
                    return
                cp(U32[:, c, a:b], pp, [pk], [f"U32_{c}.{ti}"], eng="act")
                cp(UBF[:, c, a:b], U32[:, c, a:b], [f"U32_{c}.{ti}"], [f"UBF{c}.{ti}"], eng="dve")
            panel_mm(pan, pkey, 8, xn_rhs, xn_keys, half, cu)
        stage_mark("O3")
        S.fence()
        ar = Bump(part2_base)
        s5(half, l, ar, U32, UBF)
        stage_mark("O4")
        GBF = s5.GBF
        SGM = [s5.SGMBUF[:, i * 512:(i + 1) * 512] for i in range(2)]
        for n_ in range(4):
            pan, pkey = wnext(f"L{l}glu{n_}")

            def cg(ti, a, b, pp, pk, n_=n_):
                sg = SGM[ti % 2]
                act(sg[:, 0:b - a], pp, AF.Sigmoid, [pk, "PCOL"], [f"XR{ti % 2}"], bias=pc(f"sgb{o}", n_), scale=1.0)
                tt(MIXO[:, 4 + n_, a:b], U32[:, n_, a:b], sg[:, 0:b - a], ALU.mult, [f"U32_{n_}.{ti}", f"XR{ti % 2}"], [f"MIXO{4 + n_}.{ti}"])
            panel_mm(pan, pkey, 4, lambda k, a, b: GBF[:, k, a:b], lambda ti: [f"GBF{c}.{ti}" for c in range(4)], half, cg)
        stage_mark("O5")
        for n_ in range(8):
            pan, pkey = wnext(f"L{l}out{n_}")
            panel_mm(pan, pkey, 8, lambda k, a, b: MIXO[:, k, a:b], lambda ti: [f"MIXO{c}.{ti}" for c in range(8)],
                     half, resid_consumer(n_))

    def s5(half, l, ar, U32, UBF):
        o = l // 2
        GBF = ar.alloc([128, 4, TT], BF16, "GBF")
        s5.GBF = GBF
        if "nos5" in DBG:
            return
        sm = lambda name, n=32: ar.alloc([128, n], F32, name)
        LST = ar.alloc([32, 128], F32, "LST")
        LRE, LIM, DTT, LA = sm("LRE"), sm("LIM"), sm("DTT"), sm("LA")
        RHO, TH, TH8, CT1, ST1 = sm("RHO"), sm("TH"), sm("TH8"), sm("CT1"), sm("ST1")
        LBR, LBI = sm("LBR"), sm("LBI")
        CR, CI, DEN, TMPA, TMPB = sm("CR"), sm("CI"), sm("DEN"), sm("TMPA"), sm("TMPB")
        CA, CBm, CA2, CB2 = sm("CA"), sm("CBm"), sm("CA2"), sm("CB2")
        LBIM = sm("LBIM")
        C8s, S8s, RHO8 = sm("C8s"), sm("S8s"), sm("RHO8")
        PWA = ar.alloc([128, 32, 17], F32, "PWA")
        PWB = ar.alloc([128, 32, 17], F32, "PWB")
        BB = ar.alloc([128, 32, 16], F32, "BB")
        BBS = ar.alloc([128, 32, 16], F32, "BBS")
        arq = Bump(SQ_OFF)
        CCT = arq.alloc([128, 128], F32, "CCT")
        CCT2 = arq.alloc([128, 128], F32, "CCT2")
        CTt = arq.alloc([128, 128], F32, "CTt")
        CTS = arq.alloc([128, 128], F32, "CTS")
        arx = Bump(XN_OFF)
        SL = []
        for s_ in range(2):
            a_ = ar if s_ == 0 else arx
            d_ = dict(CLT=a_.alloc([128, 8, 128], BF16, f"CLT{s_}"), BJT=a_.alloc([128, 8, 128], BF16, f"BJT{s_}"),
                      BJTS=a_.alloc([128, 8, 128], BF16, f"BJTS{s_}"), KT=a_.alloc([128, 8, 128], BF16, f"KT{s_}"),
                      COS=a_.alloc([128, 8, 128], F32, f"COS{s_}"), SINP=a_.alloc([128, 8, 128], F32, f"SINP{s_}"),
                      CPAD=arq.alloc([128, 8, 128], BF16, f"CPAD{s_}"))
            SL.append(d_)
        assert arx.off <= XN_OFF + 16640 and arq.off <= SQ_OFF + 8192
        GX = ar.alloc([128, 2304], F32, "GX")
        XXb = ar.alloc([128, 2048], F32, "XXb")
        XRb = XXb[:, 0:1024]
        XSb = XXb[:, 1024:2048]
        UD = ar.alloc([128, 8, 128], BF16, "UD")
        UM4 = XXb[:, 0:2048].bitcast(BF16).rearrange("p (g t) -> p g t", g=4)
        HT = ar.alloc([128, 8, 128], F32, "HT")
        HS = ar.alloc([128, 8, 128], F32, "HS")
        S32 = ar.alloc([128, 8, 129], F32, "SSTATE")
        SBb = Bump(RS_OFF).alloc([128, 8, 128], BF16, "SBb")
        UMS = ar.alloc([128, 8, NSAMP], BF16, "UMS")
        INIT = ar.alloc([128, 8], F32, "INIT")
        HLS = ar.alloc([128, 8], F32, "HLS")
        TM8 = ar.alloc([128, 8], F32, "TM8")
        H0 = arq.alloc([128, 8, NSAMP], F32, "H0")
        H0S = arq.alloc([128, 8, NSAMP], F32, "H0S")
        HN = arq.alloc([128, 8, NSAMP], F32, "HN")
        HNB = ar.alloc([128, 8, NSAMP], BF16, "HNB")
        OHP = arq.alloc([32, 128], F32, "OHP")
        assert arq.off <= SQ_OFF + 8192
        s5.SGMBUF = XRb
        KXR, KXS = ["XR0", "XR1"], ["XS0", "XS1"]
        XR = XRb[:, 0:1024].rearrange("p (g t) -> p g t", g=8)
        XS_ = XSb[:, 0:1024].rearrange("p (g t) -> p g t", g=8)
        GXa, GXb = GX[:, 0:1152], GX[:, 1152:2304]
        CLall = GXa.rearrange("p (m x) -> p m x", m=9)
        CLtmp = GXb.rearrange("p (m x) -> p m x", m=9)
        BBJ = GXa[:, 0:1024].rearrange("p (j x) -> p j x", j=8)
        BBJ2 = GXb[:, 0:1024].rearrange("p (j x) -> p j x", j=8)
        YT = HT.rearrange("p g t -> p (g t)")
        BRE = HT[:, 0:4, :].rearrange("p g (a c) -> p (g a) c", c=16)
        BIM = HT[:, 4:8, :].rearrange("p g (a c) -> p (g a) c", c=16)
        BTMP = HS[:, 0:4, :].rearrange("p g (a c) -> p (g a) c", c=16)
        E1 = XRb[:, 0:544]
        E2 = XSb[:, 0:544]
        E5 = HT.rearrange("p g t -> p (g t)")[:, 0:544]
        OST = XRb[0:NSAMP, 0:1024].rearrange("t (g x) -> t g x", g=8)
        sgn = signc[:, 0:1]

        def L(x):
            return list(x) if isinstance(x, (list, tuple)) else [x]

        def range_reduce(dst, src, tmp, k):
            ks, kd, kt = L(k[0]), L(k[1]), L(k[2])
            S.op("dve", lambda e: e.tensor_scalar(out=tmp.bitcast(mybir.dt.int32), in0=src, scalar1=1.0 / (2 * math.pi), scalar2=None, op0=ALU.mult), ks, kt)
            S.op("dve", lambda e: e.tensor_copy(out=dst, in_=tmp.bitcast(mybir.dt.int32)), kt, kd)
            stt(dst, dst, -2 * math.pi, src, ALU.mult, ALU.add, kd + ks, kd)
            ts(tmp, dst, math.pi, -2 * math.pi, ALU.is_gt, ALU.mult, kd, kt)
            tt(dst, dst, tmp, ALU.add, kd + kt, kd)
            ts(tmp, dst, -math.pi, 2 * math.pi, ALU.is_lt, ALU.mult, kd, kt)
            tt(dst, dst, tmp, ALU.add, kd + kt, kd)

        def sincos(cos_dst, sin_dst, ang, tmp, kc, ks, ka, kt):
            kc, ks, ka, kt = L(kc), L(ks), L(ka), L(kt)
            act(sin_dst, ang, AF.Sin, ka, ks)
            act(tmp, ang, AF.Abs, ka, kt)
            act(cos_dst, tmp, AF.Sin, kt + ["cst"], kc, bias=cst[:, 2:3], scale=-1.0)

        if half == 0:
            for (dst, src, nm) in ((LRE, "s5_lam_re", "LRE"), (LIM, "s5_lam_im", "LIM")):
                dma(LST[:, 0:64], dap(DI[src], o * 2048, [[64, 32], [1, 64]]), (), ["LST"])
                dma(LST[:, 64:128], dap(DI[src], o * 2048, [[64, 32], [1, 64]]), (), ["LST"])
                pp, pk = psum("a")
                tr(pp[:, 0:32], LST, ["LST"], [pk])
                cp(dst, pp[:, 0:32], [pk], [nm])
            dma(DTT, dap(DI["s5_log_dt"], o * 32, [[0, 128], [1, 32]]), (), ["DTT"])
            act(DTT, DTT, AF.Exp, ["DTT"], ["DTT"])
            tt(LA, LRE, DTT, ALU.mult, ["LRE", "DTT"], ["LA"])
            act(RHO, LA, AF.Exp, ["LA"], ["RHO"])
            tt(TH, LIM, DTT, ALU.mult, ["LIM", "DTT"], ["TH"])
            ts(TH8, TH, 8.0, None, ALU.mult, None, ["TH"], ["TH8"])
            range_reduce(TMPA, TH, TMPB, ["TH", "TMPA", "TMPB"])
            sincos(CT1, ST1, TMPA, TMPB, "CT1", "ST1", "TMPA", "TMPB")
            tt(LBR, RHO, CT1, ALU.mult, ["RHO", "CT1"], ["LBR"])
            tt(LBI, RHO, ST1, ALU.mult, ["RHO", "ST1"], ["LBI"])
            ts(LBIM, LBI, sgn, -1.0, ALU.mult, ALU.mult, ["LBI", "signc"], ["LBIM"])
            e3 = lambda t: t.rearrange("p (g e) -> p g e", e=17)
            thb = TH.unsqueeze(2).to_broadcast([128, 32, 17])
            lab = LA.unsqueeze(2).to_broadcast([128, 32, 17])
            etb = etab.unsqueeze(1).to_broadcast([128, 32, 17])
            tt(e3(E5), thb, etb, ALU.mult, ["TH", "etab"], ["HT"])
            range_reduce(E1, E5, E2, ["HT", KXR, KXS])
            PA2 = PWA.rearrange("p g e -> p (g e)")
            PB2 = PWB.rearrange("p g e -> p (g e)")
            sincos(PA2, PB2, E1, E2, "PWA", "PWB", KXR, KXS)
            ts(PB2, PB2, sgn, None, ALU.mult, None, ["PWB", "signc"], ["PWB"])
            cp(C8s, PWA[:, :, 16], ["PWA"], ["C8s"])
            cp(S8s, PWB[:, :, 16], ["PWB"], ["S8s"])
            tt(e3(E5), lab, etb, ALU.mult, ["LA", "etab"], ["HT"])
            act(E5, E5, AF.Exp, ["HT"], ["HT"])
            cp(RHO8, e3(E5)[:, :, 16], ["HT"], ["RHO8"])
            tt(PA2, PA2, E5, ALU.mult, ["PWA", "HT"], ["PWA"])
            tt(PB2, PB2, E5, ALU.mult, ["PWB", "HT"], ["PWB"])
            ts(TMPA, LBR, -1.0, None, ALU.add, None, ["LBR"], ["TMPA"])
            tt(DEN, LRE, LRE, ALU.mult, ["LRE"], ["DEN"])
            tt(TMPB, LIM, LIM, ALU.mult, ["LIM"], ["TMPB"])
            tt(DEN, DEN, TMPB, ALU.add, ["DEN", "TMPB"], ["DEN"])
            S.op("dve", lambda e: e.reciprocal(out=DEN, in_=DEN), ["DEN"], ["DEN"])
            tt(CR, TMPA, LRE, ALU.mult, ["TMPA", "LRE"], ["CR"])
            tt(TMPB, LBI, LIM, ALU.mult, ["LBI", "LIM"], ["TMPB"])
            tt(CR, CR, TMPB, ALU.add, ["CR", "TMPB"], ["CR"])
            tt(CR, CR, DEN, ALU.mult, ["CR", "DEN"], ["CR"])
            tt(CI, LBI, LRE, ALU.mult, ["LBI", "LRE"], ["CI"])
            tt(TMPB, TMPA, LIM, ALU.mult, ["TMPA", "LIM"], ["TMPB"])
            tt(CI, CI, TMPB, ALU.subtract, ["CI", "TMPB"], ["CI"])
            tt(CI, CI, DEN, ALU.mult, ["CI", "DEN"], ["CI"])
            cp(CA[0:64, :], CR[0:64, :], ["CR"], ["CA"])
            cp(CA[64:128, :], CI[64:128, :], ["CI"], ["CA"])
            ts(CBm[0:64, :], CI[0:64, :], -1.0, None, ALU.mult, None, ["CI"], ["CBm"])
            cp(CBm[64:128, :], CR[64:128, :], ["CR"], ["CBm"])
            cp(CA2[0:64, :], CI[0:64, :], ["CI"], ["CA2"])
            cp(CA2[64:128, :], CR[64:128, :], ["CR"], ["CA2"])
            cp(CB2[0:64, :], CR[0:64, :], ["CR"], ["CB2"])
            ts(CB2[64:128, :], CI[64:128, :], -1.0, None, ALU.mult, None, ["CI"], ["CB2"])
            for hh in range(2):
                dma(BRE[hh * 64:(hh + 1) * 64, :, :], dap(DI["s5_b_re"], o * 32768, [[16, 64], [1024, 32], [1, 16]]), (), ["HT"])
                dma(BIM[hh * 64:(hh + 1) * 64, :, :], dap(DI["s5_b_im"], o * 32768, [[16, 64], [1024, 32], [1, 16]]), (), ["HT"])

            def bc(t):
                return t.unsqueeze(2).to_broadcast([128, 32, 16])
            tt(BB, BRE, bc(CA), ALU.mult, ["HT", "CA"], ["BB"])
            tt(BTMP, BIM, bc(CBm), ALU.mult, ["HT", "CBm"], ["HS"])
            tt(BB, BB, BTMP, ALU.add, ["BB", "HS"], ["BB"])
            tt(BBS, BRE, bc(CA2), ALU.mult, ["HT", "CA2"], ["BBS"])
            tt(BTMP, BIM, bc(CB2), ALU.mult, ["HT", "CB2"], ["HS"])
            tt(BBS, BBS, BTMP, ALU.add, ["BBS", "HS"], ["BBS"])

            for i_, (nm_, t_) in enumerate((("RHO8", RHO8), ("C8s", C8s), ("S8s", S8s), ("LBR", LBR), ("LBIM", LBIM))):
                dma(dap(SCR[f"sm{o}"], i_ * 32, [[160, 128], [1, 32]]), t_, [nm_], [f"SCRsm{o}"])
        else:
            for i_, (nm_, t_) in enumerate((("RHO8", RHO8), ("C8s", C8s), ("S8s", S8s), ("LBR", LBR), ("LBIM", LBIM))):
                dma(t_, dap(SCR[f"sm{o}"], i_ * 32, [[160, 128], [1, 32]]), [f"SCRsm{o}"], [nm_])
        stage_mark("S5a")
        F2 = lambda t: t.rearrange("p g t -> p (g t)")

        def gen(k, sl):
            g0 = k * 8
            B_ = SL[sl]
            CLT, CPAD, BJT, BJTS, KT, COS, SINP = (B_[n] for n in ("CLT", "CPAD", "BJT", "BJTS", "KT", "COS", "SINP"))
            kn = lambda n: f"{n}{sl}"
            cofs = o * 32768 + g0 * 1024
            dma(CCT[:, 0:64], dap(DI["s5_c_re"], cofs, [[64, 128], [1, 64]]), (), ["CCT"])
            dma(CCT[:, 64:128], dap(DI["s5_c_im"], cofs, [[64, 128], [1, 64]]), (), ["CCT"])
            dma(CCT2[:, 0:64], dap(DI["s5_c_im"], cofs, [[64, 128], [1, 64]]), (), ["CCT2"])
            dma(CCT2[:, 64:128], dap(DI["s5_c_re"], cofs, [[64, 128], [1, 64]]), (), ["CCT2"])
            yield
            ts(CCT[:, 64:128], CCT[:, 64:128], -1.0, None, ALU.mult, None, ["CCT"], ["CCT"])
            ts(CCT2[:, 0:64], CCT2[:, 0:64], -1.0, None, ALU.mult, None, ["CCT2"], ["CCT2"])
            pp, pk = psum("ga")
            tr(pp[:, 0:128], CCT, ["CCT"], [pk])
            tr(pp[:, 128:256], CCT2, ["CCT2"], [pk])
            cp(CTt, pp[:, 0:128], [pk], ["CTt"])
            cp(CTS, pp[:, 128:256], [pk], ["CTS"], eng="dve")
            yield
            pwa_c = bass.AP(PWA.tensor, PWA.offset + g0 * 17 + 8, [list(PWA.ap[0]), [1, 9], [17, 8], [0, 16]])
            pwb_c = bass.AP(PWB.tensor, PWB.offset + g0 * 17 + 8, [list(PWB.ap[0]), [1, 9], [17, 8], [0, 16]])
            ct_b = CTt.rearrange("p (g c) -> p g c", g=8).unsqueeze(1).to_broadcast([128, 9, 8, 16])
            cts_b = CTS.rearrange("p (g c) -> p g c", g=8).unsqueeze(1).to_broadcast([128, 9, 8, 16])
            cl4 = CLall.rearrange("p m (g c) -> p m g c", g=8)
            clt4 = CLtmp.rearrange("p m (g c) -> p m g c", g=8)
            tt(cl4, pwa_c, ct_b, ALU.mult, ["PWA", "CTt"], ["GXa"])
            yield
            tt(clt4, pwb_c, cts_b, ALU.mult, ["PWB", "CTS"], ["GXb"])
            yield
            tt(GXa, GXa, GXb, ALU.add, ["GXa", "GXb"], ["GXa"])
            yield
            cp(CLT, CLall[:, 1:9, :], ["GXa"], [kn("CLT")], eng="act")
            if True:
                memset(CPAD, 0.0, [kn("CPAD")])
                for j in range(8):
                    cp(CPAD[:, j, j * 16:(j + 1) * 16], CTt[:, j * 16:(j + 1) * 16], ["CTt"], [kn("CPAD")], eng="pool")
            yield
            pk0, pk0k = psum("ga")
            pk1, pk1k = psum("ga")
            bbv = BB[:, g0:g0 + 8, :].rearrange("p g c -> p (g c)")
            for m in range(8):
                dstp = (pk0 if m < 4 else pk1)[:, (m % 4) * 128:(m % 4 + 1) * 128]
                mm(dstp, [(bbv, CLall[:, m, :])], ["BB", "GXa"], [pk0k if m < 4 else pk1k])
            bmb = bmask.unsqueeze(1).to_broadcast([128, 4, 128])
            tt(KT[:, 0:4, :], pk0.rearrange("p (m x) -> p m x", m=4), bmb, ALU.mult, [pk0k, "bmask"], [kn("KT")])
            yield
            tt(KT[:, 4:8, :], pk1.rearrange("p (m x) -> p m x", m=4), bmb, ALU.mult, [pk1k, "bmask"], [kn("KT")])
            yield
            pwa_b = bass.AP(PWA.tensor, PWA.offset + g0 * 17, [list(PWA.ap[0]), [1, 8], [17, 8], [0, 16]])
            pwb_b = bass.AP(PWB.tensor, PWB.offset + g0 * 17, [list(PWB.ap[0]), [1, 8], [17, 8], [0, 16]])
            bb_b = BB[:, g0:g0 + 8, :].unsqueeze(1).to_broadcast([128, 8, 8, 16])
            bbs_b = BBS[:, g0:g0 + 8, :].unsqueeze(1).to_broadcast([128, 8, 8, 16])
            j4 = lambda t: t.rearrange("p j (g c) -> p j g c", g=8)
            f2 = lambda t: t.rearrange("p j x -> p (j x)")
            for (x1, x2, op_, dstT, nm) in ((bb_b, bbs_b, ALU.subtract, BJT, "BJT"), (bbs_b, bb_b, ALU.add, BJTS, "BJTS")):
                tt(j4(BBJ), pwa_b, x1, ALU.mult, ["PWA", "BB", "BBS"], ["GXa"])
                yield
                tt(j4(BBJ2), pwb_b, x2, ALU.mult, ["PWB", "BB", "BBS"], ["GXb"])
                yield
                tt(f2(BBJ), f2(BBJ), f2(BBJ2), op_, ["GXa", "GXb"], ["GXa"])
                yield
                for hh in range(2):
                    pp, pk = psum("ga")
                    for j in range(4):
                        tr(pp[:, j * 128:(j + 1) * 128], BBJ[:, hh * 4 + j, :], ["GXa"], [pk])
                    cp(dstT[:, hh * 4:hh * 4 + 4, :], pp.rearrange("p (j x) -> p j x", j=4), [pk], [kn(nm)], eng=("act" if hh == 0 else "dve"))
                    yield
            A2, T2, C2, S2 = GXa[:, 0:1024], GXb[:, 0:1024], F2(COS), F2(SINP)
            a3 = A2.rearrange("p (g t) -> p g t", g=8)
            for j in range(8):
                ts(a3[:, j, :], ttab, TH8[:, g0 + j:g0 + j + 1], None, ALU.mult, None, ["ttab", "TH8"], ["GXa"])
                if j % 2 == 1:
                    yield
            S.op("dve", lambda e: e.tensor_scalar(out=T2.bitcast(mybir.dt.int32), in0=A2, scalar1=1.0 / (2 * math.pi), scalar2=None, op0=ALU.mult), ["GXa"], ["GXb"])
            yield
            S.op("dve", lambda e: e.tensor_copy(out=C2, in_=T2.bitcast(mybir.dt.int32)), ["GXb"], [kn("COS")])
            yield
            stt(C2, C2, -2 * math.pi, A2, ALU.mult, ALU.add, [kn("COS"), "GXa"], [kn("COS")])
            yield
            ts(T2, C2, math.pi, -2 * math.pi, ALU.is_gt, ALU.mult, [kn("COS")], ["GXb"])
            yield
            tt(C2, C2, T2, ALU.add, [kn("COS"), "GXb"], [kn("COS")])
            yield
            ts(T2, C2, -math.pi, 2 * math.pi, ALU.is_lt, ALU.mult, [kn("COS")], ["GXb"])
            yield
            tt(A2, C2, T2, ALU.add, [kn("COS"), "GXb"], ["GXa"])
            yield
            act(S2, A2, AF.Sin, ["GXa"], [kn("SINP")])
            act(T2, A2, AF.Abs, ["GXa"], ["GXb"])
            act(C2, T2, AF.Sin, ["GXb", "cst"], [kn("COS")], bias=cst[:, 2:3], scale=-1.0)
            yield
            ts(S2, S2, sgn, None, ALU.mult, None, [kn("SINP"), "signc"], [kn("SINP")])
            yield
            for nm_ in ("CLT", "BJT", "BJTS", "KT", "CPAD", "COS", "SINP"):
                dma(SCR[f"{nm_}{o}{k}"].ap(), B_[nm_].rearrange("p g t -> p (g t)"), [kn(nm_)], [f"SCR{nm_}{o}{k}"])
            yield

        def gen_load(k, sl):
            B_ = SL[sl]
            kn = lambda n: f"{n}{sl}"
            for nm_ in ("BJT", "BJTS", "COS", "SINP", "KT", "CLT", "CPAD"):
                dma(B_[nm_].rearrange("p g t -> p (g t)"), SCR[f"{nm_}{o}{k}"].ap(), [f"SCR{nm_}{o}{k}"], [kn(nm_)])
                yield

        def run(k, sl):
            g0 = k * 8
            B_ = SL[sl]
            CLT, CPAD, BJT, BJTS, KT, COS, SINP = (B_[n] for n in ("CLT", "CPAD", "BJT", "BJTS", "KT", "COS", "SINP"))
            kn = lambda n: f"{n}{sl}"
            hl = HL[:, o, g0:g0 + 8]
            cp(SBb[:, :, 0], hl, ["HL"], ["SBb"])
            cp(HLS[0:64, :], hl[64:128, :], ["HL"], ["HLS"], eng="pool")
            cp(HLS[64:128, :], hl[0:64, :], ["HL"], ["HLS"], eng="pool")
            tt(INIT, hl, C8s[:, g0:g0 + 8], ALU.mult, ["HL", "C8s"], ["INIT"])
            yield
            tt(TM8, HLS, S8s[:, g0:g0 + 8], ALU.mult, ["HLS", "S8s"], ["TM8"])
            yield
            tt(INIT, INIT, TM8, ALU.subtract, ["INIT", "TM8"], ["INIT"])
            yield
            pg = [psum("m") for _ in range(4)]
            ukeys = [f"UBF{k}.{ti}" for ti in range(3 if half == 1 else 2)]
            um = UM4.rearrange("p g (r b) -> p g r b", r=8)
            cp(UD, UBF[:, k, 0:TPH].rearrange("p (b r) -> p r b", r=8), ukeys, ["UD"], eng="act")
            yield
            udf = UD.rearrange("p r b -> p (r b)")
            for hq in range(2):
                for j4_ in range(4):
                    ts(UM4[:, j4_, :], udf, rowmask[:, hq * 4 + j4_:hq * 4 + j4_ + 1], None, ALU.mult, None,
                       ["UD", "rowmask"], KXR + KXS)
                    yield
                for sw in range(2):
                    W_ = BJTS if sw else BJT
                    pgp, pgk = pg[sw * 2 + hq]
                    mm(pgp.rearrange("p (g b) -> p g b", g=4), [(W_[:, jj, :], um[:, :, jj, :]) for jj in range(8)],
                       KXR + KXS + [kn("BJTS") if sw else kn("BJT")], [pgk])
                yield
            for hh in range(2):
                (p1, p1k), (p2_, p2k_) = pg[hh], pg[2 + hh]
                sl_ = slice(hh * 4, hh * 4 + 4)
                tt(F2(XR[:, sl_, :]), p1, F2(COS[:, sl_, :]), ALU.mult, [p1k, kn("COS")], [f"XR{hh}"])
                yield
                tt(F2(XS_[:, sl_, :]), p2_, F2(SINP[:, sl_, :]), ALU.mult, [p2k_, kn("SINP")], [f"XS{hh}"])
                yield
                tt(F2(XR[:, sl_, :]), F2(XR[:, sl_, :]), F2(XS_[:, sl_, :]), ALU.add, [f"XR{hh}", f"XS{hh}"], [f"XR{hh}"])
                yield
            for j in range(8):
                S.op("dve", lambda e, j=j, g0=g0: e.tensor_tensor_scan(
                    out=HT[:, j, :], data0=RHO8[:, g0 + j:g0 + j + 1].to_broadcast([128, 128]), data1=XR[:, j, :],
                    initial=INIT[:, j:j + 1], op0=ALU.mult, op1=ALU.add), [f"XR{j // 4}", "RHO8", "INIT"], ["HT"])
                if j % 2 == 1:
                    yield
            cp(HS[0:64, :, :], HT[64:128, :, :], ["HT"], ["HS"], eng="act")
            cp(HS[64:128, :, :], HT[0:64, :, :], ["HT"], ["HS"], eng="act")
            tt(F2(XR), F2(HT), F2(COS), ALU.mult, ["HT", kn("COS")], KXR)
            yield
            tt(F2(XS_), F2(HS), F2(SINP), ALU.mult, ["HS", kn("SINP")], KXS)
            yield
            tt(SBb[:, :, 1:128], XR[:, :, 0:127], XS_[:, :, 0:127], ALU.subtract, KXR + KXS, ["SBb"])
            tt(HL[:, o, g0:g0 + 8], XR[:, :, 127], XS_[:, :, 127], ALU.subtract, KXR + KXS, ["HL"])
            yield
            u8 = UD
            py = [psum("ra"), psum("ra")]
            pyA, pyB = py[0][0], py[1][0]

            def yfn(e, u8=u8, pyA=pyA, pyB=pyB, KT=KT, CLT=CLT):
                ins = None
                for jj in range(8):
                    if jj <= 3:
                        n_ = 4 - jj
                        ins = e.matmul(pyA[:, jj * 128:512], lhsT=u8[:, jj, :], rhs=KT[:, 0:n_, :].rearrange("p m x -> p (m x)"),
                                       start=(jj == 0), stop=False, skip_group_check=True)
                    t0_ = max(jj, 4)
                    n_ = 8 - t0_
                    m0_ = t0_ - jj
                    ins = e.matmul(pyB[:, (t0_ - 4) * 128:512], lhsT=u8[:, jj, :], rhs=KT[:, m0_:m0_ + n_, :].rearrange("p m x -> p (m x)"),
                                   start=(jj == 0), stop=False, skip_group_check=True)
                for j in range(8):
                    for t_ in range(8):
                        pyp = pyA if t_ < 4 else pyB
                        c0_ = (t_ % 4) * 128 + j * 16
                        ins = e.matmul(pyp[:, c0_:c0_ + 16], lhsT=SBb[:, j, :], rhs=CLT[:, t_, j * 16:(j + 1) * 16],
                                       start=False, stop=(j == 7), skip_group_check=True)
                return ins
            S.op("pe", yfn, ["UD", kn("KT"), "SBb", kn("CLT")], [py[0][1], py[1][1]])
            cp(YT[:, 0:512], py[0][0], [py[0][1]], ["HT"], eng="act")
            cp(YT[:, 512:1024], py[1][0], [py[1][1]], ["HT"], eng="act")
            yield
            pt = [psum("ra"), psum("ra")]
            for t_ in range(8):
                ptp, ptk = pt[t_ // 4]
                tr(ptp[:, (t_ % 4) * 128:(t_ % 4 + 1) * 128], YT[:, t_ * 128:(t_ + 1) * 128], ["HT"], [ptk])
            uview = U32[:, k, 0:TPH].rearrange("p (b r) -> p r b", r=8)
            for hh in range(2):
                ptp, ptk = pt[hh]
                stt(uview[:, hh * 4:hh * 4 + 4, :], uview[:, hh * 4:hh * 4 + 4, :], pc(f"sd{o}", k), ptp.rearrange("p (r b) -> p r b", r=4),
                    ALU.mult, ALU.add, [f"U32_{k}.0", f"U32_{k}.1", "PCOL", ptk], [f"U32_{k}.0", f"U32_{k}.1"])
                yield
            if half == 1 and "nosamp" not in DBG:
                SST = F2(COS)[0:NSAMP, :].rearrange("t (g r p) -> t g r p", g=8, r=2)
                SSW = F2(SINP)[0:NSAMP, :].rearrange("t (g r p) -> t g r p", g=8, r=2)
                sso = o * NSAMP * 2048 + g0 * 64
                spat = [[2048, NSAMP], [64, 8], [1, 64]]
                dma(SST[:, :, 0, :], dap(DI["st_re"], sso, spat), (), [kn("COS")])
                dma(SST[:, :, 1, :], dap(DI["st_im"], sso, spat), (), [kn("COS")])
                dma(SSW[:, :, 0, :], dap(DI["st_im"], sso, spat), (), [kn("SINP")])
                dma(SSW[:, :, 1, :], dap(DI["st_re"], sso, spat), (), [kn("SINP")])
                yield
                for (src, dst, nm, sk) in ((SST, H0, "H0", kn("COS")), (SSW, H0S, "H0S", kn("SINP"))):
                    pp, pk = psum("ra")
                    for j in range(8):
                        tr(pp[:, j * NSAMP:(j + 1) * NSAMP], src[:, j, :, :].rearrange("t r p -> t (r p)"), [sk], [pk])
                    cp(dst.rearrange("p g t -> p (g t)"), pp[:, 0:128], [pk], [nm])
                    yield
                bcg = lambda t: t[:, g0:g0 + 8].unsqueeze(2).to_broadcast([128, 8, NSAMP])
                tt(HN, H0, bcg(LBR), ALU.mult, ["H0", "LBR"], ["HN"])
                tt(H0S, H0S, bcg(LBIM), ALU.mult, ["H0S", "LBIM"], ["H0S"])
                tt(HN, HN, H0S, ALU.add, ["HN", "H0S"], ["HN"])
                yield
                pbs, pbsk = psum("ra")
                for j in range(8):
                    ts(UMS[:, j, :], UBF[:, k, TPH:TT], rowmask[:, j:j + 1], None, ALU.mult, None, [f"UBF{k}.2", "rowmask"], ["UMS"])
                for j in range(8):
                    mm(pbs[:, j * NSAMP:(j + 1) * NSAMP], [(BJT[:, 7, :], UMS[:, j, :])], ["UMS", kn("BJT")], [pbsk])
                tt(HN.rearrange("p g t -> p (g t)"), HN.rearrange("p g t -> p (g t)"), pbs[:, 0:128], ALU.add, ["HN", pbsk], ["HN"])
                cp(HNB, HN, ["HN"], ["HNB"], eng="act")
                yield
                pys, pysk = psum("ra")
                mm(pys[:, 0:NSAMP], [(CPAD[:, j, :], HNB[:, j, :]) for j in range(8)], [kn("CPAD"), "HNB"], [pysk])
                stt(U32[:, k, TPH:TT], U32[:, k, TPH:TT], pc(f"sd{o}", k), pys[:, 0:NSAMP], ALU.mult, ALU.add,
                    [f"U32_{k}.2", "PCOL", pysk], [f"U32_{k}.2"])
                po, pok = psum("ra")
                po2, po2k = psum("ra")
                for j in range(8):
                    d_ = (po if j < 4 else po2)[0:NSAMP, (j % 4) * 128:(j % 4 + 1) * 128]
                    tr(d_, HN[:, j, :], ["HN"], [pok if j < 4 else po2k])
                cp(OST[:, 0:4, :].rearrange("t g x -> t (g x)"), po[0:NSAMP, :], [pok], KXR)
                cp(OST[:, 4:8, :].rearrange("t g x -> t (g x)"), po2[0:NSAMP, :], [po2k], KXR)
                dma(dap(DO["srs"], o * NSAMP * 2048 + g0 * 64, [[2048, NSAMP], [64, 8], [1, 64]]), OST[:, :, 0:64], KXR, ())
                dma(dap(DO["sis"], o * NSAMP * 2048 + g0 * 64, [[2048, NSAMP], [64, 8], [1, 64]]), OST[:, :, 64:128], KXR, ())
                yield
            for ti, (a, b) in enumerate(tiles_of(half)):
                act(U32[:, k, a:b], U32[:, k, a:b], AF.Gelu_apprx_tanh, [f"U32_{k}.{ti}"], [f"U32_{k}.{ti}"])
                cp(GBF[:, k, a:b], U32[:, k, a:b], [f"U32_{k}.{ti}"], [f"GBF{k}.{ti}"], eng="pool")
            yield

        gen_ = gen if half == 0 else gen_load
        for _ in gen_(0, 0):
            pass
        for k in range(4):
            streams = [run(k, k % 2)]
            if k < 3 and "noilv" not in DBG:
                streams.append(gen_(k + 1, (k + 1) % 2))
            while streams:
                for st_ in list(streams):
                    try:
                        next(st_)
                    except StopIteration:
                        streams.remove(st_)
            if k < 3 and "noilv" in DBG:
                for _ in gen_(k + 1, (k + 1) % 2):
                    pass
        if half == 1:
            pp, pk = psum("a")
            tr(pp[0:32, 0:128], HL[:, o, :], ["HL"], [pk])
            cp(OHP, pp[0:32, 0:128], [pk], ["OHP"])
            dma(dap(DO["srp"], o * 2048, [[64, 32], [1, 64]]), OHP[:, 0:64], ["OHP"], ())
            dma(dap(DO["sip"], o * 2048, [[64, 32], [1, 64]]), OHP[:, 64:128], ["OHP"], ())

    try:
      for half in range(2):
        S.fence()
        load_x(half)
        for l in LAYERS:
            if l % 2 == 0 or "evenonly" in DBG:
                mixer_even(half, l)
            else:
                mixer_odd(half, l)
            ffn(half, l)
        S.fence()
        rmsnorm(half, "nfin", final=True)
        store_y(half)
    except StopBuild:
        pass
    S.finish()
    S.emit()
    return nc


_CONSTS = None


def _consts():
    global _CONSTS
    if _CONSTS is None:
        i = np.arange(128)
        _CONSTS = dict(
            cst_ident=np.eye(128, dtype=np.float32),
            cst_masku=(i[None, :] >= i[:, None]).astype(np.float32),
            cst_ttab=np.broadcast_to(i[None, :].astype(np.float32), (128, 128)).copy(),
            cst_rowmask=(i[:, None] // 16 == np.arange(8)[None, :]).astype(np.float32),
            cst_etab=np.broadcast_to(np.array([7, 6, 5, 4, 3, 2, 1, 0, 0, 1, 2, 3, 4, 5, 6, 7, 8], np.float32)[None, :], (128, 17)).copy(),
            cst_bmask=(i[:, None] // 16 == i[None, :] // 16).astype(np.float32),
            cst_sign=np.stack([np.where(i < 64, 1.0, -1.0), np.where(i < 64, -1.0, 1.0)], axis=1).astype(np.float32),
        )
    return _CONSTS


_NC_CACHE = {}


def kernel(**inputs):
    f = lambda k: np.ascontiguousarray(np.asarray(inputs[k], dtype=np.float32))
    if "nc" not in _NC_CACHE:
        _NC_CACHE["nc"] = build()
    nc = _NC_CACHE["nc"]
    shared = {k: f(k) for k in IN_SHAPES if not k.startswith("cst_") and k not in
              ("xp", "xs", "st_cb", "st_cc", "st_re", "st_im", "st_ff")}
    shared.update(_consts())
    xp, xs = f("x_prompt"), f("x_sample")
    scb, scc, sre, sim, sff = f("state_conv_b"), f("state_conv_c"), f("state_ssm_re"), f("state_ssm_im"), f("state_ffn_conv")
    in_maps = []
    ncores = int(os.environ.get("KCORES", N_CORES))
    for c in range(ncores):
        sl = slice(c * NSAMP, (c + 1) * NSAMP)
        m = dict(shared)
        m["xp"] = xp[c]
        m["xs"] = np.ascontiguousarray(xs[sl, 0, :])
        m["st_cb"] = np.ascontiguousarray(scb[:, sl])
        m["st_cc"] = np.ascontiguousarray(scc[:, sl])
        m["st_re"] = np.ascontiguousarray(sre[:, sl])
        m["st_im"] = np.ascontiguousarray(sim[:, sl])
        m["st_ff"] = np.ascontiguousarray(sff[:, sl])
        in_maps.append(m)
    res = run_bass_kernel_spmd(nc, in_maps, core_ids=list(range(ncores)))
    R = list(res.results)
    while len(R) < N_CORES:
        R.append({k: np.zeros_like(v) for k, v in R[0].items()})
    cat = lambda k, ax: np.concatenate([np.asarray(r[k], dtype=np.float32) for r in R], axis=ax)
    stk = lambda k, ax: np.stack([np.asarray(r[k], dtype=np.float32) for r in R], axis=ax)
    y_prompt = stk("yp", 0)
    y_sample = cat("ys", 0)[:, None, :]
    v_rows = cat("vrows", 1)[:, :, None, :]
    return (y_prompt, y_sample, v_rows,
            stk("cbp", 1), cat("cbs", 1), stk("ccp", 1), cat("ccs", 1),
            stk("srp", 1), cat("srs", 1), stk("sip", 1), cat("sis", 1),
            stk("ffp", 1), cat("ffs", 1))
```
